# Optimizing a Trainium2 kernel written in Bass

```python
import math
import jax, jax.numpy as jnp
from jax import lax
import numpy as np

D_MODEL = 2048
BATCH = 16
SEQ = 256
DEPTH = 2
DEC_BATCH = 4
DEC_SEQ = 1024
PAST_LEN = 512

GRID_W = 64
EPS = 1e-6
D_FF = 4 * D_MODEL

HG_HEADS = 8
HG_DK = 128
HG_DV = 128
HG_F = HG_HEADS * HG_DK
HG_V = HG_HEADS * HG_DV
HG_CHUNK = 16

CM_GROUPS = 8
CM_GROUP_DIM = 128
CM_W = CM_GROUPS * CM_GROUP_DIM
CM_CHUNK = 128

GDN_HEADS = 8
GDN_DK = 128
GDN_DV = 128
GDN_K = GDN_HEADS * GDN_DK
GDN_V = GDN_HEADS * GDN_DV
GDN_QKV = 2 * GDN_K + GDN_V
GDN_CHUNK = 64
CONV_K = 3

IN_SIZES = (HG_F, HG_V, HG_V, HG_F, HG_F,
            CM_W, CM_W,
            GDN_QKV, GDN_V, 2 * GDN_HEADS, 2 * GDN_HEADS,
            3 * D_MODEL)
IN_DIM = sum(IN_SIZES)

kernel_name = 'hybrid_hgrn2_gmlp_gdn_diffusion_step'


def rmsnorm(x, g):
    xf = x.astype(jnp.float32)
    y = xf * lax.rsqrt(jnp.mean(jnp.square(xf), axis=-1, keepdims=True) + EPS)
    return (y * g.astype(jnp.float32)).astype(x.dtype)


def l2norm(x):
    return x * lax.rsqrt(jnp.sum(jnp.square(x), axis=-1, keepdims=True) + EPS)


def rev(t):
    return jnp.flip(t, axis=1)


def hgrn2_scan(q, k, v, log_f, s0):
    B, L, H, DK = q.shape
    DV = v.shape[-1]
    C = HG_CHUNK
    N = L // C
    q, k, v, log_f = (t.reshape(B, N, C, H, t.shape[-1]) for t in (q, k, v, log_f))
    b = jnp.cumsum(log_f, axis=2)
    causal = jnp.tril(jnp.ones((C, C), dtype=bool))[:, :, None, None]
    decay = jnp.exp(jnp.where(causal, b[:, :, :, None] - b[:, :, None, :], -jnp.inf))
    scores = jnp.einsum('bntHd,bnsHd,bntsHd->bnHts', q, k, decay)
    o_intra = jnp.einsum('bnHts,bnsHv->bntHv', scores, v)
    b_last = b[:, :, -1]
    q_dec = q * jnp.exp(b)
    k_dec = k * jnp.exp(b_last[:, :, None] - b)

    def step(S, xs):
        qd, kd, vv, bl = xs
        o = jnp.einsum('bcHd,bHdv->bcHv', qd, S)
        S = S * jnp.exp(bl)[..., None] + jnp.einsum('bcHd,bcHv->bHdv', kd, vv)
        return S, o

    xs = tuple(jnp.moveaxis(t, 1, 0) for t in (q_dec, k_dec, v, b_last))
    s_T, o_inter = lax.scan(step, s0.astype(jnp.float32), xs)
    o = o_intra + jnp.moveaxis(o_inter, 0, 1)
    return o.reshape(B, L, H, DV), s_T


def gdn_scan(q, k, v, g, beta, s0):
    B, L, H, DK = q.shape
    DV = v.shape[-1]
    C = GDN_CHUNK
    N = L // C
    to_chunks = lambda t: jnp.moveaxis(t.reshape((B, N, C) + t.shape[2:]), 3, 2)
    q, k, v, g, beta = (to_chunks(t) for t in (q, k, v, g, beta))
    gc = jnp.cumsum(g, axis=-1)
    idx = jnp.arange(C)
    lower_incl = idx[:, None] >= idx[None, :]
    strict = idx[:, None] > idx[None, :]
    decay = jnp.exp(jnp.where(lower_incl, gc[..., :, None] - gc[..., None, :], -jnp.inf))
    kb = k * beta[..., None]
    m = jnp.where(strict, jnp.einsum('bnhtd,bnhsd->bnhts', kb, k) * decay, 0.0)
    eye = jnp.broadcast_to(jnp.eye(C, dtype=m.dtype), m.shape)
    T = lax.linalg.triangular_solve(m, eye, left_side=True, lower=True, unit_diagonal=True)
    u = T @ (v * beta[..., None])
    w = T @ (kb * jnp.exp(gc)[..., None])
    attn = jnp.einsum('bnhtd,bnhsd->bnhts', q, k) * decay
    q_dec = q * jnp.exp(gc)[..., None]
    g_last = gc[..., -1]
    k_dec = k * jnp.exp(g_last[..., None] - gc)[..., None]

    def step(S, xs):
        qd, kd, uu, ww, at, gl = xs
        v_new = uu - jnp.einsum('bhcd,bhdv->bhcv', ww, S)
        o = jnp.einsum('bhcd,bhdv->bhcv', qd, S) + jnp.einsum('bhts,bhsv->bhtv', at, v_new)
        S = S * jnp.exp(gl)[..., None, None] + jnp.einsum('bhcd,bhcv->bhdv', kd, v_new)
        return S, o

    xs = tuple(jnp.moveaxis(t, 1, 0) for t in (q_dec, k_dec, u, w, attn, g_last))
    s_T, o = lax.scan(step, s0.astype(jnp.float32), xs)
    o = jnp.moveaxis(jnp.moveaxis(o, 0, 1), 2, 3).reshape(B, L, H, DV)
    return o, s_T


def hgrn2_branch(q_raw, i_raw, g_raw, ff_raw, fb_raw, lb, onorm_g, s0):
    B, L, _ = q_raw.shape
    heads = lambda t, d: t.reshape(B, L, HG_HEADS, d)
    q = heads(jax.nn.silu(q_raw.astype(jnp.float32)), HG_DK)
    v = heads(i_raw.astype(jnp.float32), HG_DV)

    def gates(z, lbd):
        f = lbd + (1.0 - lbd) * jax.nn.sigmoid(z.astype(jnp.float32))
        return heads(1.0 - f, HG_DK), heads(jnp.log(f), HG_DK)

    k_f, lf_f = gates(ff_raw, lb[0])
    k_b, lf_b = gates(fb_raw, lb[1])
    o_f, s_f = hgrn2_scan(q, k_f, v, lf_f, s0[:, 0])
    o_b, s_b = hgrn2_scan(rev(q), rev(k_b), rev(v), rev(lf_b), s0[:, 1])
    o = rmsnorm(o_f + rev(o_b), onorm_g).reshape(B, L, HG_V) * jax.nn.silu(g_raw.astype(jnp.float32))
    return o.astype(q_raw.dtype), jnp.stack([s_f, s_b], axis=1)


def chunk_mlp_branch(u_raw, v_raw, vnorm_g, ws, bs):
    B, L, _ = u_raw.shape
    N = L // CM_CHUNK
    u = jax.nn.gelu(u_raw, approximate=False)
    v = jax.nn.gelu(v_raw, approximate=False).reshape(B, L, CM_GROUPS, CM_GROUP_DIM)
    v = rmsnorm(v, vnorm_g.reshape(CM_GROUPS, CM_GROUP_DIM)).reshape(B, N, CM_CHUNK, CM_GROUPS, CM_GROUP_DIM)
    s = jnp.einsum('gpq,bnqgc->bnpgc', ws, v) + bs.T[:, :, None]
    return u * s.reshape(B, L, CM_W)


def short_conv(x, w, rows):
    B, L, Cc = x.shape
    p = CONV_K // 2
    if rows is not None:
        y = lax.conv_general_dilated(x.reshape(B, rows, GRID_W, Cc), w[:, :, None, :].astype(x.dtype),
                                     (1, 1), [(p, p), (p, p)], dimension_numbers=('NHWC', 'HWIO', 'NHWC'),
                                     feature_group_count=Cc)
        return y.reshape(B, L, Cc)
    return lax.conv_general_dilated(x, w[p][:, None, :].astype(x.dtype), (1,), [(p, p)],
                                    dimension_numbers=('NWC', 'WIO', 'NWC'), feature_group_count=Cc)


def gdn_branch(qkv_raw, g_raw, a_raw, b_raw, conv_w, A_log, dt_bias, onorm_g, s0, rows):
    B, L, _ = qkv_raw.shape
    qkv = jax.nn.silu(short_conv(qkv_raw, conv_w, rows).astype(jnp.float32))
    q, k, v = jnp.split(qkv, [GDN_K, 2 * GDN_K], axis=-1)
    q = l2norm(q.reshape(B, L, GDN_HEADS, GDN_DK)) * (GDN_DK ** -0.5)
    k = l2norm(k.reshape(B, L, GDN_HEADS, GDN_DK))
    v = v.reshape(B, L, GDN_HEADS, GDN_DV)
    a = a_raw.astype(jnp.float32).reshape(B, L, 2, GDN_HEADS)
    g = -jnp.exp(A_log.astype(jnp.float32)) * jax.nn.softplus(a + dt_bias.astype(jnp.float32))
    beta = jax.nn.sigmoid(b_raw.astype(jnp.float32).reshape(B, L, 2, GDN_HEADS))
    o_f, s_f = gdn_scan(q, k, v, g[:, :, 0], beta[:, :, 0], s0[:, 0])
    o_b, s_b = gdn_scan(rev(q), rev(k), rev(v), rev(g[:, :, 1]), rev(beta[:, :, 1]), s0[:, 1])
    o = rmsnorm(o_f + rev(o_b), onorm_g).reshape(B, L, GDN_V) * jax.nn.silu(g_raw.astype(jnp.float32))
    return o.astype(qkv_raw.dtype), jnp.stack([s_f, s_b], axis=1)


def block(x, mod, p, lb, s0_hg, s0_gdn, rows):
    shift1, scale1, gate1, shift2, scale2, gate2 = jnp.split(mod, 6, axis=-1)
    h = rmsnorm(x, p['norm1_g']) * (1.0 + scale1) + shift1
    (hq, hi, hgo, hff, hfb, cu, cv, gqkv, ggo, ga, gbeta, gates) = jnp.split(
        h @ p['w_in'], np.cumsum(IN_SIZES)[:-1].tolist(), axis=-1)
    o_a, s_hg = hgrn2_branch(hq, hi, hgo, hff, hfb, lb, p['hg_onorm_g'], s0_hg)
    o_b = chunk_mlp_branch(cu, cv, p['cm_vnorm_g'], p['cm_ws'], p['cm_bs'])
    o_c, s_gdn = gdn_branch(gqkv, ggo, ga, gbeta, p['gdn_conv'], p['gdn_A_log'], p['gdn_dt_bias'],
                            p['gdn_onorm_g'], s0_gdn, rows)
    g_a, g_b, g_c = jnp.split(jax.nn.sigmoid(gates), 3, axis=-1)
    merged = (g_a * (o_a @ p['w_br_hg']) + g_b * (o_b @ p['w_br_cm']) + g_c * (o_c @ p['w_br_gdn']))
    x = x + gate1 * (merged @ p['w_out'])
    h2 = rmsnorm(x, p['norm2_g']) * (1.0 + scale2) + shift2
    x = x + gate2 * (jnp.square(jax.nn.relu(h2 @ p['w_ff1'])) @ p['w_ff2'])
    return x, s_hg, s_gdn


def setup_inputs(seed: int = 0) -> dict:
    key = jax.random.key(seed)
    ks = iter(jax.random.split(key, 40))
    f32 = jnp.float32
    D = D_MODEL

    def nrm(shape, scale):
        return jax.random.normal(next(ks), shape, f32) * scale

    x_prompt = nrm((BATCH, SEQ, D), 1.0)
    x_sample = nrm((DEC_BATCH, DEC_SEQ, D), 1.0)
    c = nrm((DEC_BATCH, D), 1.0)
    state_hgrn = nrm((DEC_BATCH, DEPTH, 2, HG_HEADS, HG_DK, HG_DV), 0.5)
    state_gdn = nrm((DEC_BATCH, DEPTH, 2, GDN_HEADS, GDN_DK, GDN_DV), 0.1)
    c_ctx = nrm((D,), 1.0)
    norm1_g = 1.0 + nrm((DEPTH, D), 0.02)
    norm2_g = 1.0 + nrm((DEPTH, D), 0.02)
    w_mod = nrm((DEPTH, D, 6 * D), D ** -0.5)
    b_mod = nrm((DEPTH, 6 * D), 0.02)
    w_in = nrm((DEPTH, D, IN_DIM), D ** -0.5)
    hg_lb = nrm((DEPTH, 2, HG_F), 1.0)
    hg_onorm_g = 1.0 + nrm((DEPTH, HG_DV), 0.02)
    cm_vnorm_g = 1.0 + nrm((DEPTH, CM_W), 0.02)
    cm_ws = nrm((DEPTH, CM_GROUPS, CM_CHUNK, CM_CHUNK), CM_CHUNK ** -0.5)
    cm_bs = 1.0 + nrm((DEPTH, CM_GROUPS, CM_CHUNK), 0.02)
    gdn_conv = nrm((DEPTH, CONV_K, CONV_K, GDN_QKV), 0.5)
    gdn_A_log = jnp.log(jax.random.uniform(next(ks), (DEPTH, 2, GDN_HEADS), f32, 1.0, 16.0))
    dt = jnp.exp(jax.random.uniform(next(ks), (DEPTH, 2, GDN_HEADS), f32, math.log(1e-3), math.log(1e-1)))
    gdn_dt_bias = dt + jnp.log(-jnp.expm1(-dt))
    gdn_onorm_g = 1.0 + nrm((DEPTH, GDN_DV), 0.02)
    w_br_hg = nrm((DEPTH, HG_V, D), HG_V ** -0.5)
    w_br_cm = nrm((DEPTH, CM_W, D), CM_W ** -0.5)
    w_br_gdn = nrm((DEPTH, GDN_V, D), GDN_V ** -0.5)
    w_out = nrm((DEPTH, D, D), D ** -0.5)
    w_ff1 = nrm((DEPTH, D, D_FF), D ** -0.5)
    w_ff2 = nrm((DEPTH, D_FF, D), D_FF ** -0.5)
    final_g = 1.0 + nrm((D,), 0.02)
    return {'x_prompt': x_prompt, 'x_sample': x_sample, 'c': c,
            'state_hgrn': state_hgrn, 'state_gdn': state_gdn, 'c_ctx': c_ctx,
            'norm1_g': norm1_g, 'norm2_g': norm2_g, 'w_mod': w_mod, 'b_mod': b_mod, 'w_in': w_in,
            'hg_lb': hg_lb, 'hg_onorm_g': hg_onorm_g, 'cm_vnorm_g': cm_vnorm_g, 'cm_ws': cm_ws,
            'cm_bs': cm_bs, 'gdn_conv': gdn_conv, 'gdn_A_log': gdn_A_log, 'gdn_dt_bias': gdn_dt_bias,
            'gdn_onorm_g': gdn_onorm_g, 'w_br_hg': w_br_hg, 'w_br_cm': w_br_cm, 'w_br_gdn': w_br_gdn,
            'w_out': w_out, 'w_ff1': w_ff1, 'w_ff2': w_ff2, 'final_g': final_g}


def reference(x_prompt, x_sample, c, state_hgrn, state_gdn, c_ctx, norm1_g, norm2_g, w_mod, b_mod, w_in,
              hg_lb, hg_onorm_g, cm_vnorm_g, cm_ws, cm_bs, gdn_conv, gdn_A_log, gdn_dt_bias, gdn_onorm_g,
              w_br_hg, w_br_cm, w_br_gdn, w_out, w_ff1, w_ff2, final_g):
    lb_all = jnp.cumsum(jax.nn.softmax(hg_lb.astype(jnp.float32), axis=0), axis=0)
    lb_all = lb_all - lb_all[:1]
    rows = x_sample.shape[1] // GRID_W
    n_ctx = x_prompt.shape[0]
    zeros_hg = jnp.zeros((n_ctx, 2, HG_HEADS, HG_DK, HG_DV), jnp.float32)
    zeros_gdn = jnp.zeros((n_ctx, 2, GDN_HEADS, GDN_DK, GDN_DV), jnp.float32)
    xp = x_prompt
    xs = x_sample
    new_hg = []
    new_gdn = []
    for l in range(DEPTH):
        p = dict(norm1_g=norm1_g[l], norm2_g=norm2_g[l], w_in=w_in[l], hg_onorm_g=hg_onorm_g[l],
                 cm_vnorm_g=cm_vnorm_g[l], cm_ws=cm_ws[l], cm_bs=cm_bs[l], gdn_conv=gdn_conv[l],
                 gdn_A_log=gdn_A_log[l], gdn_dt_bias=gdn_dt_bias[l], gdn_onorm_g=gdn_onorm_g[l],
                 w_br_hg=w_br_hg[l], w_br_cm=w_br_cm[l], w_br_gdn=w_br_gdn[l], w_out=w_out[l],
                 w_ff1=w_ff1[l], w_ff2=w_ff2[l])
        mod_ctx = (jax.nn.silu(c_ctx) @ w_mod[l] + b_mod[l])[None, None, :]
        mod_lat = (jax.nn.silu(c) @ w_mod[l] + b_mod[l])[:, None, :]
        xp, s_hg, s_gdn = block(xp, mod_ctx, p, lb_all[l], zeros_hg, zeros_gdn, None)
        new_hg.append(s_hg)
        new_gdn.append(s_gdn)
        xs, _, _ = block(xs, mod_lat, p, lb_all[l], state_hgrn[:, l], state_gdn[:, l], rows)
    y_prompt = rmsnorm(xp, final_g)
    y_sample = rmsnorm(xs, final_g)
    new_state_hgrn = jnp.stack(new_hg, axis=1).astype(x_prompt.dtype)
    new_state_gdn = jnp.stack(new_gdn, axis=1).astype(x_prompt.dtype)
    return (y_prompt, y_sample, new_state_hgrn, new_state_gdn)
```

```python
import contextlib
import numpy as np
import concourse.bass as bass
import concourse.mybir as mybir
from concourse.ap import AP
from concourse.bass_utils import run_bass_kernel_spmd

F32 = mybir.dt.float32
BF16 = mybir.dt.bfloat16
AF = mybir.ActivationFunctionType
ALU = mybir.AluOpType

P = 128
T = 1024
D = 2048
KC = 16
DEPTH = 2
EPS = 1e-6
NCORES = 8
NDMASEM = 8
C_HQ, C_HI, C_HGO, C_HFF, C_HFB, C_CU, C_CV = 0, 1024, 2048, 3072, 4096, 5120, 6144
C_GQ, C_GK, C_GV, C_GGO, C_GA, C_GB = 7168, 8192, 9216, 10240, 11264, 11280
C_GATE_A, C_GATE_B, C_GATE_C = 11296, 13344, 15392


class Op:
    __slots__ = ("issuer", "stream", "idx", "fn", "waits", "signal", "semval", "is_dma", "src")


def apkeys(x):
    if not isinstance(x, AP):
        return [x]
    tn = type(x.tensor).__name__
    if tn.startswith("DRam"):
        return []
    esz = 2 if x.dtype == BF16 else 4
    rowlen = x.tensor.shape[1]
    off = int(x.offset) % rowlen
    lo = hi = off
    for st, cnt in x.ap[1:]:
        if st >= 0:
            hi += st * (cnt - 1)
        else:
            lo += st * (cnt - 1)
    b0 = lo * esz
    b1 = (hi + 1) * esz
    gran = 2048 if tn.startswith("PSum") else 512
    nm = "ps" if tn.startswith("PSum") else "sb"
    return [(nm, g) for g in range(b0 // gran, (b1 - 1) // gran + 1)]


class Sched:
    def __init__(self, nc):
        self.nc = nc
        self.per_issuer = {e: [] for e in ("pe", "act", "dve", "pool", "sp")}
        self.stream_ops = {}
        self.last_writer = {}
        self.readers = {}
        self.clock = {e: {} for e in self.per_issuer}
        self.opclock = {}
        self.dma_count = {e: 0 for e in self.per_issuer}
        self.nops = 0
        self.debug_src = False
        self.srcmap = {}

    def add(self, issuer, fn, reads=(), writes=(), dma=False):
        op = Op()
        op.issuer = issuer
        op.fn = fn
        op.is_dma = dma
        op.signal = False
        op.semval = None
        op.src = None
        if self.debug_src:
            import sys as _sys
            f = _sys._getframe(1)
            names = []
            while f is not None and len(names) < 4:
                if f.f_code.co_name not in ("dma", "mm", "tr", "act", "tt", "ts", "stt", "cp", "scan", "mmblk"):
                    names.append(f.f_lineno)
                f = f.f_back
            op.src = names
        if dma:
            n = self.dma_count[issuer]
            self.dma_count[issuer] += 1
            op.stream = "d_%s_%d" % (issuer, n % NDMASEM)
        else:
            op.stream = issuer
        so = self.stream_ops.setdefault(op.stream, [])
        op.idx = len(so) + 1
        rkeys = []
        for r in reads:
            rkeys.extend(apkeys(r))
        wkeys = []
        for w in writes:
            wkeys.extend(apkeys(w))
        deps = []
        raw = set()
        for k in rkeys:
            w = self.last_writer.get(k)
            if w is not None:
                deps.append(w)
                raw.add(id(w))
            if k[0] == "ps":
                rd = self.readers.get(k)
                if rd:
                    for r_ in rd.values():
                        if r_.stream != op.stream:
                            deps.append(r_)
        for k in wkeys:
            w = self.last_writer.get(k)
            if w is not None:
                deps.append(w)
            rd = self.readers.get(k)
            if rd:
                deps.extend(rd.values())
        if dma and so:
            deps.append(so[-1])
        clk = self.clock[issuer]
        best = {}
        for d in deps:
            if d.stream == op.stream and not dma:
                if issuer == "pe":
                    continue
            if clk.get(d.stream, 0) >= d.idx:
                continue
            if d.stream not in best or best[d.stream].idx < d.idx:
                best[d.stream] = d
        op.waits = list(best.values())
        for d in op.waits:
            d.signal = True
            if clk.get(d.stream, 0) < d.idx:
                clk[d.stream] = d.idx
            for s, i in self.opclock[id(d)].items():
                if clk.get(s, 0) < i:
                    clk[s] = i
        so.append(op)
        myclk = dict(clk)
        myclk[op.stream] = op.idx
        self.opclock[id(op)] = myclk
        for k in rkeys:
            self.readers.setdefault(k, {})[op.stream] = op
        for k in wkeys:
            self.last_writer[k] = op
            self.readers[k] = {}
        self.per_issuer[issuer].append(op)
        self.nops += 1
        return op

    def emit(self, final_waits=()):
        nc = self.nc
        for s, so in self.stream_ops.items():
            c = 0
            for o in so:
                if s.startswith("d_"):
                    o.signal = True
                if o.signal:
                    c += 1
                    o.semval = c
        sems = {}
        with contextlib.ExitStack() as es:
            for s in self.stream_ops:
                sems[s] = es.enter_context(nc.semaphore("s_" + s))
            block = es.enter_context(nc.Block())
            engs = {"pe": block.tensor, "act": block.scalar, "dve": block.vector,
                    "pool": block.gpsimd, "sp": block.sync}

            def make(issuer):
                def body(eng):
                    for o in self.per_issuer[issuer]:
                        for d in o.waits:
                            eng.wait_ge(sems[d.stream], d.semval * (16 if d.stream.startswith("d_") else 1))
                        ins = o.fn(eng)
                        if self.debug_src:
                            try:
                                self.srcmap[ins.ins.name] = o.src
                            except Exception:
                                pass
                        if o.signal:
                            ins.then_inc(sems[o.stream], 16 if o.is_dma else 1)
                    if issuer == "sp":
                        for s_, so_ in self.stream_ops.items():
                            if s_.startswith("d_") and so_:
                                eng.wait_ge(sems[s_], so_[-1].semval * 16)
                return body
            for issuer in ("sp", "pe", "act", "dve", "pool"):
                engs[issuer](make(issuer))


def tile_order():
    o = []
    for hg in range(2):
        o += [("gq%d" % hg, 8192), ("gk%d" % hg, 8192), ("gv%d" % hg, 8192), ("ggo%d" % hg, 8192)]
    o += [("mC%d" % j, 6144) for j in range(8)]
    o += [("hi%d" % c, 8192) for c in range(2)]
    for hp in range(4):
        o += [("hA%d" % hp, 8192), ("hB%d" % hp, 8192)]
    o += [("mA%d" % j, 6144) for j in range(8)]
    o += [("cu%d" % c, 8192) for c in range(2)]
    o += [("cv%d" % c, 8192) for c in range(2)]
    o += [("mB%d" % j, 6144) for j in range(8)]
    o += [("wo%d" % j, 8192) for j in range(4)]
    for g in range(4):
        o += [("f1_%d_%d" % (g, c), 8192) for c in range(4)]
        o += [("f2_%d_%d" % (g, j), 8192) for j in range(4)]
    return o


def build_wstream(w_in, wbr_hg, wbr_cm, wbr_gdn, w_out, w_ff1, w_ff2):
    def full(M, c0, n=512):
        return [(M, kc * 128, c0, n) for kc in range(16)]
    spec = {}
    for hg in range(2):
        spec["gq%d" % hg] = full(w_in, C_GQ + hg * 512)
        spec["gk%d" % hg] = full(w_in, C_GK + hg * 512)
        spec["gv%d" % hg] = full(w_in, C_GV + hg * 512)
        spec["ggo%d" % hg] = full(w_in, C_GGO + hg * 512)
    for nm, gc0, wbr in (("mC", C_GATE_C, wbr_gdn), ("mA", C_GATE_A, wbr_hg), ("mB", C_GATE_B, wbr_cm)):
        for j in range(8):
            spec["%s%d" % (nm, j)] = full(w_in, gc0 + j * 256, 256) + [(wbr, kc * 128, j * 256, 256) for kc in range(8)]
    for c in range(2):
        spec["hi%d" % c] = full(w_in, C_HI + c * 512)
        spec["cu%d" % c] = full(w_in, C_CU + c * 512)
        spec["cv%d" % c] = full(w_in, C_CV + c * 512)
    for hp in range(4):
        sA = []
        sB = []
        for kc in range(16):
            sA += [(w_in, kc * 128, C_HQ + hp * 256, 256), (w_in, kc * 128, C_HGO + hp * 256, 256)]
            sB += [(w_in, kc * 128, C_HFF + hp * 256, 256), (w_in, kc * 128, C_HFB + hp * 256, 256)]
        spec["hA%d" % hp] = sA
        spec["hB%d" % hp] = sB
    for j in range(4):
        spec["wo%d" % j] = full(w_out, j * 512)
    for g in range(4):
        for c in range(4):
            spec["f1_%d_%d" % (g, c)] = full(w_ff1, g * 2048 + c * 512)
        for j in range(4):
            spec["f2_%d_%d" % (g, j)] = [(w_ff2, (g * 16 + kc) * 128, j * 512, 512) for kc in range(16)]
    parts = []
    for nm, n in tile_order():
        tot = 0
        for (M, r0, c0, w) in spec[nm]:
            parts.append(M[r0:r0 + 128, c0:c0 + w])
            tot += w
        assert tot == n, (nm, tot, n)
    return np.ascontiguousarray(np.concatenate(parts, axis=1))


def fm(vec, nchunk):
    return np.ascontiguousarray(np.asarray(vec, np.float32).reshape(nchunk, 128).T)


NF = 52992
OFF_CONST = 0
SZ_CONST = 30 * 1024
OFF_H = OFF_CONST + SZ_CONST
OFF_MG = OFF_H + 32 * 1024
OFF_TM = OFF_MG + 32 * 1024
OFF_W = OFF_TM + 16 * 1024
OFF_AR = OFF_W + 32 * 1024
assert OFF_AR + 64 * 1024 <= NF * 4


class _Stop(Exception):
    pass


def build_program(dbg=None, stop=None):
    nc = bass.Bass("TRN2", target_bir_lowering=False)
    WTOT = sum(n for _, n in tile_order())

    def din(name, shape):
        return nc.dram_tensor(name, list(shape), F32, kind="ExternalInput").ap()

    def dout(name, shape):
        return nc.dram_tensor(name, list(shape), F32, kind="ExternalOutput").ap()

    xT = din("xT", [D, T])
    cond = din("cond", [P, 16])
    s0hg = din("s0hg", [DEPTH, P, 16, 128])
    s0gd = din("s0gd", [DEPTH, P, 16, 128])
    flags = din("flags", [P, 16])
    cmask = din("cmask", [2, T])
    ws = [din("ws%d" % l, [P, WTOT]) for l in range(DEPTH)]
    wmod = din("wmod", [P, DEPTH * 24 * 8192])
    wgab = din("wgab", [DEPTH, P, 16 * 32])
    vecs = din("vecs", [P, 512])
    convw = din("convw", [DEPTH, P, 24 * 9])
    cmwsT = din("cmwsT", [DEPTH, P, 8 * 128])
    cmbs = din("cmbs", [DEPTH, 1, 1024])
    rowc = din("rowc", [DEPTH, 1, 64])
    consts = din("consts", [P, 2048])
    yT = dout("yT", [D, T])
    nshg = dout("nshg", [4, DEPTH, 2, 8, P, 128])
    nsgd = dout("nsgd", [4, DEPTH, 2, 8, P, 128])
    xscr = nc.dram_tensor("xscr", [D, T], F32, kind="Internal").ap()
    dbg_out = {}
    if dbg:
        for nm, shp in dbg.items():
            dbg_out[nm] = dout("dbg_" + nm, shp)

    es = contextlib.ExitStack()
    arena = es.enter_context(nc.sbuf_tensor("arena", [P, NF], F32))
    psum = es.enter_context(nc.psum_tensor("psum", [P, 4096], F32))
    S = Sched(nc)
    import os as _os
    S.debug_src = bool(_os.environ.get("KDEBUG_SRC"))
    nc._sched = S
    outs_dma = []

    def V(off, dims, dt=F32):
        n = 1
        for d_ in dims:
            n *= d_
        assert off % 4 == 0
        if dt == F32:
            ap = arena[:, off // 4: off // 4 + n]
        else:
            assert n % 2 == 0
            ap = arena[:, off // 4: off // 4 + n // 2].bitcast(BF16)
        if len(dims) == 2:
            ap = ap.rearrange("p (a b) -> p a b", a=dims[0])
        elif len(dims) == 3:
            ap = ap.rearrange("p (a b c) -> p a b c", a=dims[0], b=dims[1])
        return ap

    class Bump:
        def __init__(self, base, size):
            self.base, self.size, self.cur = base, size, base

        def alloc(self, dims, dt=F32):
            n = 1
            for d_ in dims:
                n *= d_
            nb = n * (4 if dt == F32 else 2)
            nb = (nb + 511) // 512 * 512
            off = self.cur
            self.cur += nb
            assert self.cur <= self.base + self.size, ("region overflow", self.base, self.cur - self.base, self.size)
            return V(off, dims, dt)

        def reset(self, to=None):
            self.cur = self.base if to is None else to

        def mark(self):
            return self.cur

    CR = Bump(OFF_CONST, SZ_CONST)
    MG = Bump(OFF_MG, 32 * 1024)
    TM = Bump(OFF_TM, 16 * 1024)
    AR = Bump(OFF_AR, 64 * 1024)
    h_bf = V(OFF_H, [16, T], BF16)
    wbuf = [V(OFF_W + i * 16384, [8192], BF16) for i in range(2)]

    def isap(x):
        return isinstance(x, AP)

    def dma(q, out, in_, rk=(), wk=()):
        op = S.add(q, lambda e: e.dma_start(out=out, in_=in_), reads=[in_] + list(rk), writes=[out] + list(wk), dma=True)
        return op

    def mm(out, lhsT, rhs, start=True, stop=True):
        S.add("pe", lambda e: e.matmul(out, lhsT=lhsT, rhs=rhs, start=start, stop=stop), reads=[lhsT, rhs], writes=[out])

    def tr(out, in_, ident):
        S.add("pe", lambda e: e.transpose(out, in_, ident), reads=[in_, ident], writes=[out])

    def act(out, in_, func, scale=1.0, bias=0.0):
        rd = [in_] + [x for x in (scale, bias) if isap(x)]
        S.add("act", lambda e: e.activation(out=out, in_=in_, func=func, scale=scale, bias=bias), reads=rd, writes=[out])

    def tt(eng, out, in0, in1, op):
        S.add(eng, lambda e: e.tensor_tensor(out=out, in0=in0, in1=in1, op=op), reads=[in0, in1], writes=[out])

    def ts(eng, out, in0, s1, s2, op0, op1=None):
        rd = [in0] + [x for x in (s1, s2) if isap(x)]
        if op1 is None:
            S.add(eng, lambda e: e.tensor_scalar(out=out, in0=in0, scalar1=s1, scalar2=None, op0=op0), reads=rd, writes=[out])
        else:
            S.add(eng, lambda e: e.tensor_scalar(out=out, in0=in0, scalar1=s1, scalar2=s2, op0=op0, op1=op1), reads=rd, writes=[out])

    def stt(out, in0, scalar, in1, op0, op1):
        rd = [in0, in1] + ([scalar] if isap(scalar) else [])
        S.add("dve", lambda e: e.scalar_tensor_tensor(out=out, in0=in0, scalar=scalar, in1=in1, op0=op0, op1=op1), reads=rd, writes=[out])

    def cp(eng, out, in_):
        if eng == "act":
            act(out, in_, AF.Copy)
        else:
            S.add(eng, lambda e: e.tensor_copy(out=out, in_=in_), reads=[in_], writes=[out])

    def scan(out, d0, d1):
        S.add("dve", lambda e: e.tensor_tensor_scan(out=out, data0=d0, data1=d1, initial=0.0, op0=ALU.mult, op1=ALU.add),
              reads=[d0, d1], writes=[out])

    def rev(ap):
        (ps_, pc_), (st, cnt) = ap.ap
        return AP(ap.tensor, ap.offset + (cnt - 1) * st, [[ps_, pc_], [-st, cnt]])

    def bc(ap, dims, axis):
        return ap.unsqueeze(axis).to_broadcast(dims)

    pspools = {"s": [4, 5, 6, 7], "lo": [0, 1], "g0": [0, 1, 2, 3], "g1": [4, 5, 6, 7], "all": [0, 1, 2, 3, 4, 5, 6, 7]}
    psctr = {k: 0 for k in pspools}
    psctr["p"] = 0

    def PSB(pool="s", bf=False):
        lst = pspools[pool]
        b = lst[psctr[pool] % len(lst)]
        psctr[pool] += 1
        ap = psum[:, b * 512:(b + 1) * 512]
        return ap.bitcast(BF16) if bf else ap

    def PSP():
        p_ = psctr["p"] % 2
        psctr["p"] += 1
        return psum[:, p_ * 1024:(p_ + 1) * 1024]

    def dbgdump(name, ap):
        if dbg and name in dbg_out:
            outs_dma.append(dma("pool" if ap.dtype == BF16 else "sp", dbg_out[name], ap))

    class WStream:
        def __init__(self):
            self.seq = []
            for l in range(DEPTH):
                off = 0
                for nm, n in tile_order():
                    self.seq.append((ws[l], off, n, "L%d_%s" % (l, nm)))
                    off += n
            self.modseq = [(wmod, i * 8192, 8192, "mod%d" % i) for i in range(DEPTH * 24)]
            self.all = list(self.modseq[0:24])
            for ent in self.seq:
                self.all.append(ent)
                if ent[3] == "L0_gv0":
                    self.all += self.modseq[24:36]
                if ent[3] == "L0_gv1":
                    self.all += self.modseq[36:48]
            self.issued = 0
            self.taken = 0

        def _issue(self):
            if self.issued < len(self.all):
                src, off, n, nm = self.all[self.issued]
                buf = wbuf[self.issued % 2]
                dma("pool", buf[:, 0:n], src[:, off:off + n])
                self.issued += 1

        def next(self, name):
            while self.issued < min(self.taken + 2, len(self.all)):
                self._issue()
            src, off, n, nm = self.all[self.taken]
            assert nm == name, (nm, name)
            buf = wbuf[self.taken % 2]
            self.taken += 1
            return buf

        def prefetch(self):
            while self.issued < min(self.taken + 2, len(self.all)):
                self._issue()

    W = WStream()

    def stage(name):
        if stop is not None and name == stop:
            raise _Stop()

    try:
        c_f32 = CR.alloc([640])
        dma("sp", c_f32, consts[:, 0:640])
        blk64_f = c_f32[:, 0:128]
        triF_f = c_f32[:, 128:256]
        triB_f = c_f32[:, 256:384]
        sel_f = [c_f32[:, 384:512], c_f32[:, 512:640]]
        cbf = CR.alloc([1024], BF16)
        dma("pool", cbf, consts[:, 640:1664])
        ident_b = cbf[:, 0:128]
        hmaskF_b = cbf[:, 128:256]
        hmaskB_b = cbf[:, 256:384]
        ones_b = cbf[:, 384:512]
        I64_b = cbf[:, 512:576]
        mSL_b, mSU_b, mLi_b, mUi_b = (cbf[:, 576 + 64 * i: 640 + 64 * i] for i in range(4))
        nb16_b = cbf[:, 832:896]
        E1m_b = cbf[:, 896:960]
        E2m_b = cbf[:, 960:1024]
        vec_t = CR.alloc([512])
        dma("sp", vec_t, vecs)
        flags_t = CR.alloc([16])
        dma("sp", flags_t, flags)
        carry = flags_t[:, 0:1]
        cond_t = CR.alloc([16])
        dma("sp", cond_t, cond)
        rbuf = CR.alloc([T + 64], BF16)
        S.add("dve", lambda e: e.memset(rbuf, 1.0), writes=[rbuf])
        S.add("dve", lambda e: e.memset(rbuf.rearrange("p (c j) -> p c j", j=64)[:, :, 0:1], 0.0), writes=[rbuf])
        reset64 = rbuf[:, 0:T]
        reset63 = rbuf[:, 1:T + 1]
        mLR = CR.alloc([2, T], BF16)
        dma("pool", mLR[:, 0, :], cmask[0:1, :].partition_broadcast(P))
        dma("pool", mLR[:, 1, :], cmask[1:2, :].partition_broadcast(P))
        modv = [CR.alloc([96]) for _ in range(DEPTH)]
        gs1 = [CR.alloc([16]) for _ in range(DEPTH)]
        gs2 = [CR.alloc([16]) for _ in range(DEPTH)]
        lb1 = CR.alloc([16])
        oml1 = CR.alloc([16])
        noml1 = CR.alloc([16])
        zero16 = CR.alloc([16])
        one16 = CR.alloc([16])
        mone16 = CR.alloc([16])
        S.add("dve", lambda e: e.memset(zero16, 0.0), writes=[zero16])
        S.add("dve", lambda e: e.memset(one16, 1.0), writes=[one16])
        S.add("dve", lambda e: e.memset(mone16, -1.0), writes=[mone16])
        scond = CR.alloc([16], BF16)
        cw = CR.alloc([24, 9])
        rowbc = CR.alloc([64])
        negA = CR.alloc([16])
        wgab_b = CR.alloc([16, 32], BF16)
        wsT_b = CR.alloc([8, 128], BF16)
        ab_tm = CR.alloc([8, 32])
        g_tm = CR.alloc([8, 16])
        beta_tm = CR.alloc([8, 16])
        gc_tm = CR.alloc([8, 16])
        bg_tm = CR.alloc([8, 16])
        ekd_tm = CR.alloc([8, 16])
        sm_tmp = CR.alloc([8, 16])
        CR_MARK = CR.mark()

        tt("dve", lb1, vec_t[:, 288:304], vec_t[:, 272:288], ALU.subtract)
        act(lb1, lb1, AF.Sigmoid)
        ts("dve", oml1, lb1, -1.0, 1.0, ALU.mult, ALU.add)
        ts("dve", noml1, lb1, 1.0, -1.0, ALU.mult, ALU.add)

        act(scond, cond_t, AF.Silu)
        def mod_tile(lm, t_, pool="all"):
            pm = PSB(pool)
            wt = W.next("mod%d" % (lm * 24 + t_)).rearrange("p (k c) -> p k c", k=16)
            for nn in range(4):
                for kc in range(KC):
                    mm(pm[:, nn:nn + 1], wt[:, kc, nn * 128:(nn + 1) * 128], scond[:, kc:kc + 1], start=(kc == 0), stop=(kc == KC - 1))
            tt("dve", modv[lm][:, t_ * 4:t_ * 4 + 4], pm[:, 0:4], vec_t[:, 80 + 96 * lm + t_ * 4: 84 + 96 * lm + t_ * 4], ALU.add)

        def mod_finish(lm):
            stt(gs1[lm], modv[lm][:, 16:32], 1.0, vec_t[:, 16 * lm:16 * lm + 16], ALU.add, ALU.mult)
            stt(gs2[lm], modv[lm][:, 64:80], 1.0, vec_t[:, 32 + 16 * lm:48 + 16 * lm], ALU.add, ALU.mult)

        for t_ in range(24):
            mod_tile(0, t_)
        mod_finish(0)
        dbgdump("modv0", modv[0])
        stage("mod")

        def rms_rstd(src_chunks, nfeat, sq, rstd):
            pp = PSP()
            n = len(src_chunks)
            for c, xc in enumerate(src_chunks):
                s_ = sq[c % 2]
                act(s_, xc, AF.Square)
                for hf in range(2):
                    mm(pp[:, hf * 512:(hf + 1) * 512], ones_b, s_[:, hf * 512:(hf + 1) * 512], start=(c == 0), stop=(c == n - 1))
            act(rstd, pp, AF.Ln, scale=1.0 / nfeat, bias=EPS)
            act(rstd, rstd, AF.Exp, scale=-0.5)
            return rstd

        xs = V(OFF_AR, [16, T])

        def norm_mod(gs, sh, out_fn):
            MG.reset()
            sq = [MG.alloc([T], BF16) for _ in range(2)]
            rstd = rms_rstd([xs[:, c, :] for c in range(16)], D, sq, MG.alloc([T]))
            tmp = [MG.alloc([T]) for _ in range(2)]
            for c in range(16):
                t_ = tmp[c % 2]
                tt("dve", t_, xs[:, c, :], rstd, ALU.mult)
                out_fn(c, t_)

        xTv = xT.rearrange("(c p) t -> p c t", p=P)
        yTv = yT.rearrange("(c p) t -> p c t", p=P)
        xsv = xscr.rearrange("(c p) t -> p c t", p=P)

        for l in range(DEPTH):
            if l == 0:
                for c in range(16):
                    dma("sp", xs[:, c, :], xTv[:, c, :])
            sh1 = modv[l][:, 0:16]
            gate1 = modv[l][:, 32:48]
            sh2 = modv[l][:, 48:64]
            gate2 = modv[l][:, 80:96]

            def out_h(c, t_, _gs=gs1[l], _sh=sh1):
                act(h_bf[:, c, :], t_, AF.Identity, scale=_gs[:, c:c + 1], bias=_sh[:, c:c + 1])
            norm_mod(gs1[l], sh1, out_h)
            if l > 0:
                for c in range(16):
                    dma("sp", xsv[:, c, :], xs[:, c, :], wk=[("xscr", c)])
            dbgdump("h%d" % l, h_bf[:, 0, :])
            stage("norm1_%d" % l)

            dma("sp", cw, convw[l])
            tt("dve", cw, cw, bc(flags_t[:, 1:10], [P, 24, 9], 1), ALU.mult)
            dma("sp", rowbc, rowc[l].partition_broadcast(P))
            act(negA, rowbc[:, 0:16], AF.Exp)
            ts("dve", negA, negA, -1.0, None, ALU.mult)
            dma("pool", wgab_b, wgab[l].rearrange("p (k c) -> p k c", k=16))
            dma("pool", wsT_b, cmwsT[l].rearrange("p (g q) -> p g q", g=8))
            if l == 0:
                lbv, omlv, nomlv = zero16, one16, mone16
            else:
                lbv, omlv, nomlv = lb1, oml1, noml1
            merged = V(OFF_MG, [16, T], BF16)

            def merge_branch(o_br, tag, first):
                sg = [AR_tmp_sig[0], AR_tmp_sig[1]]
                for j8 in range(8):
                    wt = W.next("L%d_m%s%d" % (l, tag, j8))
                    wg = wt[:, 0:4096].rearrange("p (k c) -> p k c", k=16)
                    wb = wt[:, 4096:6144].rearrange("p (k c) -> p k c", k=8)
                    for jj in range(2):
                        j = j8 * 2 + jj
                        pg = PSP()
                        for hf in range(2):
                            for kc in range(KC):
                                mm(pg[:, hf * 512:(hf + 1) * 512], wg[:, kc, jj * 128:(jj + 1) * 128], h_bf[:, kc, hf * 512:(hf + 1) * 512],
                                   start=(kc == 0), stop=(kc == KC - 1))
                        pb = PSP()
                        for hf in range(2):
                            for kc in range(8):
                                mm(pb[:, hf * 512:(hf + 1) * 512], wb[:, kc, jj * 128:(jj + 1) * 128], o_br[:, kc, hf * 512:(hf + 1) * 512],
                                   start=(kc == 0), stop=(kc == 7))
                        s_ = sg[j % 2]
                        act(s_, pg, AF.Sigmoid)
                        if first:
                            tt("dve", merged[:, j, :], pb, s_, ALU.mult)
                        else:
                            tt("dve", s_, pb, s_, ALU.mult)
                            tt("pool", merged[:, j, :], merged[:, j, :], s_, ALU.add)

            AR.reset()
            TM.reset()
            MG.reset()
            o_c = AR.alloc([8, T], BF16)
            AR_C0 = AR.mark()
            pab = PSB()
            pabv = pab[:, 0:256].rearrange("p (t c) -> p t c", t=8)
            for tt_ in range(8):
                for kc in range(KC):
                    mm(pabv[:, tt_, :], h_bf[:, kc, tt_ * 128:(tt_ + 1) * 128], wgab_b[:, kc, :], start=(kc == 0), stop=(kc == KC - 1))
            cp("dve", ab_tm, pabv)
            tt("dve", g_tm, ab_tm[:, :, 0:16], bc(rowbc[:, 16:32], [P, 8, 16], 1), ALU.add)
            act(g_tm, g_tm, AF.Exp)
            act(g_tm, g_tm, AF.Ln, bias=1.0)
            tt("dve", g_tm, g_tm, bc(negA, [P, 8, 16], 1), ALU.mult)
            act(beta_tm, ab_tm[:, :, 16:32], AF.Sigmoid)
            pgc = PSB()
            pgcv = pgc[:, 0:128].rearrange("p (t c) -> p t c", t=8)
            pgl = PSB()
            pglv = pgl[:, 0:128].rearrange("p (t c) -> p t c", t=8)
            for tt_ in range(8):
                mm(pgcv[:, tt_, 0:8], triF_f, g_tm[:, tt_, 0:8])
                mm(pgcv[:, tt_, 8:16], triB_f, g_tm[:, tt_, 8:16])
                mm(pglv[:, tt_, :], blk64_f, g_tm[:, tt_, :])
            cp("dve", gc_tm, pgcv)
            tt("dve", ekd_tm, pglv, gc_tm, ALU.subtract)
            act(ekd_tm, ekd_tm, AF.Exp)
            act(sm_tmp, gc_tm, AF.Exp)
            tt("dve", bg_tm, sm_tmp, beta_tm, ALU.mult)
            dbgdump("gc_tm", gc_tm.rearrange("p a b -> p (a b)"))
            dbgdump("beta_tm", beta_tm.rearrange("p a b -> p (a b)"))
            stage("gdn_tok%d" % l)

            for hg in range(2):
                AR.reset(AR_C0)
                TM.reset()
                MG.reset()
                kT = AR.alloc([4, T], BF16)
                qT = AR.alloc([4, T], BF16)
                opart = AR.alloc([4, T], BF16)
                Sst = AR.alloc([8, 128])
                Sbf = AR.alloc([8, 128], BF16)
                k_tm = TM.alloc([8, 512], BF16)
                v_tm = TM.alloc([8, 512], BF16)
                AR_C1 = AR.mark()
                csets = []
                for si in range(2):
                    reg = AR if si == 0 else MG
                    cs_ = {}
                    for nm in ("xc", "xL", "xR", "acc"):
                        cs_[nm] = reg.alloc([T])
                    cs_["vT"] = reg.alloc([T], BF16)
                    cs_["sq"] = MG.alloc([T], BF16)
                    cs_["rstd"] = MG.alloc([T])
                    csets.append(cs_)
                cchunks = [(part, h4) for part in range(3) for h4 in range(4)]
                wts_ = {}

                def conv_s1(j):
                    part, h4 = cchunks[j]
                    if part not in wts_:
                        wts_[part] = W.next("L%d_g%s%d" % (l, "qkv"[part], hg)).rearrange("p (k c) -> p k c", k=16)
                    wt = wts_[part]
                    cs_ = csets[j % 2]
                    pp = PSP()
                    for hf in range(2):
                        for kc in range(KC):
                            mm(pp[:, hf * 512:(hf + 1) * 512], wt[:, kc, h4 * 128:(h4 + 1) * 128], h_bf[:, kc, hf * 512:(hf + 1) * 512],
                               start=(kc == 0), stop=(kc == KC - 1))
                    cp("act", cs_["xc"], pp)
                    tt("pool", cs_["xL"], cs_["xc"], mLR[:, 0, :], ALU.mult)
                    tt("pool", cs_["xR"], cs_["xc"], mLR[:, 1, :], ALU.mult)

                def conv_s2(j):
                    part, h4 = cchunks[j]
                    cc = part * 8 + hg * 4 + h4
                    cs_ = csets[j % 2]
                    acc = cs_["acc"]
                    act(acc, cs_["xc"], AF.Identity, scale=cw[:, cc, 4:5])
                    for dr in range(3):
                        for dc in range(3):
                            if dr == 1 and dc == 1:
                                continue
                            off = (dr - 1) * 64 + (dc - 1)
                            src = (cs_["xL"], cs_["xc"], cs_["xR"])[dc]
                            a0 = max(0, -off)
                            a1 = min(T, T - off)
                            stt(acc[:, a0:a1], src[:, a0 + off:a1 + off], cw[:, cc, dr * 3 + dc: dr * 3 + dc + 1], acc[:, a0:a1], ALU.mult, ALU.add)

                def conv_s3(j):
                    part, h4 = cchunks[j]
                    cs_ = csets[j % 2]
                    acc = cs_["acc"]
                    if part == 2:
                        act(cs_["vT"], acc, AF.Silu)
                        srcT = cs_["vT"]
                        dst = v_tm
                    else:
                        act(acc, acc, AF.Silu)
                        rstd = rms_rstd([acc], 1.0, [cs_["sq"]], cs_["rstd"])
                        dstT = (qT, kT)[part]
                        if part == 0:
                            stt(dstT[:, h4, :], acc, 128.0 ** -0.5, rstd, ALU.mult, ALU.mult)
                        else:
                            tt("dve", dstT[:, h4, :], acc, rstd, ALU.mult)
                        srcT = kT[:, h4, :]
                        dst = k_tm
                    if part >= 1:
                        for half in range(2):
                            pt = PSB("s", bf=True)
                            for q4 in range(4):
                                tt_ = half * 4 + q4
                                tr(pt[:, q4 * 128:(q4 + 1) * 128], srcT[:, tt_ * 128:(tt_ + 1) * 128], ident_b)
                            cp("act", dst[:, half * 4:(half + 1) * 4, h4 * 128:(h4 + 1) * 128],
                               pt[:, 0:512].rearrange("p (a b) -> p a b", a=4))

                conv_s1(0)
                for j in range(12):
                    if j + 1 < 12:
                        conv_s1(j + 1)
                    conv_s2(j)
                    if j >= 1:
                        conv_s3(j - 1)
                conv_s3(11)
                if hg == 0:
                    dbgdump("kT", kT[:, 0, :])
                    dbgdump("qT", qT[:, 0, :])
                    dbgdump("v_tm", v_tm.rearrange("p a b -> p (a b)"))
                stage("gdn_conv%d_%d" % (l, hg))
                for d_ in range(2):
                    dma("sp", Sst[:, d_ * 4:(d_ + 1) * 4, :], s0gd[l][:, d_ * 8 + hg * 4: d_ * 8 + hg * 4 + 4, :])
                cp("act", Sbf, Sst)
                if l == 0 and hg == 0:
                    stage("gdn_st")
                AR.reset(AR_C1)
                MG.reset()
                NW = 256

                class TB:
                    pass
                sets = []
                def anyalloc(dims, dt=F32):
                    n = 1
                    for d__ in dims:
                        n *= d__
                    nb = (n * (4 if dt == F32 else 2) + 511) // 512 * 512
                    reg = MG if MG.cur + nb <= MG.base + MG.size else AR
                    return reg.alloc(dims, dt)
                for si in range(2):
                    b = TB()
                    b.pool = "g%d" % si
                    b.M = anyalloc([NW])
                    b.Dm = anyalloc([NW])
                    b.E2 = anyalloc([NW])
                    b.ERd = [[anyalloc([NW]) for _ in range(2)] for _ in range(2)]
                    b.tmp = anyalloc([NW])
                    b.pbanks = (0, 1) if si == 0 else (2, 3)
                    b.sbanks = (4, 5) if si == 0 else (6, 7)
                    for nm in ("A", "AT", "P0", "P0T", "R", "RT", "E1a", "E1Ta", "E2a", "Pa", "PaT", "Pb", "PbT", "Y", "Z", "T2", "W2", "W3"):
                        setattr(b, nm, anyalloc([NW], BF16))
                    b.attnTd = [anyalloc([NW], BF16) for _ in range(2)]
                    b.vbd = [anyalloc([512], BF16) for _ in range(2)]
                    b.kbg = anyalloc([512], BF16)
                    b.kdecd = [anyalloc([512], BF16) for _ in range(2)]
                    b.negwT = anyalloc([512], BF16)
                    b.vnew = anyalloc([512], BF16)
                    sets.append(b)

                def v3(ap, a):
                    return ap.rearrange("p (a b) -> p a b", a=a)

                def mmblk(ps, lhsT_src, rhs_src):
                    for c2 in range(2):
                        pr = slice(c2 * 64, (c2 + 1) * 64)
                        for h4 in range(4):
                            cs = slice(h4 * 64, (h4 + 1) * 64)
                            mm(ps[pr, cs], lhsT_src[pr, cs], rhs_src[pr, cs])

                def gdn_prep(tt_, d_, b, par):
                    def H(k_, half_):
                        bb = b.pbanks[k_]
                        return psum[:, bb * 512 + half_ * 256: bb * 512 + half_ * 256 + 256]
                    ER_ = b.ERd[par]
                    attnT_ = b.attnTd[par]
                    vb_ = b.vbd[par]
                    kdec_ = b.kdecd[par]
                    hs = slice(d_ * 8 + hg * 4, d_ * 8 + hg * 4 + 4)
                    gcs = gc_tm[:, tt_, hs]
                    bts = beta_tm[:, tt_, hs]
                    mA = (mSL_b, mSU_b)[d_]
                    mT = (mUi_b, mLi_b)[d_]
                    pk = H(0, 0)
                    pq = H(0, 1)
                    for c2 in range(2):
                        pr = slice(c2 * 64, (c2 + 1) * 64)
                        tok = slice(tt_ * 128 + c2 * 64, tt_ * 128 + c2 * 64 + 64)
                        for h4 in range(4):
                            cs = slice(h4 * 64, (h4 + 1) * 64)
                            mm(pk[pr, cs], kT[:, h4, tok], kT[:, h4, tok])
                            mm(pq[pr, cs], kT[:, h4, tok], qT[:, h4, tok])
                    first_ = (l == 0 and hg == 0 and tt_ == 0 and d_ == 0)
                    sec_ = (l == 0 and hg == 0 and tt_ == 7 and d_ == 1)
                    if first_:
                        stage("p_a")
                    if sec_:
                        stage("q_a")
                    tt("pool", v3(b.M, 4), bc(gcs, [P, 4, 64], 2), bc(I64_b, [P, 4, 64], 1), ALU.mult)
                    if first_:
                        stage("p_b")
                    if sec_:
                        stage("q_b")
                    pr_ = [H(1, 0), H(1, 1)]
                    for c2 in range(2):
                        prr = slice(c2 * 64, (c2 + 1) * 64)
                        mm(pr_[c2][:, 0:NW], sel_f[c2], b.M)
                        if first_ and c2 == 0:
                            stage("p_c")
                        if sec_:
                            stage("q_c%d" % c2)
                        act(ER_[c2], pr_[c2][:, 0:NW], AF.Exp)
                        if first_ and c2 == 0:
                            stage("p_d")
                        if sec_:
                            stage("q_d%d" % c2)
                        tt("dve", v3(b.Dm[prr, :], 4), bc(gcs[prr, :], [64, 4, 64], 2), v3(pr_[c2][prr, 0:NW], 4), ALU.subtract)
                        if first_:
                            stage("p_e%d" % c2)
                        if sec_:
                            stage("q_e%d" % c2)
                    yield
                    ts("dve", b.E2, b.Dm, -1.0, 0.0, ALU.mult, ALU.min)
                    ts("dve", b.Dm, b.Dm, 0.0, None, ALU.min)
                    act(b.Dm, b.Dm, AF.Exp)
                    act(b.E2, b.E2, AF.Exp)
                    tt("pool", v3(b.Dm, 4), v3(b.Dm, 4), bc(mA, [P, 4, 64], 1), ALU.mult)
                    tt("pool", v3(b.Dm, 4), v3(b.Dm, 4), bc(bts, [P, 4, 64], 2), ALU.mult)
                    tt("dve", b.A, pk[:, 0:NW], b.Dm, ALU.mult)
                    tt("pool", v3(b.E2, 4), v3(b.E2, 4), bc(mT, [P, 4, 64], 1), ALU.mult)
                    tt("dve", attnT_, pq[:, 0:NW], b.E2, ALU.mult)
                    pa = psum[:, b.pbanks[0] * 512: b.pbanks[0] * 512 + 512].bitcast(BF16)
                    for c2 in range(2):
                        prr = slice(c2 * 64, (c2 + 1) * 64)
                        for h4 in range(4):
                            cs = slice(h4 * 64, (h4 + 1) * 64)
                            tr(pa[prr, cs], b.A[prr, cs], ident_b[prr, prr])
                    cp("act", b.AT, pa[:, 0:NW])
                    yield
                    nb = bc(nb16_b, [P, 4, 64], 1)
                    tt("dve", v3(b.P0, 4), v3(b.A, 4), nb, ALU.mult)
                    tt("dve", v3(b.P0T, 4), v3(b.AT, 4), nb, ALU.mult)
                    tt("pool", v3(b.R, 4), v3(b.P0, 4), bc(I64_b, [P, 4, 64], 1), ALU.add)
                    tt("pool", v3(b.RT, 4), v3(b.P0T, 4), bc(I64_b, [P, 4, 64], 1), ALU.add)
                    tt("pool", v3(b.E1a, 4), v3(b.A, 4), bc(E1m_b, [P, 4, 64], 1), ALU.mult)
                    tt("pool", v3(b.E1Ta, 4), v3(b.AT, 4), bc(E1m_b, [P, 4, 64], 1), ALU.mult)
                    tt("pool", v3(b.E2a, 4), v3(b.A, 4), bc(E2m_b, [P, 4, 64], 1), ALU.mult)
                    tt("pool", v3(vb_, 4), v3(v_tm[:, tt_, :], 4), bc(bts, [P, 4, 128], 2), ALU.mult)
                    tt("pool", v3(b.kbg, 4), v3(k_tm[:, tt_, :], 4), bc(bg_tm[:, tt_, hs], [P, 4, 128], 2), ALU.mult)
                    tt("pool", v3(kdec_, 4), v3(k_tm[:, tt_, :], 4), bc(ekd_tm[:, tt_, hs], [P, 4, 128], 2), ALU.mult)
                    Ps = [(b.P0, b.P0T), (b.Pa, b.PaT), (b.Pb, b.PbT), (b.Y, b.Z)]
                    for rnd_ in range(4):
                        Pc, PcT = Ps[rnd_]
                        if rnd_ < 3:
                            Pn, PnT = Ps[rnd_ + 1]
                            p1 = H(0, 0)
                            p2 = H(0, 1)
                            mmblk(p1, PcT, Pc)
                            mmblk(p2, Pc, PcT)
                        if rnd_ >= 1:
                            p3 = H(1, 0)
                            p4 = H(1, 1)
                            mmblk(p3, PcT, b.R)
                            mmblk(p4, Pc, b.RT)
                        if rnd_ < 3:
                            cp("act", Pn, p1[:, 0:NW])
                            cp("act", PnT, p2[:, 0:NW])
                        if rnd_ >= 1:
                            tt("dve", b.R, p3[:, 0:NW], b.R, ALU.add)
                            tt("dve", b.RT, p4[:, 0:NW], b.RT, ALU.add)
                        yield
                    p1 = H(0, 0)
                    p2 = H(0, 1)
                    mmblk(p1, b.E1Ta, b.R)
                    mmblk(p2, b.E1a, b.RT)
                    cp("act", b.Y, p1[:, 0:NW])
                    cp("dve", b.Z, p2[:, 0:NW])
                    yield
                    p3 = H(1, 0)
                    p4 = H(1, 1)
                    mmblk(p3, b.RT, b.Y)
                    mmblk(p4, b.R, b.Z)
                    tt("dve", b.T2, b.R, p3[:, 0:NW], ALU.subtract)
                    tt("dve", b.W2, b.RT, p4[:, 0:NW], ALU.subtract)
                    yield
                    p1 = H(0, 0)
                    mmblk(p1, b.E2a, b.W2)
                    cp("act", b.Z, p1[:, 0:NW])
                    yield
                    p3 = H(1, 0)
                    mmblk(p3, b.T2, b.Z)
                    tt("dve", b.W3, b.W2, p3[:, 0:NW], ALU.subtract)
                    yield
                    for c2 in range(2):
                        prr = slice(c2 * 64, (c2 + 1) * 64)
                        pw = H(c2, 0)
                        for h4 in range(4):
                            mm(pw[:, h4 * 64:(h4 + 1) * 64], b.kbg[prr, h4 * 128:(h4 + 1) * 128], b.W3[prr, h4 * 64:(h4 + 1) * 64])
                        act(b.negwT[:, c2 * 256:(c2 + 1) * 256], pw[:, 0:256], AF.Identity, scale=-1.0)
                    yield

                def gdn_state(tt_, d_, b, par):
                    ER_ = b.ERd[par]
                    attnT_ = b.attnTd[par]
                    vb_ = b.vbd[par]
                    kdec_ = b.kdecd[par]
                    bC, bD = b.sbanks
                    order = (0, 1) if d_ == 0 else (1, 0)
                    for c2 in order:
                        c = tt_ * 2 + c2
                        prr = slice(c2 * 64, (c2 + 1) * 64)
                        tok = slice(c * 64, c * 64 + 64)
                        pv = psum[:, bC * 512:(bC + 1) * 512]
                        for h4 in range(4):
                            cs = slice(h4 * 128, (h4 + 1) * 128)
                            mm(pv[prr, cs], b.W3[prr, h4 * 64:(h4 + 1) * 64], vb_[prr, cs], start=True, stop=False)
                            mm(pv[prr, cs], b.negwT[:, c2 * 256 + h4 * 64: c2 * 256 + (h4 + 1) * 64], Sbf[:, d_ * 4 + h4, :], start=False, stop=True)
                        cp("act", b.vnew[prr, :], pv[prr, :])
                        yield
                        poi = psum[:, bD * 512: bD * 512 + 256]
                        poa = psum[:, bD * 512 + 256: bD * 512 + 512]
                        pds = psum[:, bC * 512:(bC + 1) * 512]
                        for h4 in range(4):
                            cs = slice(h4 * 128, (h4 + 1) * 128)
                            c64 = slice(h4 * 64, (h4 + 1) * 64)
                            mm(poi[:, c64], Sbf[:, d_ * 4 + h4, :], qT[:, h4, tok])
                            mm(poa[:, c64], b.vnew[prr, cs], attnT_[prr, c64])
                            mm(pds[:, cs], kdec_[prr, cs], b.vnew[prr, cs])
                        tt("dve", b.tmp, poi[:, 0:NW], ER_[c2], ALU.mult)
                        first = (tt_ <= 3) if d_ == 0 else (tt_ >= 4)
                        ov = opart[:, :, tok]
                        if first:
                            tt("dve", ov, v3(b.tmp, 4), v3(poa[:, 0:NW], 4), ALU.add)
                        else:
                            tt("dve", b.tmp, b.tmp, poa[:, 0:NW], ALU.add)
                            tt("pool", ov, ov, v3(b.tmp, 4), ALU.add)
                        Sd = Sst[:, d_ * 4:(d_ + 1) * 4, :]
                        jl = 63 if d_ == 0 else 0
                        egl = v3(ER_[c2], 4)[:, :, jl:jl + 1].to_broadcast([P, 4, 128])
                        tt("pool", Sd, Sd, egl, ALU.mult)
                        last = (c == 15) if d_ == 0 else (c == 0)
                        segend = (c % 4 == 3) if d_ == 0 else (c % 4 == 0)
                        if not segend:
                            tt("dve", Sbf[:, d_ * 4:(d_ + 1) * 4, :], Sd, v3(pds, 4), ALU.add)
                            tt("dve", Sd, Sd, v3(pds, 4), ALU.add)
                        else:
                            tt("dve", Sd, Sd, v3(pds, 4), ALU.add)
                            seg = c // 4
                            outs_dma.append(dma("sp", nsgd[seg, l, d_, hg * 4:hg * 4 + 4].rearrange("h p v -> p h v"), Sd))
                            if not last:
                                ts("dve", Sd, Sd, carry, None, ALU.mult)
                                cp("act", Sbf[:, d_ * 4:(d_ + 1) * 4, :], Sd)
                        yield

                def run_gens(gens):
                    gens = list(gens)
                    if stop == "only_g2":
                        gens = gens[1:]
                    if stop == "only_g1":
                        gens = gens[:1]
                    rnd = 0
                    while gens:
                        rnd += 1
                        if l == 0 and hg == 0:
                            stage("gp_r%d" % rnd)
                            if rnd == 2 and stop in ("only_g1", "only_g2"):
                                raise _Stop()
                        nxt = []
                        for g_ in gens:
                            try:
                                next(g_)
                                nxt.append(g_)
                            except StopIteration:
                                pass
                        gens = nxt

                run_gens([gdn_prep(0, 0, sets[0], 0), gdn_prep(7, 1, sets[1], 0)])
                for i in range(8):
                    gens_ = [gdn_state(i, 0, sets[0], i % 2), gdn_state(7 - i, 1, sets[1], i % 2)]
                    if i < 7:
                        gens_ += [gdn_prep(i + 1, 0, sets[0], (i + 1) % 2), gdn_prep(6 - i, 1, sets[1], (i + 1) % 2)]
                    run_gens(gens_)
                    if l == 0:
                        for k_ in range(2 if i % 2 == 0 else 1):
                            mod_tile(1, 24 + 0 * 0 + hg * 12 + (i // 2) * 3 + (i % 2) * 2 + k_ - 24)
                        if hg == 1 and i == 7:
                            mod_finish(1)
                if hg == 0:
                    dbgdump("opart", opart[:, 0, :])
                stage("gdn_loop%d_%d" % (l, hg))
                AR.reset(AR_C1)
                MG.reset()
                wt = W.next("L%d_ggo%d" % (l, hg)).rearrange("p (k c) -> p k c", k=16)
                go = [AR.alloc([T], BF16) for _ in range(2)]
                on = [AR.alloc([T]) for _ in range(2)]
                for h4 in range(4):
                    pp = PSP()
                    for hf in range(2):
                        for kc in range(KC):
                            mm(pp[:, hf * 512:(hf + 1) * 512], wt[:, kc, h4 * 128:(h4 + 1) * 128], h_bf[:, kc, hf * 512:(hf + 1) * 512],
                               start=(kc == 0), stop=(kc == KC - 1))
                    act(go[h4 % 2], pp, AF.Silu)
                    MG.reset()
                    rstd = rms_rstd([opart[:, h4, :]], 128.0, [MG.alloc([T], BF16)], MG.alloc([T]))
                    stt(on[h4 % 2], opart[:, h4, :], vec_t[:, 306 + l:307 + l], rstd, ALU.mult, ALU.mult)
                    tt("dve", o_c[:, hg * 4 + h4, :], on[h4 % 2], go[h4 % 2], ALU.mult)
            dbgdump("o_c", o_c[:, 0, :])
            stage("gdn_fin%d" % l)
            AR.reset(AR_C0)
            AR_tmp_sig = [AR.alloc([T], BF16) for _ in range(2)]
            merge_branch(o_c, "C", True)
            dbgdump("mergedC", merged[:, 0, :])
            stage("mergeC%d" % l)

            AR.reset()
            TM.reset()
            o_a = AR.alloc([8, T], BF16)
            v_tm = TM.alloc([8, T], BF16)
            for cg in range(2):
                wt = W.next("L%d_hi%d" % (l, cg)).rearrange("p (k c) -> p k c", k=16)
                for tt_ in range(8):
                    pp = PSB()
                    for kc in range(KC):
                        mm(pp, h_bf[:, kc, tt_ * 128:(tt_ + 1) * 128], wt[:, kc, :], start=(kc == 0), stop=(kc == KC - 1))
                    cp("act" if tt_ % 2 else "dve", v_tm[:, tt_, cg * 512:(cg + 1) * 512], pp)
            AR_A0 = AR.mark()
            for hp in range(4):
                AR.reset(AR_A0)
                q_b = AR.alloc([2, T], BF16)
                go_b = AR.alloc([2, T], BF16)
                oacc = AR.alloc([2, T])
                wA = W.next("L%d_hA%d" % (l, hp)).rearrange("p (k c) -> p k c", k=16)
                for i in range(4):
                    pp = PSP()
                    for hf in range(2):
                        for kc in range(KC):
                            mm(pp[:, hf * 512:(hf + 1) * 512], wA[:, kc, i * 128:(i + 1) * 128], h_bf[:, kc, hf * 512:(hf + 1) * 512],
                               start=(kc == 0), stop=(kc == KC - 1))
                    act((q_b, go_b)[i // 2][:, i % 2, :], pp, AF.Silu)
                wB = W.next("L%d_hB%d" % (l, hp)).rearrange("p (k c) -> p k c", k=16)
                sig = AR.alloc([T])
                lf = AR.alloc([T])
                kk = AR.alloc([T])
                Bc = AR.alloc([T])
                t1, t2 = sig, lf
                sb = {}
                for nm in ("q16", "kl0", "kl1", "qd", "kds", "sT"):
                    sb[nm] = AR.alloc([T], BF16)
                S.add("pool", (lambda ap_: lambda e: e.memset(ap_, 0.0))(sb["sT"]), writes=[sb["sT"]])
                sb["kds_tm"] = AR.alloc([8, 128], BF16)
                sb["eBl"] = AR.alloc([16])
                sb["S"] = AR.alloc([128])
                sb["Sbf"] = AR.alloc([128], BF16)
                sb["Sbf2"] = AR.alloc([128], BF16)
                gate_pp = psum[:, 2 * 512:4 * 512]

                def gate_proj_part(i_, part_):
                    for idx_ in range(part_ * 8, part_ * 8 + 8):
                        hf, kc = idx_ // KC, idx_ % KC
                        mm(gate_pp[:, hf * 512:(hf + 1) * 512], wB[:, kc, i_ * 128:(i_ + 1) * 128], h_bf[:, kc, hf * 512:(hf + 1) * 512],
                           start=(kc == 0), stop=(kc == KC - 1))
                for part_ in range(4):
                    gate_proj_part(0, part_)
                for i in range(4):
                    d_ = i // 2
                    hh = i % 2
                    h_ = hp * 2 + hh
                    act(sig, gate_pp, AF.Sigmoid)
                    lbc = lbv[:, d_ * 8 + h_: d_ * 8 + h_ + 1]
                    omc = omlv[:, d_ * 8 + h_: d_ * 8 + h_ + 1]
                    nomc = nomlv[:, d_ * 8 + h_: d_ * 8 + h_ + 1]
                    ts("dve", lf, sig, omc, lbc, ALU.mult, ALU.add)
                    act(lf, lf, AF.Ln)
                    ts("dve", kk, sig, nomc, omc, ALU.mult, ALU.add)
                    if d_ == 0:
                        scan(Bc, reset64, lf)
                    else:
                        scan(rev(Bc), rev(reset63), rev(lf))
                    Bv = Bc.rearrange("p (c j) -> p c j", j=64)
                    jl = 63 if d_ == 0 else 0
                    qh = q_b[:, hh, :]
                    B4 = Bc.rearrange("p (c i j) -> p c i j", i=4, j=16)
                    t14 = t1.rearrange("p (c i j) -> p c i j", i=4, j=16)
                    if d_ == 0:
                        cp("dve", t14[:, :, 0, :], B4[:, :, 0, :])
                        tt("dve", t14[:, :, 1:4, :], B4[:, :, 1:4, :], B4[:, :, 0:3, 15:16].to_broadcast([P, 16, 3, 16]), ALU.subtract)
                    else:
                        cp("dve", t14[:, :, 3, :], B4[:, :, 3, :])
                        tt("dve", t14[:, :, 0:3, :], B4[:, :, 0:3, :], B4[:, :, 1:4, 0:1].to_broadcast([P, 16, 3, 16]), ALU.subtract)
                    act(t1, t1, AF.Exp)
                    tt("dve", sb["q16"], qh, t1, ALU.mult)
                    tt("dve", t2.rearrange("p (c j) -> p c j", j=64), Bv[:, :, jl:jl + 1].to_broadcast([P, 16, 64]), Bv, ALU.subtract)
                    act(sb["eBl"], Bv[:, :, jl], AF.Exp)
                    act(t2, t2, AF.Exp)
                    tt("dve", sb["kds"], kk, t2, ALU.mult)
                    act(t1, Bc, AF.Exp)
                    tt("dve", sb["qd"], qh, t1, ALU.mult)
                    for half in range(2):
                        pt = PSB("s", bf=True)
                        for q4 in range(4):
                            tt_ = half * 4 + q4
                            tr(pt[:, q4 * 128:(q4 + 1) * 128], sb["kds"][:, tt_ * 128:(tt_ + 1) * 128], ident_b)
                        cp("act", sb["kds_tm"][:, half * 4:(half + 1) * 4, :], pt[:, 0:512].rearrange("p (a b) -> p a b", a=4))
                    hm = (hmaskF_b, hmaskB_b)[d_]
                    pts = [PSB("lo"), PSB("lo")]
                    kk3 = kk.rearrange("p (c j) -> p c j", j=64)
                    t13 = t1.rearrange("p (c j) -> p c j", j=64)
                    for ip, I in enumerate((0, 1, 2, 3) if d_ == 0 else (3, 2, 1, 0)):
                        kl = sb["kl%d" % (ip % 2)]
                        kl3 = kl.rearrange("p (c j) -> p c j", j=64)
                        if ip < 2:
                            S.add("pool", (lambda ap_: lambda e: e.memset(ap_, 0.0))(kl), writes=[kl])
                        rng_ = slice(0, 16 * (I + 1)) if d_ == 0 else slice(16 * I, 64)
                        n_ = rng_.stop - rng_.start
                        if d_ == 0:
                            if I == 0:
                                ts("dve", t13[:, :, rng_], Bv[:, :, rng_], -1.0, None, ALU.mult)
                            else:
                                tt("dve", t13[:, :, rng_], Bv[:, :, 16 * I - 1:16 * I].to_broadcast([P, 16, n_]), Bv[:, :, rng_], ALU.subtract)
                        else:
                            if I == 3:
                                ts("dve", t13[:, :, rng_], Bv[:, :, rng_], -1.0, None, ALU.mult)
                            else:
                                tt("dve", t13[:, :, rng_], Bv[:, :, 16 * (I + 1):16 * (I + 1) + 1].to_broadcast([P, 16, n_]), Bv[:, :, rng_], ALU.subtract)
                        act(t13[:, :, rng_], t13[:, :, rng_], AF.Exp)
                        tt("dve", kl3[:, :, rng_], kk3[:, :, rng_], t13[:, :, rng_], ALU.mult)
                        for tb in range(8):
                            for c2 in range(2):
                                c = tb * 2 + c2
                                col = (tb % 4) * 128 + c2 * 64 + I * 16
                                mm(pts[tb // 4][c2 * 64:(c2 + 1) * 64, col:col + 16], kl[:, c * 64:(c + 1) * 64], sb["q16"][:, c * 64 + I * 16: c * 64 + I * 16 + 16])
                    for half in range(2):
                        for c2 in range(2):
                            prr = slice(c2 * 64, (c2 + 1) * 64)
                            cs_ = slice(c2 * 64, (c2 + 1) * 64)
                            tt("dve", sb["sT"][prr, half * 512:(half + 1) * 512].rearrange("p (a b) -> p a b", a=4)[:, :, cs_],
                               pts[half][prr, :].rearrange("p (a b) -> p a b", a=4)[:, :, cs_], bc(hm[prr, cs_], [64, 4, 64], 1), ALU.mult)
                    dma("sp", sb["S"], s0hg[l][:, d_ * 8 + h_, :])
                    sring = [sb["Sbf"], sb["Sbf2"]]
                    cur = 0
                    cp("act", sring[cur], sb["S"])
                    pdb = [PSB("s") for _ in range(4)]
                    def pdt(c):
                        tb_, c2_ = c // 2, c % 2
                        return pdb[c2_ * 2 + tb_ // 4][:, (tb_ % 4) * 128:(tb_ % 4 + 1) * 128]
                    for c in range(16):
                        tb, c2 = c // 2, c % 2
                        prr = slice(c2 * 64, (c2 + 1) * 64)
                        mm(pdt(c), sb["kds_tm"][prr, tb, :], v_tm[prr, tb, h_ * 128:(h_ + 1) * 128])
                    tbs = range(8) if d_ == 0 else range(7, -1, -1)
                    po = None
                    for n_, tb in enumerate(tbs):
                        if i < 3 and n_ % 2 == 0:
                            gate_proj_part(i + 1, n_ // 2)
                        if n_ % 4 == 0:
                            po = PSB("lo")
                            hfidx = tb // 4
                        q4 = tb % 4
                        blk = slice(tb * 128, (tb + 1) * 128)
                        mm(po[:, q4 * 128:(q4 + 1) * 128], v_tm[:, tb, h_ * 128:(h_ + 1) * 128], sb["sT"][:, blk], start=True, stop=False)
                        c2s = (0, 1) if d_ == 0 else (1, 0)
                        for ci, c2 in enumerate(c2s):
                            c = tb * 2 + c2
                            tok = slice(c * 64, (c + 1) * 64)
                            mm(po[:, q4 * 128 + c2 * 64: q4 * 128 + c2 * 64 + 64], sring[cur], sb["qd"][:, tok], start=False, stop=(ci == 1))
                            pd = pdt(c)
                            last = (c == 15) if d_ == 0 else (c == 0)
                            segend = (c % 4 == 3) if d_ == 0 else (c % 4 == 0)
                            if not segend:
                                stt(sb["S"], sb["S"], sb["eBl"][:, c:c + 1], pd, ALU.mult, ALU.add)
                                cp("act", sring[1 - cur], sb["S"])
                                cur = 1 - cur
                            else:
                                stt(sb["S"], sb["S"], sb["eBl"][:, c:c + 1], pd, ALU.mult, ALU.add)
                                outs_dma.append(dma("sp", nshg[c // 4, l, d_, h_], sb["S"]))
                                if not last:
                                    ts("dve", sb["S"], sb["S"], carry, None, ALU.mult)
                                    cp("act", sring[1 - cur], sb["S"])
                                    cur = 1 - cur
                        if n_ % 4 == 3:
                            ov = oacc[:, hh, hfidx * 512:(hfidx + 1) * 512]
                            if d_ == 0:
                                cp("act", ov, po)
                            else:
                                tt("dve", ov, ov, po, ALU.add)
                if hp == 0:
                    dbgdump("oacc", oacc[:, 0, :])
                sqb = [kk[:, 0:512].bitcast(BF16), kk[:, 512:1024].bitcast(BF16)]
                for hh in range(2):
                    rstd = rms_rstd([oacc[:, hh, :]], 128.0, sqb, Bc)
                    stt(t1, oacc[:, hh, :], vec_t[:, 304 + l:305 + l], rstd, ALU.mult, ALU.mult)
                    tt("dve", o_a[:, hp * 2 + hh, :], t1, go_b[:, hh, :], ALU.mult)
            dbgdump("o_a", o_a[:, 0, :])
            stage("hgrn%d" % l)
            AR.reset(AR_A0)
            AR_tmp_sig = [AR.alloc([T], BF16) for _ in range(2)]
            merge_branch(o_a, "A", False)

            AR.reset()
            TM.reset()
            o_b = AR.alloc([8, T], BF16)
            u_b = AR.alloc([8, T], BF16)
            bsb = AR.alloc([T])
            dma("sp", bsb, cmbs[l].partition_broadcast(P))
            vn_tm = TM.alloc([8, T], BF16)
            for cg in range(2):
                wt = W.next("L%d_cu%d" % (l, cg)).rearrange("p (k c) -> p k c", k=16)
                for j in range(4):
                    pp = PSP()
                    for hf in range(2):
                        for kc in range(KC):
                            mm(pp[:, hf * 512:(hf + 1) * 512], wt[:, kc, j * 128:(j + 1) * 128], h_bf[:, kc, hf * 512:(hf + 1) * 512],
                               start=(kc == 0), stop=(kc == KC - 1))
                    act(u_b[:, cg * 4 + j, :], pp, AF.Gelu)
            gv = [AR.alloc([512]) for _ in range(2)]
            sqv = AR.alloc([512])
            ssv = AR.alloc([8, 8])
            for cg in range(2):
                wt = W.next("L%d_cv%d" % (l, cg)).rearrange("p (k c) -> p k c", k=16)
                for tt_ in range(8):
                    pp = PSB()
                    for kc in range(KC):
                        mm(pp, h_bf[:, kc, tt_ * 128:(tt_ + 1) * 128], wt[:, kc, :], start=(kc == 0), stop=(kc == KC - 1))
                    g_ = gv[tt_ % 2]
                    act(g_, pp, AF.Gelu)
                    act(sqv, g_, AF.Square)
                    ssl = ssv[:, tt_, cg * 4:(cg + 1) * 4]
                    S.add("dve", (lambda ssl=ssl: lambda e: e.reduce_sum(out=ssl, in_=sqv.rearrange("p (g c) -> p g c", g=4), axis=mybir.AxisListType.X))(),
                          reads=[sqv], writes=[ssl])
                    act(ssl, ssl, AF.Ln, scale=1.0 / 128.0, bias=EPS)
                    act(ssl, ssl, AF.Exp, scale=-0.5)
                    tt("dve", vn_tm[:, tt_, cg * 512:(cg + 1) * 512].rearrange("p (g c) -> p g c", g=4), g_.rearrange("p (g c) -> p g c", g=4),
                       bc(ssl, [P, 4, 128], 2), ALU.mult)
            for g in range(8):
                for half in range(2):
                    pt = PSB()
                    for q4 in range(4):
                        tt_ = half * 4 + q4
                        mm(pt[:, q4 * 128:(q4 + 1) * 128], vn_tm[:, tt_, g * 128:(g + 1) * 128], wsT_b[:, g, :])
                    s_ = gv[half]
                    stt(s_.rearrange("p (a b) -> p a b", a=4), pt.rearrange("p (a b) -> p a b", a=4), vec_t[:, 308 + 8 * l + g: 309 + 8 * l + g],
                        bc(bsb[:, g * 128:(g + 1) * 128], [P, 4, 128], 1), ALU.mult, ALU.add)
                    tt("dve", o_b[:, g, half * 512:(half + 1) * 512], s_, u_b[:, g, half * 512:(half + 1) * 512], ALU.mult)
            dbgdump("o_b", o_b[:, 0, :])
            stage("gmlp%d" % l)
            AR_tmp_sig = [AR.alloc([T], BF16) for _ in range(2)]
            merge_branch(o_b, "B", False)
            dbgdump("merged", merged[:, 0, :])
            stage("merged%d" % l)

            for j4 in range(4):
                wt = W.next("L%d_wo%d" % (l, j4)).rearrange("p (k c) -> p k c", k=16)
                for jj in range(4):
                    j = j4 * 4 + jj
                    if l == 0:
                        dma("sp", xs[:, j, :], xTv[:, j, :])
                    else:
                        dma("sp", xs[:, j, :], xsv[:, j, :], rk=[("xscr", j)])
                    pp = PSP()
                    for hf in range(2):
                        for kc in range(KC):
                            mm(pp[:, hf * 512:(hf + 1) * 512], wt[:, kc, jj * 128:(jj + 1) * 128], merged[:, kc, hf * 512:(hf + 1) * 512],
                               start=(kc == 0), stop=(kc == KC - 1))
                    stt(xs[:, j, :], pp, gate1[:, j:j + 1], xs[:, j, :], ALU.mult, ALU.add)
            dbgdump("x_mid%d" % l, xs[:, 0, :])
            stage("xmid%d" % l)

            def out_h2(c, t_, _gs=gs2[l], _sh=sh2):
                act(h_bf[:, c, :], t_, AF.Identity, scale=_gs[:, c:c + 1], bias=_sh[:, c:c + 1])
            norm_mod(gs2[l], sh2, out_h2)
            a_b = V(OFF_MG, [16, T], BF16)
            sqf = [V(OFF_TM + i * 2048, [T], BF16) for i in range(2)]
            for g in range(4):
                for c4 in range(4):
                    wt = W.next("L%d_f1_%d_%d" % (l, g, c4)).rearrange("p (k c) -> p k c", k=16)
                    for jj in range(4):
                        pp = PSP()
                        for hf in range(2):
                            for kc in range(KC):
                                mm(pp[:, hf * 512:(hf + 1) * 512], wt[:, kc, jj * 128:(jj + 1) * 128], h_bf[:, kc, hf * 512:(hf + 1) * 512],
                                   start=(kc == 0), stop=(kc == KC - 1))
                        act(sqf[jj % 2], pp, AF.Square)
                        stt(a_b[:, c4 * 4 + jj, :], pp, 0.0, sqf[jj % 2], ALU.is_gt, ALU.mult)
                for j4 in range(4):
                    wt = W.next("L%d_f2_%d_%d" % (l, g, j4)).rearrange("p (k c) -> p k c", k=16)
                    for jj in range(4):
                        j = j4 * 4 + jj
                        pp = PSP()
                        for hf in range(2):
                            for kc in range(KC):
                                mm(pp[:, hf * 512:(hf + 1) * 512], wt[:, kc, jj * 128:(jj + 1) * 128], a_b[:, kc, hf * 512:(hf + 1) * 512],
                                   start=(kc == 0), stop=(kc == KC - 1))
                        stt(xs[:, j, :], pp, gate2[:, j:j + 1], xs[:, j, :], ALU.mult, ALU.add)
            dbgdump("x_out%d" % l, xs[:, 0, :])
            stage("xout%d" % l)

        yb = [V(OFF_H + i * 4096, [T]) for i in range(2)]

        def out_y(c, t_):
            act(yb[c % 2], t_, AF.Identity, scale=vec_t[:, 64 + c:65 + c])
            outs_dma.append(dma("sp", yTv[:, c, :], yb[c % 2]))
        norm_mod(None, None, out_y)
    except _Stop:
        pass

    S.emit(final_waits=outs_dma)
    es.close()
    return nc


def make_consts():
    c = np.zeros((P, 2048), np.float32)
    idx = np.arange(128)
    same64 = (idx[:, None] // 64) == (idx[None, :] // 64)
    c[:, 0:128] = same64
    c[:, 128:256] = same64 & (idx[:, None] <= idx[None, :])
    c[:, 256:384] = same64 & (idx[:, None] >= idx[None, :])
    c[0:64, 384:512] = 1.0
    c[64:128, 512:640] = 1.0
    o = 640
    c[:, o:o + 128] = np.eye(128)
    c[:, o + 128:o + 256] = same64 & (idx[:, None] <= idx[None, :])
    c[:, o + 256:o + 384] = same64 & (idx[:, None] >= idx[None, :])
    c[:, o + 384:o + 512] = 1.0
    i64 = np.arange(64)
    p64 = idx % 64
    m = o + 512
    c[:, m:m + 64] = (p64[:, None] == i64[None, :])
    c[:, m + 64:m + 128] = (p64[:, None] > i64[None, :])
    c[:, m + 128:m + 192] = (p64[:, None] < i64[None, :])
    c[:, m + 192:m + 256] = (p64[:, None] >= i64[None, :])
    c[:, m + 256:m + 320] = (p64[:, None] <= i64[None, :])
    b16 = (p64[:, None] // 16) == (i64[None, :] // 16)
    b32 = (p64[:, None] // 32) == (i64[None, :] // 32)
    c[:, m + 320:m + 384] = -1.0 * b16
    c[:, m + 384:m + 448] = b32 & ~b16
    c[:, m + 448:m + 512] = ~b32
    return c


_CACHE = {}


def _get_program(dbg=None):
    key = None if not dbg else tuple(sorted(dbg.items()))
    if key not in _CACHE:
        _CACHE[key] = build_program(dbg)
    return _CACHE[key]


def kernel(x_prompt, x_sample, c, state_hgrn, state_gdn, c_ctx, norm1_g, norm2_g, w_mod, b_mod, w_in,
           hg_lb, hg_onorm_g, cm_vnorm_g, cm_ws, cm_bs, gdn_conv, gdn_A_log, gdn_dt_bias, gdn_onorm_g,
           w_br_hg, w_br_cm, w_br_gdn, w_out, w_ff1, w_ff2, final_g, _dbg=None, _stop=None, _cores=None):
    f32 = np.float32
    A = lambda t: np.asarray(t, f32)
    x_prompt, x_sample, c, state_hgrn, state_gdn, c_ctx = map(A, (x_prompt, x_sample, c, state_hgrn, state_gdn, c_ctx))
    w_mod, b_mod, w_in = A(w_mod), A(b_mod), A(w_in)
    wsl = [build_wstream(w_in[l], A(w_br_hg)[l], A(w_br_cm)[l], A(w_br_gdn)[l], A(w_out)[l], A(w_ff1)[l], A(w_ff2)[l])
           for l in range(DEPTH)]
    parts = []
    for l in range(DEPTH):
        for t_ in range(24):
            for kc in range(16):
                parts.append(w_mod[l, kc * 128:(kc + 1) * 128, t_ * 512:(t_ + 1) * 512])
    wmod_h = np.ascontiguousarray(np.concatenate(parts, axis=1))
    wgab_h = np.ascontiguousarray(np.stack([
        np.concatenate([w_in[l, kc * 128:(kc + 1) * 128, C_GA:C_GA + 32] for kc in range(16)], axis=1) for l in range(DEPTH)]))
    vecs_h = np.zeros((P, 512), f32)
    for l in range(DEPTH):
        vecs_h[:, 16 * l:16 * l + 16] = fm(A(norm1_g)[l], 16)
        vecs_h[:, 32 + 16 * l:48 + 16 * l] = fm(A(norm2_g)[l], 16)
        vecs_h[:, 80 + 96 * l:176 + 96 * l] = fm(b_mod[l], 96)
        vecs_h[:, 272 + 16 * l:288 + 16 * l] = fm(A(hg_lb)[l].reshape(-1), 16)
        vecs_h[:, 304 + l] = A(hg_onorm_g)[l]
        vecs_h[:, 306 + l] = A(gdn_onorm_g)[l]
        vecs_h[:, 308 + 8 * l:316 + 8 * l] = fm(A(cm_vnorm_g)[l], 8)
    vecs_h[:, 64:80] = fm(A(final_g), 16)
    gc_ = A(gdn_conv)
    convw_h = np.ascontiguousarray(np.stack([
        gc_[l].reshape(9, 24, 128).transpose(2, 1, 0).reshape(P, 24 * 9) for l in range(DEPTH)]))
    cmwsT_h = np.ascontiguousarray(np.stack([A(cm_ws)[l].transpose(2, 0, 1).reshape(P, 8 * 128) for l in range(DEPTH)]))
    cmbs_h = np.ascontiguousarray(A(cm_bs).reshape(DEPTH, 1, 1024))
    rowc_h = np.zeros((DEPTH, 1, 64), f32)
    for l in range(DEPTH):
        rowc_h[l, 0, 0:16] = A(gdn_A_log)[l].reshape(-1)
        rowc_h[l, 0, 16:32] = A(gdn_dt_bias)[l].reshape(-1)
    consts_h = make_consts()
    tpos = np.arange(T)
    in_maps = []
    for core in range(NCORES):
        sample = core < 4
        if sample:
            b = core
            xt = x_sample[b]
            cv = c[b]
            shg = state_hgrn[b]
            sgd = state_gdn[b]
            period = 64
        else:
            b0 = (core - 4) * 4
            xt = x_prompt[b0:b0 + 4].reshape(T, D)
            cv = c_ctx
            shg = np.zeros_like(state_hgrn[0])
            sgd = np.zeros_like(state_gdn[0])
            period = 256
        flags_h = np.zeros((P, 16), f32)
        flags_h[:, 0] = 1.0 if sample else 0.0
        for dr in range(3):
            for dc in range(3):
                flags_h[:, 1 + dr * 3 + dc] = 1.0 if (sample or dr == 1) else 0.0
        cm = np.zeros((2, T), f32)
        cm[0] = (tpos % period) != (period - 1)
        cm[1] = (tpos % period) != 0
        m = {
            "xT": np.ascontiguousarray(xt.T),
            "cond": fm(cv, 16),
            "s0hg": np.ascontiguousarray(shg.reshape(DEPTH, 16, 128, 128).transpose(0, 2, 1, 3)),
            "s0gd": np.ascontiguousarray(sgd.reshape(DEPTH, 16, 128, 128).transpose(0, 2, 1, 3)),
            "flags": flags_h, "cmask": cm, "wmod": wmod_h, "wgab": wgab_h, "vecs": vecs_h, "convw": convw_h,
            "cmwsT": cmwsT_h, "cmbs": cmbs_h, "rowc": rowc_h, "consts": consts_h,
        }
        for l in range(DEPTH):
            m["ws%d" % l] = wsl[l]
        in_maps.append(m)
    if _cores is not None:
        nc = build_program(_dbg, _stop)
        res = run_bass_kernel_spmd(nc, [in_maps[k] for k in _cores], core_ids=list(range(len(_cores))))
        return [{k: np.asarray(v) for k, v in ri.items()} for ri in res.results]
    nc = _get_program(_dbg)
    res = run_bass_kernel_spmd(nc, in_maps, core_ids=list(range(NCORES)))
    r = res.results
    y_sample = np.stack([r[i]["yT"].T for i in range(4)]).astype(f32)
    y_prompt = np.concatenate([r[i]["yT"].T.reshape(4, 256, D) for i in range(4, 8)]).astype(f32)
    nhg = np.concatenate([r[i]["nshg"] for i in range(4, 8)]).astype(f32)
    ngd = np.concatenate([r[i]["nsgd"] for i in range(4, 8)]).astype(f32)
    if _dbg:
        kernel.last_dbg = [{k: v for k, v in ri.items() if k.startswith("dbg_")} for ri in r]
    return (y_prompt, y_sample, nhg, ngd)
```

```python
import contextlib
import numpy as np
import concourse.bass as bass
import concourse.mybir as mybir
from concourse.ap import AP
from concourse.bass_utils import run_bass_kernel_spmd

F32 = mybir.dt.float32
BF16 = mybir.dt.bfloat16
AF = mybir.ActivationFunctionType
ALU = mybir.AluOpType

P = 128
T = 1024
D = 2048
KC = 16
DEPTH = 2
EPS = 1e-6
NCORES = 8
NDMASEM = 8
C_HQ, C_HI, C_HGO, C_HFF, C_HFB, C_CU, C_CV = 0, 1024, 2048, 3072, 4096, 5120, 6144
C_GQ, C_GK, C_GV, C_GGO, C_GA, C_GB = 7168, 8192, 9216, 10240, 11264, 11280
C_GATE_A, C_GATE_B, C_GATE_C = 11296, 13344, 15392


class Op:
    __slots__ = ("issuer", "stream", "idx", "fn", "waits", "signal", "semval", "is_dma", "src")


def apkeys(x):
    if not isinstance(x, AP):
        return [x]
    tn = type(x.tensor).__name__
    if tn.startswith("DRam"):
        return []
    esz = 2 if x.dtype == BF16 else 4
    rowlen = x.tensor.shape[1]
    off = int(x.offset) % rowlen
    lo = hi = off
    for st, cnt in x.ap[1:]:
        if st >= 0:
            hi += st * (cnt - 1)
        else:
            lo += st * (cnt - 1)
    b0 = lo * esz
    b1 = (hi + 1) * esz
    gran = 2048 if tn.startswith("PSum") else 512
    nm = "ps" if tn.startswith("PSum") else "sb"
    return [(nm, g) for g in range(b0 // gran, (b1 - 1) // gran + 1)]


class Sched:
    def __init__(self, nc):
        self.nc = nc
        self.per_issuer = {e: [] for e in ("pe", "act", "dve", "pool", "sp")}
        self.stream_ops = {}
        self.last_writer = {}
        self.readers = {}
        self.clock = {e: {} for e in self.per_issuer}
        self.opclock = {}
        self.dma_count = {e: 0 for e in self.per_issuer}
        self.nops = 0
        self.debug_src = False
        self.srcmap = {}

    def add(self, issuer, fn, reads=(), writes=(), dma=False):
        op = Op()
        op.issuer = issuer
        op.fn = fn
        op.is_dma = dma
        op.signal = False
        op.semval = None
        op.src = None
        if self.debug_src:
            import sys as _sys
            f = _sys._getframe(1)
            names = []
            while f is not None and len(names) < 4:
                if f.f_code.co_name not in ("dma", "mm", "tr", "act", "tt", "ts", "stt", "cp", "scan", "mmblk"):
                    names.append(f.f_lineno)
                f = f.f_back
            op.src = names
        if dma:
            n = self.dma_count[issuer]
            self.dma_count[issuer] += 1
            op.stream = "d_%s_%d" % (issuer, n % NDMASEM)
        else:
            op.stream = issuer
        so = self.stream_ops.setdefault(op.stream, [])
        op.idx = len(so) + 1
        rkeys = []
        for r in reads:
            rkeys.extend(apkeys(r))
        wkeys = []
        for w in writes:
            wkeys.extend(apkeys(w))
        deps = []
        raw = set()
        for k in rkeys:
            w = self.last_writer.get(k)
            if w is not None:
                deps.append(w)
                raw.add(id(w))
            if k[0] == "ps":
                rd = self.readers.get(k)
                if rd:
                    for r_ in rd.values():
                        if r_.stream != op.stream:
                            deps.append(r_)
        for k in wkeys:
            w = self.last_writer.get(k)
            if w is not None:
                deps.append(w)
            rd = self.readers.get(k)
            if rd:
                deps.extend(rd.values())
        if dma and so:
            deps.append(so[-1])
        clk = self.clock[issuer]
        best = {}
        for d in deps:
            if d.stream == op.stream and not dma:
                if issuer == "pe":
                    continue
            if clk.get(d.stream, 0) >= d.idx:
                continue
            if d.stream not in best or best[d.stream].idx < d.idx:
                best[d.stream] = d
        op.waits = list(best.values())
        for d in op.waits:
            d.signal = True
            if clk.get(d.stream, 0) < d.idx:
                clk[d.stream] = d.idx
            for s, i in self.opclock[id(d)].items():
                if clk.get(s, 0) < i:
                    clk[s] = i
        so.append(op)
        myclk = dict(clk)
        myclk[op.stream] = op.idx
        self.opclock[id(op)] = myclk
        for k in rkeys:
            self.readers.setdefault(k, {})[op.stream] = op
        for k in wkeys:
            self.last_writer[k] = op
            self.readers[k] = {}
        self.per_issuer[issuer].append(op)
        self.nops += 1
        return op

    def emit(self, final_waits=()):
        nc = self.nc
        for s, so in self.stream_ops.items():
            c = 0
            for o in so:
                if s.startswith("d_"):
                    o.signal = True
                if o.signal:
                    c += 1
                    o.semval = c
        sems = {}
        with contextlib.ExitStack() as es:
            for s in self.stream_ops:
                sems[s] = es.enter_context(nc.semaphore("s_" + s))
            block = es.enter_context(nc.Block())
            engs = {"pe": block.tensor, "act": block.scalar, "dve": block.vector,
                    "pool": block.gpsimd, "sp": block.sync}

            def make(issuer):
                def body(eng):
                    for o in self.per_issuer[issuer]:
                        for d in o.waits:
                            eng.wait_ge(sems[d.stream], d.semval * (16 if d.stream.startswith("d_") else 1))
                        ins = o.fn(eng)
                        if self.debug_src:
                            try:
                                self.srcmap[ins.ins.name] = o.src
                            except Exception:
                                pass
                        if o.signal:
                            ins.then_inc(sems[o.stream], 16 if o.is_dma else 1)
                    if issuer == "sp":
                        for s_, so_ in self.stream_ops.items():
                            if s_.startswith("d_") and so_:
                                eng.wait_ge(sems[s_], so_[-1].semval * 16)
                return body
            for issuer in ("sp", "pe", "act", "dve", "pool"):
                engs[issuer](make(issuer))


def tile_order():
    o = []
    for hg in range(2):
        o += [("gq%d" % hg, 8192), ("gk%d" % hg, 8192), ("gv%d" % hg, 8192), ("ggo%d" % hg, 8192)]
    o += [("mC%d" % j, 6144) for j in range(8)]
    o += [("hi%d" % c, 8192) for c in range(2)]
    for hp in range(4):
        o += [("hA%d" % hp, 8192), ("hB%d" % hp, 8192)]
    o += [("mA%d" % j, 6144) for j in range(8)]
    o += [("cu%d" % c, 8192) for c in range(2)]
    o += [("cv%d" % c, 8192) for c in range(2)]
    o += [("mB%d" % j, 6144) for j in range(8)]
    o += [("wo%d" % j, 8192) for j in range(4)]
    for g in range(4):
        o += [("f1_%d_%d" % (g, c), 8192) for c in range(4)]
        o += [("f2_%d_%d" % (g, j), 8192) for j in range(4)]
    return o


def build_wstream(w_in, wbr_hg, wbr_cm, wbr_gdn, w_out, w_ff1, w_ff2):
    def full(M, c0, n=512):
        return [(M, kc * 128, c0, n) for kc in range(16)]
    spec = {}
    for hg in range(2):
        spec["gq%d" % hg] = full(w_in, C_GQ + hg * 512)
        spec["gk%d" % hg] = full(w_in, C_GK + hg * 512)
        spec["gv%d" % hg] = full(w_in, C_GV + hg * 512)
        spec["ggo%d" % hg] = full(w_in, C_GGO + hg * 512)
    for nm, gc0, wbr in (("mC", C_GATE_C, wbr_gdn), ("mA", C_GATE_A, wbr_hg), ("mB", C_GATE_B, wbr_cm)):
        for j in range(8):
            spec["%s%d" % (nm, j)] = full(w_in, gc0 + j * 256, 256) + [(wbr, kc * 128, j * 256, 256) for kc in range(8)]
    for c in range(2):
        spec["hi%d" % c] = full(w_in, C_HI + c * 512)
        spec["cu%d" % c] = full(w_in, C_CU + c * 512)
        spec["cv%d" % c] = full(w_in, C_CV + c * 512)
    for hp in range(4):
        sA = []
        sB = []
        for kc in range(16):
            sA += [(w_in, kc * 128, C_HQ + hp * 256, 256), (w_in, kc * 128, C_HGO + hp * 256, 256)]
            sB += [(w_in, kc * 128, C_HFF + hp * 256, 256), (w_in, kc * 128, C_HFB + hp * 256, 256)]
        spec["hA%d" % hp] = sA
        spec["hB%d" % hp] = sB
    for j in range(4):
        spec["wo%d" % j] = full(w_out, j * 512)
    for g in range(4):
        for c in range(4):
            spec["f1_%d_%d" % (g, c)] = full(w_ff1, g * 2048 + c * 512)
        for j in range(4):
            spec["f2_%d_%d" % (g, j)] = [(w_ff2, (g * 16 + kc) * 128, j * 512, 512) for kc in range(16)]
    parts = []
    for nm, n in tile_order():
        tot = 0
        for (M, r0, c0, w) in spec[nm]:
            parts.append(M[r0:r0 + 128, c0:c0 + w])
            tot += w
        assert tot == n, (nm, tot, n)
    return np.ascontiguousarray(np.concatenate(parts, axis=1))


def fm(vec, nchunk):
    return np.ascontiguousarray(np.asarray(vec, np.float32).reshape(nchunk, 128).T)


NF = 52992
OFF_CONST = 0
SZ_CONST = 30 * 1024
OFF_H = OFF_CONST + SZ_CONST
OFF_MG = OFF_H + 32 * 1024
OFF_TM = OFF_MG + 32 * 1024
OFF_W = OFF_TM + 16 * 1024
OFF_AR = OFF_W + 32 * 1024
assert OFF_AR + 64 * 1024 <= NF * 4


class _Stop(Exception):
    pass


def build_program(dbg=None, stop=None):
    nc = bass.Bass("TRN2", target_bir_lowering=False)
    WTOT = sum(n for _, n in tile_order())

    def din(name, shape):
        return nc.dram_tensor(name, list(shape), F32, kind="ExternalInput").ap()

    def dout(name, shape):
        return nc.dram_tensor(name, list(shape), F32, kind="ExternalOutput").ap()

    xT = din("xT", [D, T])
    cond = din("cond", [P, 16])
    s0hg = din("s0hg", [DEPTH, P, 16, 128])
    s0gd = din("s0gd", [DEPTH, P, 16, 128])
    flags = din("flags", [P, 16])
    cmask = din("cmask", [2, T])
    ws = [din("ws%d" % l, [P, WTOT]) for l in range(DEPTH)]
    wmod = din("wmod", [P, DEPTH * 24 * 8192])
    wgab = din("wgab", [DEPTH, P, 16 * 32])
    vecs = din("vecs", [P, 512])
    convw = din("convw", [DEPTH, P, 24 * 9])
    cmwsT = din("cmwsT", [DEPTH, P, 8 * 128])
    cmbs = din("cmbs", [DEPTH, 1, 1024])
    rowc = din("rowc", [DEPTH, 1, 64])
    consts = din("consts", [P, 2048])
    yT = dout("yT", [D, T])
    nshg = dout("nshg", [4, DEPTH, 2, 8, P, 128])
    nsgd = dout("nsgd", [4, DEPTH, 2, 8, P, 128])
    xscr = nc.dram_tensor("xscr", [D, T], F32, kind="Internal").ap()
    dbg_out = {}
    if dbg:
        for nm, shp in dbg.items():
            dbg_out[nm] = dout("dbg_" + nm, shp)

    es = contextlib.ExitStack()
    arena = es.enter_context(nc.sbuf_tensor("arena", [P, NF], F32))
    psum = es.enter_context(nc.psum_tensor("psum", [P, 4096], F32))
    S = Sched(nc)
    import os as _os
    S.debug_src = bool(_os.environ.get("KDEBUG_SRC"))
    nc._sched = S
    outs_dma = []

    def V(off, dims, dt=F32):
        n = 1
        for d_ in dims:
            n *= d_
        assert off % 4 == 0
        if dt == F32:
            ap = arena[:, off // 4: off // 4 + n]
        else:
            assert n % 2 == 0
            ap = arena[:, off // 4: off // 4 + n // 2].bitcast(BF16)
        if len(dims) == 2:
            ap = ap.rearrange("p (a b) -> p a b", a=dims[0])
        elif len(dims) == 3:
            ap = ap.rearrange("p (a b c) -> p a b c", a=dims[0], b=dims[1])
        return ap

    class Bump:
        def __init__(self, base, size):
            self.base, self.size, self.cur = base, size, base

        def alloc(self, dims, dt=F32):
            n = 1
            for d_ in dims:
                n *= d_
            nb = n * (4 if dt == F32 else 2)
            nb = (nb + 511) // 512 * 512
            off = self.cur
            self.cur += nb
            assert self.cur <= self.base + self.size, ("region overflow", self.base, self.cur - self.base, self.size)
            return V(off, dims, dt)

        def reset(self, to=None):
            self.cur = self.base if to is None else to

        def mark(self):
            return self.cur

    CR = Bump(OFF_CONST, SZ_CONST)
    MG = Bump(OFF_MG, 32 * 1024)
    TM = Bump(OFF_TM, 16 * 1024)
    AR = Bump(OFF_AR, 64 * 1024)
    h_bf = V(OFF_H, [16, T], BF16)
    wbuf = [V(OFF_W + i * 16384, [8192], BF16) for i in range(2)]

    def isap(x):
        return isinstance(x, AP)

    def dma(q, out, in_, rk=(), wk=()):
        op = S.add(q, lambda e: e.dma_start(out=out, in_=in_), reads=[in_] + list(rk), writes=[out] + list(wk), dma=True)
        return op

    def mm(out, lhsT, rhs, start=True, stop=True):
        S.add("pe", lambda e: e.matmul(out, lhsT=lhsT, rhs=rhs, start=start, stop=stop), reads=[lhsT, rhs], writes=[out])

    def tr(out, in_, ident):
        S.add("pe", lambda e: e.transpose(out, in_, ident), reads=[in_, ident], writes=[out])

    def act(out, in_, func, scale=1.0, bias=0.0):
        rd = [in_] + [x for x in (scale, bias) if isap(x)]
        S.add("act", lambda e: e.activation(out=out, in_=in_, func=func, scale=scale, bias=bias), reads=rd, writes=[out])

    def tt(eng, out, in0, in1, op):
        S.add(eng, lambda e: e.tensor_tensor(out=out, in0=in0, in1=in1, op=op), reads=[in0, in1], writes=[out])

    def ts(eng, out, in0, s1, s2, op0, op1=None):
        rd = [in0] + [x for x in (s1, s2) if isap(x)]
        if op1 is None:
            S.add(eng, lambda e: e.tensor_scalar(out=out, in0=in0, scalar1=s1, scalar2=None, op0=op0), reads=rd, writes=[out])
        else:
            S.add(eng, lambda e: e.tensor_scalar(out=out, in0=in0, scalar1=s1, scalar2=s2, op0=op0, op1=op1), reads=rd, writes=[out])

    def stt(out, in0, scalar, in1, op0, op1):
        rd = [in0, in1] + ([scalar] if isap(scalar) else [])
        S.add("dve", lambda e: e.scalar_tensor_tensor(out=out, in0=in0, scalar=scalar, in1=in1, op0=op0, op1=op1), reads=rd, writes=[out])

    def cp(eng, out, in_):
        if eng == "act":
            act(out, in_, AF.Copy)
        else:
            S.add(eng, lambda e: e.tensor_copy(out=out, in_=in_), reads=[in_], writes=[out])

    def scan(out, d0, d1):
        S.add("dve", lambda e: e.tensor_tensor_scan(out=out, data0=d0, data1=d1, initial=0.0, op0=ALU.mult, op1=ALU.add),
              reads=[d0, d1], writes=[out])

    def rev(ap):
        (ps_, pc_), (st, cnt) = ap.ap
        return AP(ap.tensor, ap.offset + (cnt - 1) * st, [[ps_, pc_], [-st, cnt]])

    def bc(ap, dims, axis):
        return ap.unsqueeze(axis).to_broadcast(dims)

    pspools = {"s": [4, 5, 6, 7], "lo": [0, 1], "g0": [0, 1, 2, 3], "g1": [4, 5, 6, 7], "all": [0, 1, 2, 3, 4, 5, 6, 7]}
    psctr = {k: 0 for k in pspools}
    psctr["p"] = 0

    def PSB(pool="s", bf=False):
        lst = pspools[pool]
        b = lst[psctr[pool] % len(lst)]
        psctr[pool] += 1
        ap = psum[:, b * 512:(b + 1) * 512]
        return ap.bitcast(BF16) if bf else ap

    def PSP():
        p_ = psctr["p"] % 2
        psctr["p"] += 1
        return psum[:, p_ * 1024:(p_ + 1) * 1024]

    def dbgdump(name, ap):
        if dbg and name in dbg_out:
            outs_dma.append(dma("pool" if ap.dtype == BF16 else "sp", dbg_out[name], ap))

    class WStream:
        def __init__(self):
            self.seq = []
            for l in range(DEPTH):
                off = 0
                for nm, n in tile_order():
                    self.seq.append((ws[l], off, n, "L%d_%s" % (l, nm)))
                    off += n
            self.modseq = [(wmod, i * 8192, 8192, "mod%d" % i) for i in range(DEPTH * 24)]
            self.all = list(self.modseq[0:24])
            for ent in self.seq:
                self.all.append(ent)
                if ent[3] == "L0_gv0":
                    self.all += self.modseq[24:36]
                if ent[3] == "L0_gv1":
                    self.all += self.modseq[36:48]
            self.issued = 0
            self.taken = 0

        def _issue(self):
            if self.issued < len(self.all):
                src, off, n, nm = self.all[self.issued]
                buf = wbuf[self.issued % 2]
                dma("pool", buf[:, 0:n], src[:, off:off + n])
                self.issued += 1

        def next(self, name):
            while self.issued < min(self.taken + 2, len(self.all)):
                self._issue()
            src, off, n, nm = self.all[self.taken]
            assert nm == name, (nm, name)
            buf = wbuf[self.taken % 2]
            self.taken += 1
            return buf

        def prefetch(self):
            while self.issued < min(self.taken + 2, len(self.all)):
                self._issue()

    W = WStream()

    def stage(name):
        if stop is not None and name == stop:
            raise _Stop()

    try:
        c_f32 = CR.alloc([640])
        dma("sp", c_f32, consts[:, 0:640])
        blk64_f = c_f32[:, 0:128]
        triF_f = c_f32[:, 128:256]
        triB_f = c_f32[:, 256:384]
        sel_f = [c_f32[:, 384:512], c_f32[:, 512:640]]
        cbf = CR.alloc([1024], BF16)
        dma("pool", cbf, consts[:, 640:1664])
        ident_b = cbf[:, 0:128]
        hmaskF_b = cbf[:, 128:256]
        hmaskB_b = cbf[:, 256:384]
        ones_b = cbf[:, 384:512]
        I64_b = cbf[:, 512:576]
        mSL_b, mSU_b, mLi_b, mUi_b = (cbf[:, 576 + 64 * i: 640 + 64 * i] for i in range(4))
        nb16_b = cbf[:, 832:896]
        E1m_b = cbf[:, 896:960]
        E2m_b = cbf[:, 960:1024]
        vec_t = CR.alloc([512])
        dma("sp", vec_t, vecs)
        flags_t = CR.alloc([16])
        dma("sp", flags_t, flags)
        carry = flags_t[:, 0:1]
        cond_t = CR.alloc([16])
        dma("sp", cond_t, cond)
        rbuf = CR.alloc([T + 64], BF16)
        S.add("dve", lambda e: e.memset(rbuf, 1.0), writes=[rbuf])
        S.add("dve", lambda e: e.memset(rbuf.rearrange("p (c j) -> p c j", j=64)[:, :, 0:1], 0.0), writes=[rbuf])
        reset64 = rbuf[:, 0:T]
        reset63 = rbuf[:, 1:T + 1]
        mLR = CR.alloc([2, T], BF16)
        dma("pool", mLR[:, 0, :], cmask[0:1, :].partition_broadcast(P))
        dma("pool", mLR[:, 1, :], cmask[1:2, :].partition_broadcast(P))
        modv = [CR.alloc([96]) for _ in range(DEPTH)]
        gs1 = [CR.alloc([16]) for _ in range(DEPTH)]
        gs2 = [CR.alloc([16]) for _ in range(DEPTH)]
        lb1 = CR.alloc([16])
        oml1 = CR.alloc([16])
        noml1 = CR.alloc([16])
        zero16 = CR.alloc([16])
        one16 = CR.alloc([16])
        mone16 = CR.alloc([16])
        S.add("dve", lambda e: e.memset(zero16, 0.0), writes=[zero16])
        S.add("dve", lambda e: e.memset(one16, 1.0), writes=[one16])
        S.add("dve", lambda e: e.memset(mone16, -1.0), writes=[mone16])
        scond = CR.alloc([16], BF16)
        cw = CR.alloc([24, 9])
        rowbc = CR.alloc([64])
        negA = CR.alloc([16])
        wgab_b = CR.alloc([16, 32], BF16)
        wsT_b = CR.alloc([8, 128], BF16)
        ab_tm = CR.alloc([8, 32])
        g_tm = CR.alloc([8, 16])
        beta_tm = CR.alloc([8, 16])
        gc_tm = CR.alloc([8, 16])
        bg_tm = CR.alloc([8, 16])
        ekd_tm = CR.alloc([8, 16])
        sm_tmp = CR.alloc([8, 16])
        CR_MARK = CR.mark()

        tt("dve", lb1, vec_t[:, 288:304], vec_t[:, 272:288], ALU.subtract)
        act(lb1, lb1, AF.Sigmoid)
        ts("dve", oml1, lb1, -1.0, 1.0, ALU.mult, ALU.add)
        ts("dve", noml1, lb1, 1.0, -1.0, ALU.mult, ALU.add)

        act(scond, cond_t, AF.Silu)
        def mod_tile(lm, t_, pool="all"):
            pm = PSB(pool)
            wt = W.next("mod%d" % (lm * 24 + t_)).rearrange("p (k c) -> p k c", k=16)
            for nn in range(4):
                for kc in range(KC):
                    mm(pm[:, nn:nn + 1], wt[:, kc, nn * 128:(nn + 1) * 128], scond[:, kc:kc + 1], start=(kc == 0), stop=(kc == KC - 1))
            tt("dve", modv[lm][:, t_ * 4:t_ * 4 + 4], pm[:, 0:4], vec_t[:, 80 + 96 * lm + t_ * 4: 84 + 96 * lm + t_ * 4], ALU.add)

        def mod_finish(lm):
            stt(gs1[lm], modv[lm][:, 16:32], 1.0, vec_t[:, 16 * lm:16 * lm + 16], ALU.add, ALU.mult)
            stt(gs2[lm], modv[lm][:, 64:80], 1.0, vec_t[:, 32 + 16 * lm:48 + 16 * lm], ALU.add, ALU.mult)

        for t_ in range(24):
            mod_tile(0, t_)
        mod_finish(0)
        dbgdump("modv0", modv[0])
        stage("mod")

        def rms_rstd(src_chunks, nfeat, sq, rstd):
            pp = PSP()
            n = len(src_chunks)
            for c, xc in enumerate(src_chunks):
                s_ = sq[c % 2]
                act(s_, xc, AF.Square)
                for hf in range(2):
                    mm(pp[:, hf * 512:(hf + 1) * 512], ones_b, s_[:, hf * 512:(hf + 1) * 512], start=(c == 0), stop=(c == n - 1))
            act(rstd, pp, AF.Ln, scale=1.0 / nfeat, bias=EPS)
            act(rstd, rstd, AF.Exp, scale=-0.5)
            return rstd

        xs = V(OFF_AR, [16, T])

        def norm_mod(gs, sh, out_fn):
            MG.reset()
            sq = [MG.alloc([T], BF16) for _ in range(2)]
            rstd = rms_rstd([xs[:, c, :] for c in range(16)], D, sq, MG.alloc([T]))
            tmp = [MG.alloc([T]) for _ in range(2)]
            for c in range(16):
                t_ = tmp[c % 2]
                tt("dve", t_, xs[:, c, :], rstd, ALU.mult)
                out_fn(c, t_)

        xTv = xT.rearrange("(c p) t -> p c t", p=P)
        yTv = yT.rearrange("(c p) t -> p c t", p=P)
        xsv = xscr.rearrange("(c p) t -> p c t", p=P)

        for l in range(DEPTH):
            if l == 0:
                for c in range(16):
                    dma("sp", xs[:, c, :], xTv[:, c, :])
            sh1 = modv[l][:, 0:16]
            gate1 = modv[l][:, 32:48]
            sh2 = modv[l][:, 48:64]
            gate2 = modv[l][:, 80:96]

            def out_h(c, t_, _gs=gs1[l], _sh=sh1):
                act(h_bf[:, c, :], t_, AF.Identity, scale=_gs[:, c:c + 1], bias=_sh[:, c:c + 1])
            norm_mod(gs1[l], sh1, out_h)
            if l > 0:
                for c in range(16):
                    dma("sp", xsv[:, c, :], xs[:, c, :], wk=[("xscr", c)])
            dbgdump("h%d" % l, h_bf[:, 0, :])
            stage("norm1_%d" % l)

            dma("sp", cw, convw[l])
            tt("dve", cw, cw, bc(flags_t[:, 1:10], [P, 24, 9], 1), ALU.mult)
            dma("sp", rowbc, rowc[l].partition_broadcast(P))
            act(negA, rowbc[:, 0:16], AF.Exp)
            ts("dve", negA, negA, -1.0, None, ALU.mult)
            dma("pool", wgab_b, wgab[l].rearrange("p (k c) -> p k c", k=16))
            dma("pool", wsT_b, cmwsT[l].rearrange("p (g q) -> p g q", g=8))
            if l == 0:
                lbv, omlv, nomlv = zero16, one16, mone16
            else:
                lbv, omlv, nomlv = lb1, oml1, noml1
            merged = V(OFF_MG, [16, T], BF16)

            def merge_branch(o_br, tag, first):
                sg = [AR_tmp_sig[0], AR_tmp_sig[1]]
                for j8 in range(8):
                    wt = W.next("L%d_m%s%d" % (l, tag, j8))
                    wg = wt[:, 0:4096].rearrange("p (k c) -> p k c", k=16)
                    wb = wt[:, 4096:6144].rearrange("p (k c) -> p k c", k=8)
                    for jj in range(2):
                        j = j8 * 2 + jj
                        pg = PSP()
                        for hf in range(2):
                            for kc in range(KC):
                                mm(pg[:, hf * 512:(hf + 1) * 512], wg[:, kc, jj * 128:(jj + 1) * 128], h_bf[:, kc, hf * 512:(hf + 1) * 512],
                                   start=(kc == 0), stop=(kc == KC - 1))
                        pb = PSP()
                        for hf in range(2):
                            for kc in range(8):
                                mm(pb[:, hf * 512:(hf + 1) * 512], wb[:, kc, jj * 128:(jj + 1) * 128], o_br[:, kc, hf * 512:(hf + 1) * 512],
                                   start=(kc == 0), stop=(kc == 7))
                        s_ = sg[j % 2]
                        act(s_, pg, AF.Sigmoid)
                        if first:
                            tt("dve", merged[:, j, :], pb, s_, ALU.mult)
                        else:
                            tt("dve", s_, pb, s_, ALU.mult)
                            tt("pool", merged[:, j, :], merged[:, j, :], s_, ALU.add)

            AR.reset()
            TM.reset()
            MG.reset()
            o_c = AR.alloc([8, T], BF16)
            AR_C0 = AR.mark()
            pab = PSB()
            pabv = pab[:, 0:256].rearrange("p (t c) -> p t c", t=8)
            for tt_ in range(8):
                for kc in range(KC):
                    mm(pabv[:, tt_, :], h_bf[:, kc, tt_ * 128:(tt_ + 1) * 128], wgab_b[:, kc, :], start=(kc == 0), stop=(kc == KC - 1))
            cp("dve", ab_tm, pabv)
            tt("dve", g_tm, ab_tm[:, :, 0:16], bc(rowbc[:, 16:32], [P, 8, 16], 1), ALU.add)
            act(g_tm, g_tm, AF.Exp)
            act(g_tm, g_tm, AF.Ln, bias=1.0)
            tt("dve", g_tm, g_tm, bc(negA, [P, 8, 16], 1), ALU.mult)
            act(beta_tm, ab_tm[:, :, 16:32], AF.Sigmoid)
            pgc = PSB()
            pgcv = pgc[:, 0:128].rearrange("p (t c) -> p t c", t=8)
            pgl = PSB()
            pglv = pgl[:, 0:128].rearrange("p (t c) -> p t c", t=8)
            for tt_ in range(8):
                mm(pgcv[:, tt_, 0:8], triF_f, g_tm[:, tt_, 0:8])
                mm(pgcv[:, tt_, 8:16], triB_f, g_tm[:, tt_, 8:16])
                mm(pglv[:, tt_, :], blk64_f, g_tm[:, tt_, :])
            cp("dve", gc_tm, pgcv)
            tt("dve", ekd_tm, pglv, gc_tm, ALU.subtract)
            act(ekd_tm, ekd_tm, AF.Exp)
            act(sm_tmp, gc_tm, AF.Exp)
            tt("dve", bg_tm, sm_tmp, beta_tm, ALU.mult)
            dbgdump("gc_tm", gc_tm.rearrange("p a b -> p (a b)"))
            dbgdump("beta_tm", beta_tm.rearrange("p a b -> p (a b)"))
            stage("gdn_tok%d" % l)

            for hg in range(2):
                AR.reset(AR_C0)
                TM.reset()
                MG.reset()
                kT = AR.alloc([4, T], BF16)
                qT = AR.alloc([4, T], BF16)
                opart = AR.alloc([4, T], BF16)
                Sst = AR.alloc([8, 128])
                Sbf = AR.alloc([8, 128], BF16)
                k_tm = TM.alloc([8, 512], BF16)
                v_tm = TM.alloc([8, 512], BF16)
                AR_C1 = AR.mark()
                csets = []
                for si in range(2):
                    reg = AR if si == 0 else MG
                    cs_ = {}
                    for nm in ("xc", "xL", "xR", "acc"):
                        cs_[nm] = reg.alloc([T])
                    cs_["vT"] = reg.alloc([T], BF16)
                    cs_["sq"] = MG.alloc([T], BF16)
                    cs_["rstd"] = MG.alloc([T])
                    csets.append(cs_)
                cchunks = [(part, h4) for part in range(3) for h4 in range(4)]
                wts_ = {}

                def conv_s1(j):
                    part, h4 = cchunks[j]
                    if part not in wts_:
                        wts_[part] = W.next("L%d_g%s%d" % (l, "qkv"[part], hg)).rearrange("p (k c) -> p k c", k=16)
                    wt = wts_[part]
                    cs_ = csets[j % 2]
                    pp = PSP()
                    for hf in range(2):
                        for kc in range(KC):
                            mm(pp[:, hf * 512:(hf + 1) * 512], wt[:, kc, h4 * 128:(h4 + 1) * 128], h_bf[:, kc, hf * 512:(hf + 1) * 512],
                               start=(kc == 0), stop=(kc == KC - 1))
                    cp("act", cs_["xc"], pp)
                    tt("pool", cs_["xL"], cs_["xc"], mLR[:, 0, :], ALU.mult)
                    tt("pool", cs_["xR"], cs_["xc"], mLR[:, 1, :], ALU.mult)

                def conv_s2(j):
                    part, h4 = cchunks[j]
                    cc = part * 8 + hg * 4 + h4
                    cs_ = csets[j % 2]
                    acc = cs_["acc"]
                    act(acc, cs_["xc"], AF.Identity, scale=cw[:, cc, 4:5])
                    for dr in range(3):
                        for dc in range(3):
                            if dr == 1 and dc == 1:
                                continue
                            off = (dr - 1) * 64 + (dc - 1)
                            src = (cs_["xL"], cs_["xc"], cs_["xR"])[dc]
                            a0 = max(0, -off)
                            a1 = min(T, T - off)
                            stt(acc[:, a0:a1], src[:, a0 + off:a1 + off], cw[:, cc, dr * 3 + dc: dr * 3 + dc + 1], acc[:, a0:a1], ALU.mult, ALU.add)

                def conv_s3(j):
                    part, h4 = cchunks[j]
                    cs_ = csets[j % 2]
                    acc = cs_["acc"]
                    if part == 2:
                        act(cs_["vT"], acc, AF.Silu)
                        srcT = cs_["vT"]
                        dst = v_tm
                    else:
                        act(acc, acc, AF.Silu)
                        rstd = rms_rstd([acc], 1.0, [cs_["sq"]], cs_["rstd"])
                        dstT = (qT, kT)[part]
                        if part == 0:
                            stt(dstT[:, h4, :], acc, 128.0 ** -0.5, rstd, ALU.mult, ALU.mult)
                        else:
                            tt("dve", dstT[:, h4, :], acc, rstd, ALU.mult)
                        srcT = kT[:, h4, :]
                        dst = k_tm
                    if part >= 1:
                        for half in range(2):
                            pt = PSB("s", bf=True)
                            for q4 in range(4):
                                tt_ = half * 4 + q4
                                tr(pt[:, q4 * 128:(q4 + 1) * 128], srcT[:, tt_ * 128:(tt_ + 1) * 128], ident_b)
                            cp("act", dst[:, half * 4:(half + 1) * 4, h4 * 128:(h4 + 1) * 128],
                               pt[:, 0:512].rearrange("p (a b) -> p a b", a=4))

                conv_s1(0)
                for j in range(12):
                    if j + 1 < 12:
                        conv_s1(j + 1)
                    conv_s2(j)
                    if j >= 1:
                        conv_s3(j - 1)
                conv_s3(11)
                if hg == 0:
                    dbgdump("kT", kT[:, 0, :])
                    dbgdump("qT", qT[:, 0, :])
                    dbgdump("v_tm", v_tm.rearrange("p a b -> p (a b)"))
                stage("gdn_conv%d_%d" % (l, hg))
                for d_ in range(2):
                    dma("sp", Sst[:, d_ * 4:(d_ + 1) * 4, :], s0gd[l][:, d_ * 8 + hg * 4: d_ * 8 + hg * 4 + 4, :])
                cp("act", Sbf, Sst)
                if l == 0 and hg == 0:
                    stage("gdn_st")
                AR.reset(AR_C1)
                MG.reset()
                NW = 256

                class TB:
                    pass
                sets = []
                def anyalloc(dims, dt=F32):
                    n = 1
                    for d__ in dims:
                        n *= d__
                    nb = (n * (4 if dt == F32 else 2) + 511) // 512 * 512
                    reg = MG if MG.cur + nb <= MG.base + MG.size else AR
                    return reg.alloc(dims, dt)
                for si in range(2):
                    b = TB()
                    b.pool = "g%d" % si
                    b.M = anyalloc([NW])
                    b.Dm = anyalloc([NW])
                    b.E2 = anyalloc([NW])
                    b.ERd = [[anyalloc([NW]) for _ in range(2)] for _ in range(2)]
                    b.tmp = anyalloc([NW])
                    b.pbanks = (0, 1) if si == 0 else (2, 3)
                    b.sbanks = (4, 5) if si == 0 else (6, 7)
                    for nm in ("A", "AT", "P0", "P0T", "R", "RT", "E1a", "E1Ta", "E2a", "Pa", "PaT", "Pb", "PbT", "Y", "Z", "T2", "W2", "W3"):
                        setattr(b, nm, anyalloc([NW], BF16))
                    b.attnTd = [anyalloc([NW], BF16) for _ in range(2)]
                    b.vbd = [anyalloc([512], BF16) for _ in range(2)]
                    b.kbg = anyalloc([512], BF16)
                    b.kdecd = [anyalloc([512], BF16) for _ in range(2)]
                    b.negwT = anyalloc([512], BF16)
                    b.vnew = anyalloc([512], BF16)
                    sets.append(b)

                def v3(ap, a):
                    return ap.rearrange("p (a b) -> p a b", a=a)

                def mmblk(ps, lhsT_src, rhs_src):
                    for c2 in range(2):
                        pr = slice(c2 * 64, (c2 + 1) * 64)
                        for h4 in range(4):
                            cs = slice(h4 * 64, (h4 + 1) * 64)
                            mm(ps[pr, cs], lhsT_src[pr, cs], rhs_src[pr, cs])

                def gdn_prep(tt_, d_, b, par):
                    def H(k_, half_):
                        bb = b.pbanks[k_]
                        return psum[:, bb * 512 + half_ * 256: bb * 512 + half_ * 256 + 256]
                    ER_ = b.ERd[par]
                    attnT_ = b.attnTd[par]
                    vb_ = b.vbd[par]
                    kdec_ = b.kdecd[par]
                    hs = slice(d_ * 8 + hg * 4, d_ * 8 + hg * 4 + 4)
                    gcs = gc_tm[:, tt_, hs]
                    bts = beta_tm[:, tt_, hs]
                    mA = (mSL_b, mSU_b)[d_]
                    mT = (mUi_b, mLi_b)[d_]
                    pk = H(0, 0)
                    pq = H(0, 1)
                    for c2 in range(2):
                        pr = slice(c2 * 64, (c2 + 1) * 64)
                        tok = slice(tt_ * 128 + c2 * 64, tt_ * 128 + c2 * 64 + 64)
                        for h4 in range(4):
                            cs = slice(h4 * 64, (h4 + 1) * 64)
                            mm(pk[pr, cs], kT[:, h4, tok], kT[:, h4, tok])
                            mm(pq[pr, cs], kT[:, h4, tok], qT[:, h4, tok])
                    first_ = (l == 0 and hg == 0 and tt_ == 0 and d_ == 0)
                    sec_ = (l == 0 and hg == 0 and tt_ == 7 and d_ == 1)
                    if first_:
                        stage("p_a")
                    if sec_:
                        stage("q_a")
                    tt("pool", v3(b.M, 4), bc(gcs, [P, 4, 64], 2), bc(I64_b, [P, 4, 64], 1), ALU.mult)
                    if first_:
                        stage("p_b")
                    if sec_:
                        stage("q_b")
                    pr_ = [H(1, 0), H(1, 1)]
                    for c2 in range(2):
                        prr = slice(c2 * 64, (c2 + 1) * 64)
                        mm(pr_[c2][:, 0:NW], sel_f[c2], b.M)
                        if first_ and c2 == 0:
                            stage("p_c")
                        if sec_:
                            stage("q_c%d" % c2)
                        act(ER_[c2], pr_[c2][:, 0:NW], AF.Exp)
                        if first_ and c2 == 0:
                            stage("p_d")
                        if sec_:
                            stage("q_d%d" % c2)
                        tt("dve", v3(b.Dm[prr, :], 4), bc(gcs[prr, :], [64, 4, 64], 2), v3(pr_[c2][prr, 0:NW], 4), ALU.subtract)
                        if first_:
                            stage("p_e%d" % c2)
                        if sec_:
                            stage("q_e%d" % c2)
                    yield
                    ts("dve", b.E2, b.Dm, -1.0, 0.0, ALU.mult, ALU.min)
                    ts("dve", b.Dm, b.Dm, 0.0, None, ALU.min)
                    act(b.Dm, b.Dm, AF.Exp)
                    act(b.E2, b.E2, AF.Exp)
                    tt("pool", v3(b.Dm, 4), v3(b.Dm, 4), bc(mA, [P, 4, 64], 1), ALU.mult)
                    tt("pool", v3(b.Dm, 4), v3(b.Dm, 4), bc(bts, [P, 4, 64], 2), ALU.mult)
                    tt("dve", b.A, pk[:, 0:NW], b.Dm, ALU.mult)
                    tt("pool", v3(b.E2, 4), v3(b.E2, 4), bc(mT, [P, 4, 64], 1), ALU.mult)
                    tt("dve", attnT_, pq[:, 0:NW], b.E2, ALU.mult)
                    pa = psum[:, b.pbanks[0] * 512: b.pbanks[0] * 512 + 512].bitcast(BF16)
                    for c2 in range(2):
                        prr = slice(c2 * 64, (c2 + 1) * 64)
                        for h4 in range(4):
                            cs = slice(h4 * 64, (h4 + 1) * 64)
                            tr(pa[prr, cs], b.A[prr, cs], ident_b[prr, prr])
                    cp("act", b.AT, pa[:, 0:NW])
                    yield
                    nb = bc(nb16_b, [P, 4, 64], 1)
                    tt("dve", v3(b.P0, 4), v3(b.A, 4), nb, ALU.mult)
                    tt("dve", v3(b.P0T, 4), v3(b.AT, 4), nb, ALU.mult)
                    tt("pool", v3(b.R, 4), v3(b.P0, 4), bc(I64_b, [P, 4, 64], 1), ALU.add)
                    tt("pool", v3(b.RT, 4), v3(b.P0T, 4), bc(I64_b, [P, 4, 64], 1), ALU.add)
                    tt("pool", v3(b.E1a, 4), v3(b.A, 4), bc(E1m_b, [P, 4, 64], 1), ALU.mult)
                    tt("pool", v3(b.E1Ta, 4), v3(b.AT, 4), bc(E1m_b, [P, 4, 64], 1), ALU.mult)
                    tt("pool", v3(b.E2a, 4), v3(b.A, 4), bc(E2m_b, [P, 4, 64], 1), ALU.mult)
                    tt("pool", v3(vb_, 4), v3(v_tm[:, tt_, :], 4), bc(bts, [P, 4, 128], 2), ALU.mult)
                    tt("pool", v3(b.kbg, 4), v3(k_tm[:, tt_, :], 4), bc(bg_tm[:, tt_, hs], [P, 4, 128], 2), ALU.mult)
                    tt("pool", v3(kdec_, 4), v3(k_tm[:, tt_, :], 4), bc(ekd_tm[:, tt_, hs], [P, 4, 128], 2), ALU.mult)
                    Ps = [(b.P0, b.P0T), (b.Pa, b.PaT), (b.Pb, b.PbT), (b.Y, b.Z)]
                    for rnd_ in range(4):
                        Pc, PcT = Ps[rnd_]
                        if rnd_ < 3:
                            Pn, PnT = Ps[rnd_ + 1]
                            p1 = H(0, 0)
                            p2 = H(0, 1)
                            mmblk(p1, PcT, Pc)
                            mmblk(p2, Pc, PcT)
                        if rnd_ >= 1:
                            p3 = H(1, 0)
                            p4 = H(1, 1)
                            mmblk(p3, PcT, b.R)
                            mmblk(p4, Pc, b.RT)
                        if rnd_ < 3:
                            cp("act", Pn, p1[:, 0:NW])
                            cp("act", PnT, p2[:, 0:NW])
                        if rnd_ >= 1:
                            tt("dve", b.R, p3[:, 0:NW], b.R, ALU.add)
                            tt("dve", b.RT, p4[:, 0:NW], b.RT, ALU.add)
                        yield
                    p1 = H(0, 0)
                    p2 = H(0, 1)
                    mmblk(p1, b.E1Ta, b.R)
                    mmblk(p2, b.E1a, b.RT)
                    cp("act", b.Y, p1[:, 0:NW])
                    cp("dve", b.Z, p2[:, 0:NW])
                    yield
                    p3 = H(1, 0)
                    p4 = H(1, 1)
                    mmblk(p3, b.RT, b.Y)
                    mmblk(p4, b.R, b.Z)
                    tt("dve", b.T2, b.R, p3[:, 0:NW], ALU.subtract)
                    tt("dve", b.W2, b.RT, p4[:, 0:NW], ALU.subtract)
                    yield
                    p1 = H(0, 0)
                    mmblk(p1, b.E2a, b.W2)
                    cp("act", b.Z, p1[:, 0:NW])
                    yield
                    p3 = H(1, 0)
                    mmblk(p3, b.T2, b.Z)
                    tt("dve", b.W3, b.W2, p3[:, 0:NW], ALU.subtract)
                    yield
                    for c2 in range(2):
                        prr = slice(c2 * 64, (c2 + 1) * 64)
                        pw = H(c2, 0)
                        for h4 in range(4):
                            mm(pw[:, h4 * 64:(h4 + 1) * 64], b.kbg[prr, h4 * 128:(h4 + 1) * 128], b.W3[prr, h4 * 64:(h4 + 1) * 64])
                        act(b.negwT[:, c2 * 256:(c2 + 1) * 256], pw[:, 0:256], AF.Identity, scale=-1.0)
                    yield

                def gdn_state(tt_, d_, b, par):
                    ER_ = b.ERd[par]
                    attnT_ = b.attnTd[par]
                    vb_ = b.vbd[par]
                    kdec_ = b.kdecd[par]
                    bC, bD = b.sbanks
                    order = (0, 1) if d_ == 0 else (1, 0)
                    for c2 in order:
                        c = tt_ * 2 + c2
                        prr = slice(c2 * 64, (c2 + 1) * 64)
                        tok = slice(c * 64, c * 64 + 64)
                        pv = psum[:, bC * 512:(bC + 1) * 512]
                        for h4 in range(4):
                            cs = slice(h4 * 128, (h4 + 1) * 128)
                            mm(pv[prr, cs], b.W3[prr, h4 * 64:(h4 + 1) * 64], vb_[prr, cs], start=True, stop=False)
                            mm(pv[prr, cs], b.negwT[:, c2 * 256 + h4 * 64: c2 * 256 + (h4 + 1) * 64], Sbf[:, d_ * 4 + h4, :], start=False, stop=True)
                        yield
                        cp("act", b.vnew[prr, :], pv[prr, :])
                        yield
                        poi = psum[:, bD * 512: bD * 512 + 256]
                        poa = psum[:, bD * 512 + 256: bD * 512 + 512]
                        pds = psum[:, bC * 512:(bC + 1) * 512]
                        for h4 in range(4):
                            cs = slice(h4 * 128, (h4 + 1) * 128)
                            c64 = slice(h4 * 64, (h4 + 1) * 64)
                            mm(poi[:, c64], Sbf[:, d_ * 4 + h4, :], qT[:, h4, tok])
                            mm(poa[:, c64], b.vnew[prr, cs], attnT_[prr, c64])
                            mm(pds[:, cs], kdec_[prr, cs], b.vnew[prr, cs])
                        yield
                        tt("dve", b.tmp, poi[:, 0:NW], ER_[c2], ALU.mult)
                        first = (tt_ <= 3) if d_ == 0 else (tt_ >= 4)
                        ov = opart[:, :, tok]
                        if first:
                            tt("dve", ov, v3(b.tmp, 4), v3(poa[:, 0:NW], 4), ALU.add)
                        else:
                            tt("dve", b.tmp, b.tmp, poa[:, 0:NW], ALU.add)
                            tt("pool", ov, ov, v3(b.tmp, 4), ALU.add)
                        yield
                        Sd = Sst[:, d_ * 4:(d_ + 1) * 4, :]
                        jl = 63 if d_ == 0 else 0
                        egl = v3(ER_[c2], 4)[:, :, jl:jl + 1].to_broadcast([P, 4, 128])
                        tt("pool", Sd, Sd, egl, ALU.mult)
                        last = (c == 15) if d_ == 0 else (c == 0)
                        segend = (c % 4 == 3) if d_ == 0 else (c % 4 == 0)
                        if not segend:
                            tt("dve", Sbf[:, d_ * 4:(d_ + 1) * 4, :], Sd, v3(pds, 4), ALU.add)
                            tt("dve", Sd, Sd, v3(pds, 4), ALU.add)
                        else:
                            tt("dve", Sd, Sd, v3(pds, 4), ALU.add)
                            seg = c // 4
                            outs_dma.append(dma("sp", nsgd[seg, l, d_, hg * 4:hg * 4 + 4].rearrange("h p v -> p h v"), Sd))
                            if not last:
                                ts("dve", Sd, Sd, carry, None, ALU.mult)
                                cp("act", Sbf[:, d_ * 4:(d_ + 1) * 4, :], Sd)
                        yield

                def run_gens(gens):
                    gens = list(gens)
                    if stop == "only_g2":
                        gens = gens[1:]
                    if stop == "only_g1":
                        gens = gens[:1]
                    rnd = 0
                    while gens:
                        rnd += 1
                        if l == 0 and hg == 0:
                            stage("gp_r%d" % rnd)
                            if rnd == 2 and stop in ("only_g1", "only_g2"):
                                raise _Stop()
                        nxt = []
                        for g_ in gens:
                            try:
                                next(g_)
                                nxt.append(g_)
                            except StopIteration:
                                pass
                        gens = nxt

                run_gens([gdn_prep(0, 0, sets[0], 0), gdn_prep(7, 1, sets[1], 0)])
                for i in range(8):
                    gens_ = [gdn_state(i, 0, sets[0], i % 2), gdn_state(7 - i, 1, sets[1], i % 2)]
                    if i < 7:
                        gens_ += [gdn_prep(i + 1, 0, sets[0], (i + 1) % 2), gdn_prep(6 - i, 1, sets[1], (i + 1) % 2)]
                    run_gens(gens_)
                    if l == 0:
                        for k_ in range(2 if i % 2 == 0 else 1):
                            mod_tile(1, 24 + 0 * 0 + hg * 12 + (i // 2) * 3 + (i % 2) * 2 + k_ - 24)
                        if hg == 1 and i == 7:
                            mod_finish(1)
                if hg == 0:
                    dbgdump("opart", opart[:, 0, :])
                stage("gdn_loop%d_%d" % (l, hg))
                AR.reset(AR_C1)
                MG.reset()
                wt = W.next("L%d_ggo%d" % (l, hg)).rearrange("p (k c) -> p k c", k=16)
                go = [AR.alloc([T], BF16) for _ in range(2)]
                on = [AR.alloc([T]) for _ in range(2)]
                for h4 in range(4):
                    pp = PSP()
                    for hf in range(2):
                        for kc in range(KC):
                            mm(pp[:, hf * 512:(hf + 1) * 512], wt[:, kc, h4 * 128:(h4 + 1) * 128], h_bf[:, kc, hf * 512:(hf + 1) * 512],
                               start=(kc == 0), stop=(kc == KC - 1))
                    act(go[h4 % 2], pp, AF.Silu)
                    MG.reset()
                    rstd = rms_rstd([opart[:, h4, :]], 128.0, [MG.alloc([T], BF16)], MG.alloc([T]))
                    stt(on[h4 % 2], opart[:, h4, :], vec_t[:, 306 + l:307 + l], rstd, ALU.mult, ALU.mult)
                    tt("dve", o_c[:, hg * 4 + h4, :], on[h4 % 2], go[h4 % 2], ALU.mult)
            dbgdump("o_c", o_c[:, 0, :])
            stage("gdn_fin%d" % l)
            AR.reset(AR_C0)
            AR_tmp_sig = [AR.alloc([T], BF16) for _ in range(2)]
            merge_branch(o_c, "C", True)
            dbgdump("mergedC", merged[:, 0, :])
            stage("mergeC%d" % l)

            AR.reset()
            TM.reset()
            o_a = AR.alloc([8, T], BF16)
            v_tm = TM.alloc([8, T], BF16)
            for cg in range(2):
                wt = W.next("L%d_hi%d" % (l, cg)).rearrange("p (k c) -> p k c", k=16)
                for tt_ in range(8):
                    pp = PSB()
                    for kc in range(KC):
                        mm(pp, h_bf[:, kc, tt_ * 128:(tt_ + 1) * 128], wt[:, kc, :], start=(kc == 0), stop=(kc == KC - 1))
                    cp("act" if tt_ % 2 else "dve", v_tm[:, tt_, cg * 512:(cg + 1) * 512], pp)
            AR_A0 = AR.mark()
            for hp in range(4):
                AR.reset(AR_A0)
                q_b = AR.alloc([2, T], BF16)
                go_b = AR.alloc([2, T], BF16)
                oacc = AR.alloc([2, T])
                wA = W.next("L%d_hA%d" % (l, hp)).rearrange("p (k c) -> p k c", k=16)
                for i in range(4):
                    pp = PSP()
                    for hf in range(2):
                        for kc in range(KC):
                            mm(pp[:, hf * 512:(hf + 1) * 512], wA[:, kc, i * 128:(i + 1) * 128], h_bf[:, kc, hf * 512:(hf + 1) * 512],
                               start=(kc == 0), stop=(kc == KC - 1))
                    act((q_b, go_b)[i // 2][:, i % 2, :], pp, AF.Silu)
                wB = W.next("L%d_hB%d" % (l, hp)).rearrange("p (k c) -> p k c", k=16)
                sig = AR.alloc([T])
                lf = AR.alloc([T])
                kk = AR.alloc([T])
                Bc = AR.alloc([T])
                t1, t2 = sig, lf
                sb = {}
                for nm in ("q16", "kl0", "kl1", "qd", "kds", "sT"):
                    sb[nm] = AR.alloc([T], BF16)
                S.add("pool", (lambda ap_: lambda e: e.memset(ap_, 0.0))(sb["sT"]), writes=[sb["sT"]])
                sb["kds_tm"] = AR.alloc([8, 128], BF16)
                sb["eBl"] = AR.alloc([16])
                sb["S"] = AR.alloc([128])
                sb["Sbf"] = AR.alloc([128], BF16)
                sb["Sbf2"] = AR.alloc([128], BF16)
                gate_pp = psum[:, 2 * 512:4 * 512]

                def gate_proj_part(i_, part_):
                    for idx_ in range(part_ * 8, part_ * 8 + 8):
                        hf, kc = idx_ // KC, idx_ % KC
                        mm(gate_pp[:, hf * 512:(hf + 1) * 512], wB[:, kc, i_ * 128:(i_ + 1) * 128], h_bf[:, kc, hf * 512:(hf + 1) * 512],
                           start=(kc == 0), stop=(kc == KC - 1))
                for part_ in range(4):
                    gate_proj_part(0, part_)
                for i in range(4):
                    d_ = i // 2
                    hh = i % 2
                    h_ = hp * 2 + hh
                    act(sig, gate_pp, AF.Sigmoid)
                    lbc = lbv[:, d_ * 8 + h_: d_ * 8 + h_ + 1]
                    omc = omlv[:, d_ * 8 + h_: d_ * 8 + h_ + 1]
                    nomc = nomlv[:, d_ * 8 + h_: d_ * 8 + h_ + 1]
                    ts("dve", lf, sig, omc, lbc, ALU.mult, ALU.add)
                    act(lf, lf, AF.Ln)
                    ts("dve", kk, sig, nomc, omc, ALU.mult, ALU.add)
                    if d_ == 0:
                        scan(Bc, reset64, lf)
                    else:
                        scan(rev(Bc), rev(reset63), rev(lf))
                    Bv = Bc.rearrange("p (c j) -> p c j", j=64)
                    jl = 63 if d_ == 0 else 0
                    qh = q_b[:, hh, :]
                    B4 = Bc.rearrange("p (c i j) -> p c i j", i=4, j=16)
                    t14 = t1.rearrange("p (c i j) -> p c i j", i=4, j=16)
                    if d_ == 0:
                        cp("dve", t14[:, :, 0, :], B4[:, :, 0, :])
                        tt("dve", t14[:, :, 1:4, :], B4[:, :, 1:4, :], B4[:, :, 0:3, 15:16].to_broadcast([P, 16, 3, 16]), ALU.subtract)
                    else:
                        cp("dve", t14[:, :, 3, :], B4[:, :, 3, :])
                        tt("dve", t14[:, :, 0:3, :], B4[:, :, 0:3, :], B4[:, :, 1:4, 0:1].to_broadcast([P, 16, 3, 16]), ALU.subtract)
                    act(t1, t1, AF.Exp)
                    tt("dve", sb["q16"], qh, t1, ALU.mult)
                    tt("dve", t2.rearrange("p (c j) -> p c j", j=64), Bv[:, :, jl:jl + 1].to_broadcast([P, 16, 64]), Bv, ALU.subtract)
                    act(sb["eBl"], Bv[:, :, jl], AF.Exp)
                    act(t2, t2, AF.Exp)
                    tt("dve", sb["kds"], kk, t2, ALU.mult)
                    act(t1, Bc, AF.Exp)
                    tt("dve", sb["qd"], qh, t1, ALU.mult)
                    for half in range(2):
                        pt = PSB("s", bf=True)
                        for q4 in range(4):
                            tt_ = half * 4 + q4
                            tr(pt[:, q4 * 128:(q4 + 1) * 128], sb["kds"][:, tt_ * 128:(tt_ + 1) * 128], ident_b)
                        cp("act", sb["kds_tm"][:, half * 4:(half + 1) * 4, :], pt[:, 0:512].rearrange("p (a b) -> p a b", a=4))
                    hm = (hmaskF_b, hmaskB_b)[d_]
                    pts = [PSB("lo"), PSB("lo")]
                    kk3 = kk.rearrange("p (c j) -> p c j", j=64)
                    t13 = t1.rearrange("p (c j) -> p c j", j=64)
                    for ip, I in enumerate((0, 1, 2, 3) if d_ == 0 else (3, 2, 1, 0)):
                        kl = sb["kl%d" % (ip % 2)]
                        kl3 = kl.rearrange("p (c j) -> p c j", j=64)
                        if ip < 2:
                            S.add("pool", (lambda ap_: lambda e: e.memset(ap_, 0.0))(kl), writes=[kl])
                        rng_ = slice(0, 16 * (I + 1)) if d_ == 0 else slice(16 * I, 64)
                        n_ = rng_.stop - rng_.start
                        if d_ == 0:
                            if I == 0:
                                ts("dve", t13[:, :, rng_], Bv[:, :, rng_], -1.0, None, ALU.mult)
                            else:
                                tt("dve", t13[:, :, rng_], Bv[:, :, 16 * I - 1:16 * I].to_broadcast([P, 16, n_]), Bv[:, :, rng_], ALU.subtract)
                        else:
                            if I == 3:
                                ts("dve", t13[:, :, rng_], Bv[:, :, rng_], -1.0, None, ALU.mult)
                            else:
                                tt("dve", t13[:, :, rng_], Bv[:, :, 16 * (I + 1):16 * (I + 1) + 1].to_broadcast([P, 16, n_]), Bv[:, :, rng_], ALU.subtract)
                        act(t13[:, :, rng_], t13[:, :, rng_], AF.Exp)
                        tt("dve", kl3[:, :, rng_], kk3[:, :, rng_], t13[:, :, rng_], ALU.mult)
                        for tb in range(8):
                            for c2 in range(2):
                                c = tb * 2 + c2
                                col = (tb % 4) * 128 + c2 * 64 + I * 16
                                mm(pts[tb // 4][c2 * 64:(c2 + 1) * 64, col:col + 16], kl[:, c * 64:(c + 1) * 64], sb["q16"][:, c * 64 + I * 16: c * 64 + I * 16 + 16])
                    for half in range(2):
                        for c2 in range(2):
                            prr = slice(c2 * 64, (c2 + 1) * 64)
                            cs_ = slice(c2 * 64, (c2 + 1) * 64)
                            tt("dve", sb["sT"][prr, half * 512:(half + 1) * 512].rearrange("p (a b) -> p a b", a=4)[:, :, cs_],
                               pts[half][prr, :].rearrange("p (a b) -> p a b", a=4)[:, :, cs_], bc(hm[prr, cs_], [64, 4, 64], 1), ALU.mult)
                    dma("sp", sb["S"], s0hg[l][:, d_ * 8 + h_, :])
                    sring = [sb["Sbf"], sb["Sbf2"]]
                    cur = 0
                    cp("act", sring[cur], sb["S"])
                    pdb = [PSB("s") for _ in range(4)]
                    def pdt(c):
                        tb_, c2_ = c // 2, c % 2
                        return pdb[c2_ * 2 + tb_ // 4][:, (tb_ % 4) * 128:(tb_ % 4 + 1) * 128]
                    for c in range(16):
                        tb, c2 = c // 2, c % 2
                        prr = slice(c2 * 64, (c2 + 1) * 64)
                        mm(pdt(c), sb["kds_tm"][prr, tb, :], v_tm[prr, tb, h_ * 128:(h_ + 1) * 128])
                    tbs = range(8) if d_ == 0 else range(7, -1, -1)
                    po = None
                    for n_, tb in enumerate(tbs):
                        if i < 3 and n_ % 2 == 0:
                            gate_proj_part(i + 1, n_ // 2)
                        if n_ % 4 == 0:
                            po = PSB("lo")
                            hfidx = tb // 4
                        q4 = tb % 4
                        blk = slice(tb * 128, (tb + 1) * 128)
                        mm(po[:, q4 * 128:(q4 + 1) * 128], v_tm[:, tb, h_ * 128:(h_ + 1) * 128], sb["sT"][:, blk], start=True, stop=False)
                        c2s = (0, 1) if d_ == 0 else (1, 0)
                        for ci, c2 in enumerate(c2s):
                            c = tb * 2 + c2
                            tok = slice(c * 64, (c + 1) * 64)
                            mm(po[:, q4 * 128 + c2 * 64: q4 * 128 + c2 * 64 + 64], sring[cur], sb["qd"][:, tok], start=False, stop=(ci == 1))
                            pd = pdt(c)
                            last = (c == 15) if d_ == 0 else (c == 0)
                            segend = (c % 4 == 3) if d_ == 0 else (c % 4 == 0)
                            if not segend:
                                stt(sb["S"], sb["S"], sb["eBl"][:, c:c + 1], pd, ALU.mult, ALU.add)
                                cp("act", sring[1 - cur], sb["S"])
                                cur = 1 - cur
                            else:
                                stt(sb["S"], sb["S"], sb["eBl"][:, c:c + 1], pd, ALU.mult, ALU.add)
                                outs_dma.append(dma("sp", nshg[c // 4, l, d_, h_], sb["S"]))
                                if not last:
                                    ts("dve", sb["S"], sb["S"], carry, None, ALU.mult)
                                    cp("act", sring[1 - cur], sb["S"])
                                    cur = 1 - cur
                        if n_ % 4 == 3:
                            ov = oacc[:, hh, hfidx * 512:(hfidx + 1) * 512]
                            if d_ == 0:
                                cp("act", ov, po)
                            else:
                                tt("dve", ov, ov, po, ALU.add)
                if hp == 0:
                    dbgdump("oacc", oacc[:, 0, :])
                sqb = [kk[:, 0:512].bitcast(BF16), kk[:, 512:1024].bitcast(BF16)]
                for hh in range(2):
                    rstd = rms_rstd([oacc[:, hh, :]], 128.0, sqb, Bc)
                    stt(t1, oacc[:, hh, :], vec_t[:, 304 + l:305 + l], rstd, ALU.mult, ALU.mult)
                    tt("dve", o_a[:, hp * 2 + hh, :], t1, go_b[:, hh, :], ALU.mult)
            dbgdump("o_a", o_a[:, 0, :])
            stage("hgrn%d" % l)
            AR.reset(AR_A0)
            AR_tmp_sig = [AR.alloc([T], BF16) for _ in range(2)]
            merge_branch(o_a, "A", False)

            AR.reset()
            TM.reset()
            o_b = AR.alloc([8, T], BF16)
            u_b = AR.alloc([8, T], BF16)
            bsb = AR.alloc([T])
            dma("sp", bsb, cmbs[l].partition_broadcast(P))
            vn_tm = TM.alloc([8, T], BF16)
            for cg in range(2):
                wt = W.next("L%d_cu%d" % (l, cg)).rearrange("p (k c) -> p k c", k=16)
                for j in range(4):
                    pp = PSP()
                    for hf in range(2):
                        for kc in range(KC):
                            mm(pp[:, hf * 512:(hf + 1) * 512], wt[:, kc, j * 128:(j + 1) * 128], h_bf[:, kc, hf * 512:(hf + 1) * 512],
                               start=(kc == 0), stop=(kc == KC - 1))
                    act(u_b[:, cg * 4 + j, :], pp, AF.Gelu)
            gv = [AR.alloc([512]) for _ in range(2)]
            sqv = AR.alloc([512])
            ssv = AR.alloc([8, 8])
            for cg in range(2):
                wt = W.next("L%d_cv%d" % (l, cg)).rearrange("p (k c) -> p k c", k=16)
                for tt_ in range(8):
                    pp = PSB()
                    for kc in range(KC):
                        mm(pp, h_bf[:, kc, tt_ * 128:(tt_ + 1) * 128], wt[:, kc, :], start=(kc == 0), stop=(kc == KC - 1))
                    g_ = gv[tt_ % 2]
                    act(g_, pp, AF.Gelu)
                    act(sqv, g_, AF.Square)
                    ssl = ssv[:, tt_, cg * 4:(cg + 1) * 4]
                    S.add("dve", (lambda ssl=ssl: lambda e: e.reduce_sum(out=ssl, in_=sqv.rearrange("p (g c) -> p g c", g=4), axis=mybir.AxisListType.X))(),
                          reads=[sqv], writes=[ssl])
                    act(ssl, ssl, AF.Ln, scale=1.0 / 128.0, bias=EPS)
                    act(ssl, ssl, AF.Exp, scale=-0.5)
                    tt("dve", vn_tm[:, tt_, cg * 512:(cg + 1) * 512].rearrange("p (g c) -> p g c", g=4), g_.rearrange("p (g c) -> p g c", g=4),
                       bc(ssl, [P, 4, 128], 2), ALU.mult)
            for g in range(8):
                for half in range(2):
                    pt = PSB()
                    for q4 in range(4):
                        tt_ = half * 4 + q4
                        mm(pt[:, q4 * 128:(q4 + 1) * 128], vn_tm[:, tt_, g * 128:(g + 1) * 128], wsT_b[:, g, :])
                    s_ = gv[half]
                    stt(s_.rearrange("p (a b) -> p a b", a=4), pt.rearrange("p (a b) -> p a b", a=4), vec_t[:, 308 + 8 * l + g: 309 + 8 * l + g],
                        bc(bsb[:, g * 128:(g + 1) * 128], [P, 4, 128], 1), ALU.mult, ALU.add)
                    tt("dve", o_b[:, g, half * 512:(half + 1) * 512], s_, u_b[:, g, half * 512:(half + 1) * 512], ALU.mult)
            dbgdump("o_b", o_b[:, 0, :])
            stage("gmlp%d" % l)
            AR_tmp_sig = [AR.alloc([T], BF16) for _ in range(2)]
            merge_branch(o_b, "B", False)
            dbgdump("merged", merged[:, 0, :])
            stage("merged%d" % l)

            for j4 in range(4):
                wt = W.next("L%d_wo%d" % (l, j4)).rearrange("p (k c) -> p k c", k=16)
                for jj in range(4):
                    j = j4 * 4 + jj
                    if l == 0:
                        dma("sp", xs[:, j, :], xTv[:, j, :])
                    else:
                        dma("sp", xs[:, j, :], xsv[:, j, :], rk=[("xscr", j)])
                    pp = PSP()
                    for hf in range(2):
                        for kc in range(KC):
                            mm(pp[:, hf * 512:(hf + 1) * 512], wt[:, kc, jj * 128:(jj + 1) * 128], merged[:, kc, hf * 512:(hf + 1) * 512],
                               start=(kc == 0), stop=(kc == KC - 1))
                    stt(xs[:, j, :], pp, gate1[:, j:j + 1], xs[:, j, :], ALU.mult, ALU.add)
            dbgdump("x_mid%d" % l, xs[:, 0, :])
            stage("xmid%d" % l)

            def out_h2(c, t_, _gs=gs2[l], _sh=sh2):
                act(h_bf[:, c, :], t_, AF.Identity, scale=_gs[:, c:c + 1], bias=_sh[:, c:c + 1])
            norm_mod(gs2[l], sh2, out_h2)
            a_b = V(OFF_MG, [16, T], BF16)
            sqf = [V(OFF_TM + i * 2048, [T], BF16) for i in range(2)]
            for g in range(4):
                for c4 in range(4):
                    wt = W.next("L%d_f1_%d_%d" % (l, g, c4)).rearrange("p (k c) -> p k c", k=16)
                    for jj in range(4):
                        pp = PSP()
                        for hf in range(2):
                            for kc in range(KC):
                                mm(pp[:, hf * 512:(hf + 1) * 512], wt[:, kc, jj * 128:(jj + 1) * 128], h_bf[:, kc, hf * 512:(hf + 1) * 512],
                                   start=(kc == 0), stop=(kc == KC - 1))
                        act(sqf[jj % 2], pp, AF.Square)
                        stt(a_b[:, c4 * 4 + jj, :], pp, 0.0, sqf[jj % 2], ALU.is_gt, ALU.mult)
                for j4 in range(4):
                    wt = W.next("L%d_f2_%d_%d" % (l, g, j4)).rearrange("p (k c) -> p k c", k=16)
                    for jj in range(4):
                        j = j4 * 4 + jj
                        pp = PSP()
                        for hf in range(2):
                            for kc in range(KC):
                                mm(pp[:, hf * 512:(hf + 1) * 512], wt[:, kc, jj * 128:(jj + 1) * 128], a_b[:, kc, hf * 512:(hf + 1) * 512],
                                   start=(kc == 0), stop=(kc == KC - 1))
                        stt(xs[:, j, :], pp, gate2[:, j:j + 1], xs[:, j, :], ALU.mult, ALU.add)
            dbgdump("x_out%d" % l, xs[:, 0, :])
            stage("xout%d" % l)

        yb = [V(OFF_H + i * 4096, [T]) for i in range(2)]

        def out_y(c, t_):
            act(yb[c % 2], t_, AF.Identity, scale=vec_t[:, 64 + c:65 + c])
            outs_dma.append(dma("sp", yTv[:, c, :], yb[c % 2]))
        norm_mod(None, None, out_y)
    except _Stop:
        pass

    S.emit(final_waits=outs_dma)
    es.close()
    return nc


def make_consts():
    c = np.zeros((P, 2048), np.float32)
    idx = np.arange(128)
    same64 = (idx[:, None] // 64) == (idx[None, :] // 64)
    c[:, 0:128] = same64
    c[:, 128:256] = same64 & (idx[:, None] <= idx[None, :])
    c[:, 256:384] = same64 & (idx[:, None] >= idx[None, :])
    c[0:64, 384:512] = 1.0
    c[64:128, 512:640] = 1.0
    o = 640
    c[:, o:o + 128] = np.eye(128)
    c[:, o + 128:o + 256] = same64 & (idx[:, None] <= idx[None, :])
    c[:, o + 256:o + 384] = same64 & (idx[:, None] >= idx[None, :])
    c[:, o + 384:o + 512] = 1.0
    i64 = np.arange(64)
    p64 = idx % 64
    m = o + 512
    c[:, m:m + 64] = (p64[:, None] == i64[None, :])
    c[:, m + 64:m + 128] = (p64[:, None] > i64[None, :])
    c[:, m + 128:m + 192] = (p64[:, None] < i64[None, :])
    c[:, m + 192:m + 256] = (p64[:, None] >= i64[None, :])
    c[:, m + 256:m + 320] = (p64[:, None] <= i64[None, :])
    b16 = (p64[:, None] // 16) == (i64[None, :] // 16)
    b32 = (p64[:, None] // 32) == (i64[None, :] // 32)
    c[:, m + 320:m + 384] = -1.0 * b16
    c[:, m + 384:m + 448] = b32 & ~b16
    c[:, m + 448:m + 512] = ~b32
    return c


_CACHE = {}


def _get_program(dbg=None):
    key = None if not dbg else tuple(sorted(dbg.items()))
    if key not in _CACHE:
        _CACHE[key] = build_program(dbg)
    return _CACHE[key]


def kernel(x_prompt, x_sample, c, state_hgrn, state_gdn, c_ctx, norm1_g, norm2_g, w_mod, b_mod, w_in,
           hg_lb, hg_onorm_g, cm_vnorm_g, cm_ws, cm_bs, gdn_conv, gdn_A_log, gdn_dt_bias, gdn_onorm_g,
           w_br_hg, w_br_cm, w_br_gdn, w_out, w_ff1, w_ff2, final_g, _dbg=None, _stop=None, _cores=None):
    f32 = np.float32
    A = lambda t: np.asarray(t, f32)
    x_prompt, x_sample, c, state_hgrn, state_gdn, c_ctx = map(A, (x_prompt, x_sample, c, state_hgrn, state_gdn, c_ctx))
    w_mod, b_mod, w_in = A(w_mod), A(b_mod), A(w_in)
    wsl = [build_wstream(w_in[l], A(w_br_hg)[l], A(w_br_cm)[l], A(w_br_gdn)[l], A(w_out)[l], A(w_ff1)[l], A(w_ff2)[l])
           for l in range(DEPTH)]
    parts = []
    for l in range(DEPTH):
        for t_ in range(24):
            for kc in range(16):
                parts.append(w_mod[l, kc * 128:(kc + 1) * 128, t_ * 512:(t_ + 1) * 512])
    wmod_h = np.ascontiguousarray(np.concatenate(parts, axis=1))
    wgab_h = np.ascontiguousarray(np.stack([
        np.concatenate([w_in[l, kc * 128:(kc + 1) * 128, C_GA:C_GA + 32] for kc in range(16)], axis=1) for l in range(DEPTH)]))
    vecs_h = np.zeros((P, 512), f32)
    for l in range(DEPTH):
        vecs_h[:, 16 * l:16 * l + 16] = fm(A(norm1_g)[l], 16)
        vecs_h[:, 32 + 16 * l:48 + 16 * l] = fm(A(norm2_g)[l], 16)
        vecs_h[:, 80 + 96 * l:176 + 96 * l] = fm(b_mod[l], 96)
        vecs_h[:, 272 + 16 * l:288 + 16 * l] = fm(A(hg_lb)[l].reshape(-1), 16)
        vecs_h[:, 304 + l] = A(hg_onorm_g)[l]
        vecs_h[:, 306 + l] = A(gdn_onorm_g)[l]
        vecs_h[:, 308 + 8 * l:316 + 8 * l] = fm(A(cm_vnorm_g)[l], 8)
    vecs_h[:, 64:80] = fm(A(final_g), 16)
    gc_ = A(gdn_conv)
    convw_h = np.ascontiguousarray(np.stack([
        gc_[l].reshape(9, 24, 128).transpose(2, 1, 0).reshape(P, 24 * 9) for l in range(DEPTH)]))
    cmwsT_h = np.ascontiguousarray(np.stack([A(cm_ws)[l].transpose(2, 0, 1).reshape(P, 8 * 128) for l in range(DEPTH)]))
    cmbs_h = np.ascontiguousarray(A(cm_bs).reshape(DEPTH, 1, 1024))
    rowc_h = np.zeros((DEPTH, 1, 64), f32)
    for l in range(DEPTH):
        rowc_h[l, 0, 0:16] = A(gdn_A_log)[l].reshape(-1)
        rowc_h[l, 0, 16:32] = A(gdn_dt_bias)[l].reshape(-1)
    consts_h = make_consts()
    tpos = np.arange(T)
    in_maps = []
    for core in range(NCORES):
        sample = core < 4
        if sample:
            b = core
            xt = x_sample[b]
            cv = c[b]
            shg = state_hgrn[b]
            sgd = state_gdn[b]
            period = 64
        else:
            b0 = (core - 4) * 4
            xt = x_prompt[b0:b0 + 4].reshape(T, D)
            cv = c_ctx
            shg = np.zeros_like(state_hgrn[0])
            sgd = np.zeros_like(state_gdn[0])
            period = 256
        flags_h = np.zeros((P, 16), f32)
        flags_h[:, 0] = 1.0 if sample else 0.0
        for dr in range(3):
            for dc in range(3):
                flags_h[:, 1 + dr * 3 + dc] = 1.0 if (sample or dr == 1) else 0.0
        cm = np.zeros((2, T), f32)
        cm[0] = (tpos % period) != (period - 1)
        cm[1] = (tpos % period) != 0
        m = {
            "xT": np.ascontiguousarray(xt.T),
            "cond": fm(cv, 16),
            "s0hg": np.ascontiguousarray(shg.reshape(DEPTH, 16, 128, 128).transpose(0, 2, 1, 3)),
            "s0gd": np.ascontiguousarray(sgd.reshape(DEPTH, 16, 128, 128).transpose(0, 2, 1, 3)),
            "flags": flags_h, "cmask": cm, "wmod": wmod_h, "wgab": wgab_h, "vecs": vecs_h, "convw": convw_h,
            "cmwsT": cmwsT_h, "cmbs": cmbs_h, "rowc": rowc_h, "consts": consts_h,
        }
        for l in range(DEPTH):
            m["ws%d" % l] = wsl[l]
        in_maps.append(m)
    if _cores is not None:
        nc = build_program(_dbg, _stop)
        res = run_bass_kernel_spmd(nc, [in_maps[k] for k in _cores], core_ids=list(range(len(_cores))))
        return [{k: np.asarray(v) for k, v in ri.items()} for ri in res.results]
    nc = _get_program(_dbg)
    res = run_bass_kernel_spmd(nc, in_maps, core_ids=list(range(NCORES)))
    r = res.results
    y_sample = np.stack([r[i]["yT"].T for i in range(4)]).astype(f32)
    y_prompt = np.concatenate([r[i]["yT"].T.reshape(4, 256, D) for i in range(4, 8)]).astype(f32)
    nhg = np.concatenate([r[i]["nshg"] for i in range(4, 8)]).astype(f32)
    ngd = np.concatenate([r[i]["nsgd"] for i in range(4, 8)]).astype(f32)
    if _dbg:
        kernel.last_dbg = [{k: v for k, v in ri.items() if k.startswith("dbg_")} for ri in r]
    return (y_prompt, y_sample, nhg, ngd)
```

```python
import contextlib
import numpy as np
import concourse.bass as bass
import concourse.mybir as mybir
from concourse.ap import AP
from concourse.bass_utils import run_bass_kernel_spmd

F32 = mybir.dt.float32
BF16 = mybir.dt.bfloat16
AF = mybir.ActivationFunctionType
ALU = mybir.AluOpType

P = 128
T = 1024
D = 2048
KC = 16
DEPTH = 2
EPS = 1e-6
NCORES = 8
NDMASEM = 8
C_HQ, C_HI, C_HGO, C_HFF, C_HFB, C_CU, C_CV = 0, 1024, 2048, 3072, 4096, 5120, 6144
C_GQ, C_GK, C_GV, C_GGO, C_GA, C_GB = 7168, 8192, 9216, 10240, 11264, 11280
C_GATE_A, C_GATE_B, C_GATE_C = 11296, 13344, 15392


class Op:
    __slots__ = ("issuer", "stream", "idx", "fn", "waits", "signal", "semval", "is_dma", "src")


def apkeys(x):
    if not isinstance(x, AP):
        return [x]
    tn = type(x.tensor).__name__
    if tn.startswith("DRam"):
        return []
    esz = 2 if x.dtype == BF16 else 4
    rowlen = x.tensor.shape[1]
    off = int(x.offset) % rowlen
    lo = hi = off
    for st, cnt in x.ap[1:]:
        if st >= 0:
            hi += st * (cnt - 1)
        else:
            lo += st * (cnt - 1)
    b0 = lo * esz
    b1 = (hi + 1) * esz
    gran = 2048 if tn.startswith("PSum") else 512
    nm = "ps" if tn.startswith("PSum") else "sb"
    return [(nm, g) for g in range(b0 // gran, (b1 - 1) // gran + 1)]


class Sched:
    def __init__(self, nc):
        self.nc = nc
        self.per_issuer = {e: [] for e in ("pe", "act", "dve", "pool", "sp")}
        self.stream_ops = {}
        self.last_writer = {}
        self.readers = {}
        self.clock = {e: {} for e in self.per_issuer}
        self.opclock = {}
        self.dma_count = {e: 0 for e in self.per_issuer}
        self.nops = 0
        self.debug_src = False
        self.srcmap = {}

    def add(self, issuer, fn, reads=(), writes=(), dma=False):
        op = Op()
        op.issuer = issuer
        op.fn = fn
        op.is_dma = dma
        op.signal = False
        op.semval = None
        op.src = None
        if self.debug_src:
            import sys as _sys
            f = _sys._getframe(1)
            names = []
            while f is not None and len(names) < 4:
                if f.f_code.co_name not in ("dma", "mm", "tr", "act", "tt", "ts", "stt", "cp", "scan", "mmblk"):
                    names.append(f.f_lineno)
                f = f.f_back
            op.src = names
        if dma:
            n = self.dma_count[issuer]
            self.dma_count[issuer] += 1
            op.stream = "d_%s_%d" % (issuer, n % NDMASEM)
        else:
            op.stream = issuer
        so = self.stream_ops.setdefault(op.stream, [])
        op.idx = len(so) + 1
        rkeys = []
        for r in reads:
            rkeys.extend(apkeys(r))
        wkeys = []
        for w in writes:
            wkeys.extend(apkeys(w))
        deps = []
        raw = set()
        for k in rkeys:
            w = self.last_writer.get(k)
            if w is not None:
                deps.append(w)
                raw.add(id(w))
            if k[0] == "ps":
                rd = self.readers.get(k)
                if rd:
                    for r_ in rd.values():
                        if r_.stream != op.stream:
                            deps.append(r_)
        for k in wkeys:
            w = self.last_writer.get(k)
            if w is not None:
                deps.append(w)
            rd = self.readers.get(k)
            if rd:
                deps.extend(rd.values())
        if dma and so:
            deps.append(so[-1])
        clk = self.clock[issuer]
        best = {}
        for d in deps:
            if d.stream == op.stream and not dma:
                if issuer == "pe":
                    continue
            if clk.get(d.stream, 0) >= d.idx:
                continue
            if d.stream not in best or best[d.stream].idx < d.idx:
                best[d.stream] = d
        op.waits = list(best.values())
        for d in op.waits:
            d.signal = True
            if clk.get(d.stream, 0) < d.idx:
                clk[d.stream] = d.idx
            for s, i in self.opclock[id(d)].items():
                if clk.get(s, 0) < i:
                    clk[s] = i
        so.append(op)
        myclk = dict(clk)
        myclk[op.stream] = op.idx
        self.opclock[id(op)] = myclk
        for k in rkeys:
            self.readers.setdefault(k, {})[op.stream] = op
        for k in wkeys:
            self.last_writer[k] = op
            self.readers[k] = {}
        self.per_issuer[issuer].append(op)
        self.nops += 1
        return op

    def emit(self, final_waits=()):
        nc = self.nc
        for s, so in self.stream_ops.items():
            c = 0
            for o in so:
                if s.startswith("d_"):
                    o.signal = True
                if o.signal:
                    c += 1
                    o.semval = c
        sems = {}
        with contextlib.ExitStack() as es:
            for s in self.stream_ops:
                sems[s] = es.enter_context(nc.semaphore("s_" + s))
            block = es.enter_context(nc.Block())
            engs = {"pe": block.tensor, "act": block.scalar, "dve": block.vector,
                    "pool": block.gpsimd, "sp": block.sync}

            def make(issuer):
                def body(eng):
                    for o in self.per_issuer[issuer]:
                        for d in o.waits:
                            eng.wait_ge(sems[d.stream], d.semval * (16 if d.stream.startswith("d_") else 1))
                        ins = o.fn(eng)
                        if self.debug_src:
                            try:
                                self.srcmap[ins.ins.name] = o.src
                            except Exception:
                                pass
                        if o.signal:
                            ins.then_inc(sems[o.stream], 16 if o.is_dma else 1)
                    if issuer == "sp":
                        for s_, so_ in self.stream_ops.items():
                            if s_.startswith("d_") and so_:
                                eng.wait_ge(sems[s_], so_[-1].semval * 16)
                return body
            for issuer in ("sp", "pe", "act", "dve", "pool"):
                engs[issuer](make(issuer))


def tile_order():
    o = []
    for hg in range(2):
        o += [("gq%d" % hg, 8192), ("gk%d" % hg, 8192), ("gv%d" % hg, 8192), ("ggo%d" % hg, 8192)]
    o += [("mC%d" % j, 6144) for j in range(8)]
    o += [("hi%d" % c, 8192) for c in range(2)]
    for hp in range(4):
        o += [("hA%d" % hp, 8192), ("hB%d" % hp, 8192)]
    o += [("mA%d" % j, 6144) for j in range(8)]
    o += [("cu%d" % c, 8192) for c in range(2)]
    o += [("cv%d" % c, 8192) for c in range(2)]
    o += [("mB%d" % j, 6144) for j in range(8)]
    o += [("wo%d" % j, 8192) for j in range(4)]
    for g in range(4):
        o += [("f1_%d_%d" % (g, c), 8192) for c in range(4)]
        o += [("f2_%d_%d" % (g, j), 8192) for j in range(4)]
    return o


def build_wstream(w_in, wbr_hg, wbr_cm, wbr_gdn, w_out, w_ff1, w_ff2):
    def full(M, c0, n=512):
        return [(M, kc * 128, c0, n) for kc in range(16)]
    spec = {}
    for hg in range(2):
        spec["gq%d" % hg] = full(w_in, C_GQ + hg * 512)
        spec["gk%d" % hg] = full(w_in, C_GK + hg * 512)
        spec["gv%d" % hg] = full(w_in, C_GV + hg * 512)
        spec["ggo%d" % hg] = full(w_in, C_GGO + hg * 512)
    for nm, gc0, wbr in (("mC", C_GATE_C, wbr_gdn), ("mA", C_GATE_A, wbr_hg), ("mB", C_GATE_B, wbr_cm)):
        for j in range(8):
            spec["%s%d" % (nm, j)] = full(w_in, gc0 + j * 256, 256) + [(wbr, kc * 128, j * 256, 256) for kc in range(8)]
    for c in range(2):
        spec["hi%d" % c] = full(w_in, C_HI + c * 512)
        spec["cu%d" % c] = full(w_in, C_CU + c * 512)
        spec["cv%d" % c] = full(w_in, C_CV + c * 512)
    for hp in range(4):
        sA = []
        sB = []
        for kc in range(16):
            sA += [(w_in, kc * 128, C_HQ + hp * 256, 256), (w_in, kc * 128, C_HGO + hp * 256, 256)]
            sB += [(w_in, kc * 128, C_HFF + hp * 256, 256), (w_in, kc * 128, C_HFB + hp * 256, 256)]
        spec["hA%d" % hp] = sA
        spec["hB%d" % hp] = sB
    for j in range(4):
        spec["wo%d" % j] = full(w_out, j * 512)
    for g in range(4):
        for c in range(4):
            spec["f1_%d_%d" % (g, c)] = full(w_ff1, g * 2048 + c * 512)
        for j in range(4):
            spec["f2_%d_%d" % (g, j)] = [(w_ff2, (g * 16 + kc) * 128, j * 512, 512) for kc in range(16)]
    parts = []
    for nm, n in tile_order():
        tot = 0
        for (M, r0, c0, w) in spec[nm]:
            parts.append(M[r0:r0 + 128, c0:c0 + w])
            tot += w
        assert tot == n, (nm, tot, n)
    return np.ascontiguousarray(np.concatenate(parts, axis=1))


def fm(vec, nchunk):
    return np.ascontiguousarray(np.asarray(vec, np.float32).reshape(nchunk, 128).T)


NF = 52992
OFF_CONST = 0
SZ_CONST = 30 * 1024
OFF_H = OFF_CONST + SZ_CONST
OFF_MG = OFF_H + 32 * 1024
OFF_TM = OFF_MG + 32 * 1024
OFF_W = OFF_TM + 16 * 1024
OFF_AR = OFF_W + 32 * 1024
assert OFF_AR + 64 * 1024 <= NF * 4


class _Stop(Exception):
    pass


def build_program(dbg=None, stop=None):
    nc = bass.Bass("TRN2", target_bir_lowering=False)
    WTOT = sum(n for _, n in tile_order())

    def din(name, shape):
        return nc.dram_tensor(name, list(shape), F32, kind="ExternalInput").ap()

    def dout(name, shape):
        return nc.dram_tensor(name, list(shape), F32, kind="ExternalOutput").ap()

    xT = din("xT", [D, T])
    cond = din("cond", [P, 16])
    s0hg = din("s0hg", [DEPTH, P, 16, 128])
    s0gd = din("s0gd", [DEPTH, P, 16, 128])
    flags = din("flags", [P, 16])
    cmask = din("cmask", [2, T])
    ws = [din("ws%d" % l, [P, WTOT]) for l in range(DEPTH)]
    wmod = din("wmod", [P, DEPTH * 24 * 8192])
    wgab = din("wgab", [DEPTH, P, 16 * 32])
    vecs = din("vecs", [P, 512])
    convw = din("convw", [DEPTH, P, 24 * 9])
    cmwsT = din("cmwsT", [DEPTH, P, 8 * 128])
    cmbs = din("cmbs", [DEPTH, 1, 1024])
    rowc = din("rowc", [DEPTH, 1, 64])
    consts = din("consts", [P, 2048])
    yT = dout("yT", [D, T])
    nshg = dout("nshg", [4, DEPTH, 2, 8, P, 128])
    nsgd = dout("nsgd", [4, DEPTH, 2, 8, P, 128])
    xscr = nc.dram_tensor("xscr", [D, T], F32, kind="Internal").ap()
    dbg_out = {}
    if dbg:
        for nm, shp in dbg.items():
            dbg_out[nm] = dout("dbg_" + nm, shp)

    es = contextlib.ExitStack()
    arena = es.enter_context(nc.sbuf_tensor("arena", [P, NF], F32))
    psum = es.enter_context(nc.psum_tensor("psum", [P, 4096], F32))
    S = Sched(nc)
    import os as _os
    S.debug_src = bool(_os.environ.get("KDEBUG_SRC"))
    nc._sched = S
    outs_dma = []

    def V(off, dims, dt=F32):
        n = 1
        for d_ in dims:
            n *= d_
        assert off % 4 == 0
        if dt == F32:
            ap = arena[:, off // 4: off // 4 + n]
        else:
            assert n % 2 == 0
            ap = arena[:, off // 4: off // 4 + n // 2].bitcast(BF16)
        if len(dims) == 2:
            ap = ap.rearrange("p (a b) -> p a b", a=dims[0])
        elif len(dims) == 3:
            ap = ap.rearrange("p (a b c) -> p a b c", a=dims[0], b=dims[1])
        return ap

    class Bump:
        def __init__(self, base, size):
            self.base, self.size, self.cur = base, size, base

        def alloc(self, dims, dt=F32):
            n = 1
            for d_ in dims:
                n *= d_
            nb = n * (4 if dt == F32 else 2)
            nb = (nb + 511) // 512 * 512
            off = self.cur
            self.cur += nb
            assert self.cur <= self.base + self.size, ("region overflow", self.base, self.cur - self.base, self.size)
            return V(off, dims, dt)

        def reset(self, to=None):
            self.cur = self.base if to is None else to

        def mark(self):
            return self.cur

    CR = Bump(OFF_CONST, SZ_CONST)
    MG = Bump(OFF_MG, 32 * 1024)
    TM = Bump(OFF_TM, 16 * 1024)
    AR = Bump(OFF_AR, 64 * 1024)
    h_bf = V(OFF_H, [16, T], BF16)
    wbuf = [V(OFF_W + i * 16384, [8192], BF16) for i in range(2)]

    def isap(x):
        return isinstance(x, AP)

    def dma(q, out, in_, rk=(), wk=()):
        op = S.add(q, lambda e: e.dma_start(out=out, in_=in_), reads=[in_] + list(rk), writes=[out] + list(wk), dma=True)
        return op

    def mm(out, lhsT, rhs, start=True, stop=True):
        S.add("pe", lambda e: e.matmul(out, lhsT=lhsT, rhs=rhs, start=start, stop=stop), reads=[lhsT, rhs], writes=[out])

    def tr(out, in_, ident):
        S.add("pe", lambda e: e.transpose(out, in_, ident), reads=[in_, ident], writes=[out])

    def act(out, in_, func, scale=1.0, bias=0.0):
        rd = [in_] + [x for x in (scale, bias) if isap(x)]
        S.add("act", lambda e: e.activation(out=out, in_=in_, func=func, scale=scale, bias=bias), reads=rd, writes=[out])

    def tt(eng, out, in0, in1, op):
        S.add(eng, lambda e: e.tensor_tensor(out=out, in0=in0, in1=in1, op=op), reads=[in0, in1], writes=[out])

    def ts(eng, out, in0, s1, s2, op0, op1=None):
        rd = [in0] + [x for x in (s1, s2) if isap(x)]
        if op1 is None:
            S.add(eng, lambda e: e.tensor_scalar(out=out, in0=in0, scalar1=s1, scalar2=None, op0=op0), reads=rd, writes=[out])
        else:
            S.add(eng, lambda e: e.tensor_scalar(out=out, in0=in0, scalar1=s1, scalar2=s2, op0=op0, op1=op1), reads=rd, writes=[out])

    def stt(out, in0, scalar, in1, op0, op1):
        rd = [in0, in1] + ([scalar] if isap(scalar) else [])
        S.add("dve", lambda e: e.scalar_tensor_tensor(out=out, in0=in0, scalar=scalar, in1=in1, op0=op0, op1=op1), reads=rd, writes=[out])

    def cp(eng, out, in_):
        if eng == "act":
            act(out, in_, AF.Copy)
        else:
            S.add(eng, lambda e: e.tensor_copy(out=out, in_=in_), reads=[in_], writes=[out])

    def scan(out, d0, d1):
        S.add("dve", lambda e: e.tensor_tensor_scan(out=out, data0=d0, data1=d1, initial=0.0, op0=ALU.mult, op1=ALU.add),
              reads=[d0, d1], writes=[out])

    def rev(ap):
        (ps_, pc_), (st, cnt) = ap.ap
        return AP(ap.tensor, ap.offset + (cnt - 1) * st, [[ps_, pc_], [-st, cnt]])

    def bc(ap, dims, axis):
        return ap.unsqueeze(axis).to_broadcast(dims)

    pspools = {"s": [4, 5, 6, 7], "lo": [0, 1], "g0": [0, 1, 2, 3], "g1": [4, 5, 6, 7], "all": [0, 1, 2, 3, 4, 5, 6, 7]}
    psctr = {k: 0 for k in pspools}
    psctr["p"] = 0

    def PSB(pool="s", bf=False):
        lst = pspools[pool]
        b = lst[psctr[pool] % len(lst)]
        psctr[pool] += 1
        ap = psum[:, b * 512:(b + 1) * 512]
        return ap.bitcast(BF16) if bf else ap

    def PSP():
        p_ = psctr["p"] % 2
        psctr["p"] += 1
        return psum[:, p_ * 1024:(p_ + 1) * 1024]

    def dbgdump(name, ap):
        if dbg and name in dbg_out:
            outs_dma.append(dma("pool" if ap.dtype == BF16 else "sp", dbg_out[name], ap))

    class WStream:
        def __init__(self):
            self.seq = []
            for l in range(DEPTH):
                off = 0
                for nm, n in tile_order():
                    self.seq.append((ws[l], off, n, "L%d_%s" % (l, nm)))
                    off += n
            self.modseq = [(wmod, i * 8192, 8192, "mod%d" % i) for i in range(DEPTH * 24)]
            self.all = list(self.modseq[0:24])
            for ent in self.seq:
                self.all.append(ent)
                if ent[3] == "L0_gv0":
                    self.all += self.modseq[24:36]
                if ent[3] == "L0_gv1":
                    self.all += self.modseq[36:48]
            self.issued = 0
            self.taken = 0

        def _issue(self):
            if self.issued < len(self.all):
                src, off, n, nm = self.all[self.issued]
                buf = wbuf[self.issued % 2]
                dma("pool", buf[:, 0:n], src[:, off:off + n])
                self.issued += 1

        def next(self, name):
            while self.issued < min(self.taken + 2, len(self.all)):
                self._issue()
            src, off, n, nm = self.all[self.taken]
            assert nm == name, (nm, name)
            buf = wbuf[self.taken % 2]
            self.taken += 1
            return buf

        def prefetch(self):
            while self.issued < min(self.taken + 2, len(self.all)):
                self._issue()

    W = WStream()

    def stage(name):
        if stop is not None and name == stop:
            raise _Stop()

    try:
        c_f32 = CR.alloc([640])
        dma("sp", c_f32, consts[:, 0:640])
        blk64_f = c_f32[:, 0:128]
        triF_f = c_f32[:, 128:256]
        triB_f = c_f32[:, 256:384]
        sel_f = [c_f32[:, 384:512], c_f32[:, 512:640]]
        cbf = CR.alloc([1024], BF16)
        dma("pool", cbf, consts[:, 640:1664])
        ident_b = cbf[:, 0:128]
        hmaskF_b = cbf[:, 128:256]
        hmaskB_b = cbf[:, 256:384]
        ones_b = cbf[:, 384:512]
        I64_b = cbf[:, 512:576]
        mSL_b, mSU_b, mLi_b, mUi_b = (cbf[:, 576 + 64 * i: 640 + 64 * i] for i in range(4))
        nb16_b = cbf[:, 832:896]
        E1m_b = cbf[:, 896:960]
        E2m_b = cbf[:, 960:1024]
        vec_t = CR.alloc([512])
        dma("sp", vec_t, vecs)
        flags_t = CR.alloc([16])
        dma("sp", flags_t, flags)
        carry = flags_t[:, 0:1]
        cond_t = CR.alloc([16])
        dma("sp", cond_t, cond)
        rbuf = CR.alloc([T + 64], BF16)
        S.add("dve", lambda e: e.memset(rbuf, 1.0), writes=[rbuf])
        S.add("dve", lambda e: e.memset(rbuf.rearrange("p (c j) -> p c j", j=64)[:, :, 0:1], 0.0), writes=[rbuf])
        reset64 = rbuf[:, 0:T]
        reset63 = rbuf[:, 1:T + 1]
        mLR = CR.alloc([2, T], BF16)
        dma("pool", mLR[:, 0, :], cmask[0:1, :].partition_broadcast(P))
        dma("pool", mLR[:, 1, :], cmask[1:2, :].partition_broadcast(P))
        modv = [CR.alloc([96]) for _ in range(DEPTH)]
        gs1 = [CR.alloc([16]) for _ in range(DEPTH)]
        gs2 = [CR.alloc([16]) for _ in range(DEPTH)]
        lb1 = CR.alloc([16])
        oml1 = CR.alloc([16])
        noml1 = CR.alloc([16])
        zero16 = CR.alloc([16])
        one16 = CR.alloc([16])
        mone16 = CR.alloc([16])
        S.add("dve", lambda e: e.memset(zero16, 0.0), writes=[zero16])
        S.add("dve", lambda e: e.memset(one16, 1.0), writes=[one16])
        S.add("dve", lambda e: e.memset(mone16, -1.0), writes=[mone16])
        scond = CR.alloc([16], BF16)
        cw = CR.alloc([24, 9])
        rowbc = CR.alloc([64])
        negA = CR.alloc([16])
        wgab_b = CR.alloc([16, 32], BF16)
        wsT_b = CR.alloc([8, 128], BF16)
        ab_tm = CR.alloc([8, 32])
        g_tm = CR.alloc([8, 16])
        beta_tm = CR.alloc([8, 16])
        gc_tm = CR.alloc([8, 16])
        bg_tm = CR.alloc([8, 16])
        ekd_tm = CR.alloc([8, 16])
        sm_tmp = CR.alloc([8, 16])
        CR_MARK = CR.mark()

        tt("dve", lb1, vec_t[:, 288:304], vec_t[:, 272:288], ALU.subtract)
        act(lb1, lb1, AF.Sigmoid)
        ts("dve", oml1, lb1, -1.0, 1.0, ALU.mult, ALU.add)
        ts("dve", noml1, lb1, 1.0, -1.0, ALU.mult, ALU.add)

        act(scond, cond_t, AF.Silu)
        def mod_tile(lm, t_, pool="all"):
            pm = PSB(pool)
            wt = W.next("mod%d" % (lm * 24 + t_)).rearrange("p (k c) -> p k c", k=16)
            for nn in range(4):
                for kc in range(KC):
                    mm(pm[:, nn:nn + 1], wt[:, kc, nn * 128:(nn + 1) * 128], scond[:, kc:kc + 1], start=(kc == 0), stop=(kc == KC - 1))
            tt("dve", modv[lm][:, t_ * 4:t_ * 4 + 4], pm[:, 0:4], vec_t[:, 80 + 96 * lm + t_ * 4: 84 + 96 * lm + t_ * 4], ALU.add)

        def mod_finish(lm):
            stt(gs1[lm], modv[lm][:, 16:32], 1.0, vec_t[:, 16 * lm:16 * lm + 16], ALU.add, ALU.mult)
            stt(gs2[lm], modv[lm][:, 64:80], 1.0, vec_t[:, 32 + 16 * lm:48 + 16 * lm], ALU.add, ALU.mult)

        xs_early = V(OFF_AR, [16, T])
        xTv_early = xT.rearrange("(c p) t -> p c t", p=P)
        for c in range(16):
            dma("sp", xs_early[:, c, :], xTv_early[:, c, :])
        for t_ in range(24):
            mod_tile(0, t_)
        mod_finish(0)
        dbgdump("modv0", modv[0])
        stage("mod")

        def rms_rstd(src_chunks, nfeat, sq, rstd):
            pp = PSP()
            n = len(src_chunks)
            for c, xc in enumerate(src_chunks):
                s_ = sq[c % 2]
                act(s_, xc, AF.Square)
                for hf in range(2):
                    mm(pp[:, hf * 512:(hf + 1) * 512], ones_b, s_[:, hf * 512:(hf + 1) * 512], start=(c == 0), stop=(c == n - 1))
            act(rstd, pp, AF.Ln, scale=1.0 / nfeat, bias=EPS)
            act(rstd, rstd, AF.Exp, scale=-0.5)
            return rstd

        xs = V(OFF_AR, [16, T])

        def norm_mod(gs, sh, out_fn):
            MG.reset()
            sq = [MG.alloc([T], BF16) for _ in range(2)]
            rstd = rms_rstd([xs[:, c, :] for c in range(16)], D, sq, MG.alloc([T]))
            tmp = [MG.alloc([T]) for _ in range(2)]
            for c in range(16):
                t_ = tmp[c % 2]
                tt("dve", t_, xs[:, c, :], rstd, ALU.mult)
                out_fn(c, t_)

        xTv = xT.rearrange("(c p) t -> p c t", p=P)
        yTv = yT.rearrange("(c p) t -> p c t", p=P)
        xsv = xscr.rearrange("(c p) t -> p c t", p=P)

        for l in range(DEPTH):
            sh1 = modv[l][:, 0:16]
            gate1 = modv[l][:, 32:48]
            sh2 = modv[l][:, 48:64]
            gate2 = modv[l][:, 80:96]

            def out_h(c, t_, _gs=gs1[l], _sh=sh1):
                act(h_bf[:, c, :], t_, AF.Identity, scale=_gs[:, c:c + 1], bias=_sh[:, c:c + 1])
            norm_mod(gs1[l], sh1, out_h)
            if l > 0:
                for c in range(16):
                    dma("sp", xsv[:, c, :], xs[:, c, :], wk=[("xscr", c)])
            dbgdump("h%d" % l, h_bf[:, 0, :])
            stage("norm1_%d" % l)

            dma("sp", cw, convw[l])
            tt("dve", cw, cw, bc(flags_t[:, 1:10], [P, 24, 9], 1), ALU.mult)
            dma("sp", rowbc, rowc[l].partition_broadcast(P))
            act(negA, rowbc[:, 0:16], AF.Exp)
            ts("dve", negA, negA, -1.0, None, ALU.mult)
            dma("pool", wgab_b, wgab[l].rearrange("p (k c) -> p k c", k=16))
            dma("pool", wsT_b, cmwsT[l].rearrange("p (g q) -> p g q", g=8))
            if l == 0:
                lbv, omlv, nomlv = zero16, one16, mone16
            else:
                lbv, omlv, nomlv = lb1, oml1, noml1
            merged = V(OFF_MG, [16, T], BF16)

            def merge_branch(o_br, tag, first):
                sg = [AR_tmp_sig[0], AR_tmp_sig[1]]
                for j8 in range(8):
                    wt = W.next("L%d_m%s%d" % (l, tag, j8))
                    wg = wt[:, 0:4096].rearrange("p (k c) -> p k c", k=16)
                    wb = wt[:, 4096:6144].rearrange("p (k c) -> p k c", k=8)
                    for jj in range(2):
                        j = j8 * 2 + jj
                        pg = PSP()
                        for hf in range(2):
                            for kc in range(KC):
                                mm(pg[:, hf * 512:(hf + 1) * 512], wg[:, kc, jj * 128:(jj + 1) * 128], h_bf[:, kc, hf * 512:(hf + 1) * 512],
                                   start=(kc == 0), stop=(kc == KC - 1))
                        pb = PSP()
                        for hf in range(2):
                            for kc in range(8):
                                mm(pb[:, hf * 512:(hf + 1) * 512], wb[:, kc, jj * 128:(jj + 1) * 128], o_br[:, kc, hf * 512:(hf + 1) * 512],
                                   start=(kc == 0), stop=(kc == 7))
                        s_ = sg[j % 2]
                        act(s_, pg, AF.Sigmoid)
                        if first:
                            tt("dve", merged[:, j, :], pb, s_, ALU.mult)
                        else:
                            tt("dve", s_, pb, s_, ALU.mult)
                            tt("pool", merged[:, j, :], merged[:, j, :], s_, ALU.add)

            AR.reset()
            TM.reset()
            MG.reset()
            o_c = AR.alloc([8, T], BF16)
            AR_C0 = AR.mark()
            pab = PSB()
            pabv = pab[:, 0:256].rearrange("p (t c) -> p t c", t=8)
            for tt_ in range(8):
                for kc in range(KC):
                    mm(pabv[:, tt_, :], h_bf[:, kc, tt_ * 128:(tt_ + 1) * 128], wgab_b[:, kc, :], start=(kc == 0), stop=(kc == KC - 1))
            cp("dve", ab_tm, pabv)
            tt("dve", g_tm, ab_tm[:, :, 0:16], bc(rowbc[:, 16:32], [P, 8, 16], 1), ALU.add)
            act(g_tm, g_tm, AF.Exp)
            act(g_tm, g_tm, AF.Ln, bias=1.0)
            tt("dve", g_tm, g_tm, bc(negA, [P, 8, 16], 1), ALU.mult)
            act(beta_tm, ab_tm[:, :, 16:32], AF.Sigmoid)
            pgc = PSB()
            pgcv = pgc[:, 0:128].rearrange("p (t c) -> p t c", t=8)
            pgl = PSB()
            pglv = pgl[:, 0:128].rearrange("p (t c) -> p t c", t=8)
            for tt_ in range(8):
                mm(pgcv[:, tt_, 0:8], triF_f, g_tm[:, tt_, 0:8])
                mm(pgcv[:, tt_, 8:16], triB_f, g_tm[:, tt_, 8:16])
                mm(pglv[:, tt_, :], blk64_f, g_tm[:, tt_, :])
            cp("dve", gc_tm, pgcv)
            tt("dve", ekd_tm, pglv, gc_tm, ALU.subtract)
            act(ekd_tm, ekd_tm, AF.Exp)
            act(sm_tmp, gc_tm, AF.Exp)
            tt("dve", bg_tm, sm_tmp, beta_tm, ALU.mult)
            dbgdump("gc_tm", gc_tm.rearrange("p a b -> p (a b)"))
            dbgdump("beta_tm", beta_tm.rearrange("p a b -> p (a b)"))
            stage("gdn_tok%d" % l)

            for hg in range(2):
                AR.reset(AR_C0)
                TM.reset()
                MG.reset()
                kT = AR.alloc([4, T], BF16)
                qT = AR.alloc([4, T], BF16)
                opart = AR.alloc([4, T], BF16)
                Sst = AR.alloc([8, 128])
                Sbf = AR.alloc([8, 128], BF16)
                k_tm = TM.alloc([8, 512], BF16)
                v_tm = TM.alloc([8, 512], BF16)
                AR_C1 = AR.mark()
                csets = []
                for si in range(2):
                    reg = AR if si == 0 else MG
                    cs_ = {}
                    for nm in ("xc", "xL", "xR", "acc"):
                        cs_[nm] = reg.alloc([T])
                    cs_["vT"] = reg.alloc([T], BF16)
                    cs_["sq"] = MG.alloc([T], BF16)
                    cs_["rstd"] = MG.alloc([T])
                    csets.append(cs_)
                cchunks = [(part, h4) for part in range(3) for h4 in range(4)]
                wts_ = {}

                def conv_s1(j):
                    part, h4 = cchunks[j]
                    if part not in wts_:
                        wts_[part] = W.next("L%d_g%s%d" % (l, "qkv"[part], hg)).rearrange("p (k c) -> p k c", k=16)
                    wt = wts_[part]
                    cs_ = csets[j % 2]
                    pp = PSP()
                    for hf in range(2):
                        for kc in range(KC):
                            mm(pp[:, hf * 512:(hf + 1) * 512], wt[:, kc, h4 * 128:(h4 + 1) * 128], h_bf[:, kc, hf * 512:(hf + 1) * 512],
                               start=(kc == 0), stop=(kc == KC - 1))
                    cp("act", cs_["xc"], pp)
                    tt("pool", cs_["xL"], cs_["xc"], mLR[:, 0, :], ALU.mult)
                    tt("pool", cs_["xR"], cs_["xc"], mLR[:, 1, :], ALU.mult)

                def conv_s2(j):
                    part, h4 = cchunks[j]
                    cc = part * 8 + hg * 4 + h4
                    cs_ = csets[j % 2]
                    acc = cs_["acc"]
                    act(acc, cs_["xc"], AF.Identity, scale=cw[:, cc, 4:5])
                    for dr in range(3):
                        for dc in range(3):
                            if dr == 1 and dc == 1:
                                continue
                            off = (dr - 1) * 64 + (dc - 1)
                            src = (cs_["xL"], cs_["xc"], cs_["xR"])[dc]
                            a0 = max(0, -off)
                            a1 = min(T, T - off)
                            stt(acc[:, a0:a1], src[:, a0 + off:a1 + off], cw[:, cc, dr * 3 + dc: dr * 3 + dc + 1], acc[:, a0:a1], ALU.mult, ALU.add)

                def conv_s3(j):
                    part, h4 = cchunks[j]
                    cs_ = csets[j % 2]
                    acc = cs_["acc"]
                    if part == 2:
                        act(cs_["vT"], acc, AF.Silu)
                        srcT = cs_["vT"]
                        dst = v_tm
                    else:
                        act(acc, acc, AF.Silu)
                        rstd = rms_rstd([acc], 1.0, [cs_["sq"]], cs_["rstd"])
                        dstT = (qT, kT)[part]
                        if part == 0:
                            stt(dstT[:, h4, :], acc, 128.0 ** -0.5, rstd, ALU.mult, ALU.mult)
                        else:
                            tt("dve", dstT[:, h4, :], acc, rstd, ALU.mult)
                        srcT = kT[:, h4, :]
                        dst = k_tm
                    if part >= 1:
                        for half in range(2):
                            pt = PSB("s", bf=True)
                            for q4 in range(4):
                                tt_ = half * 4 + q4
                                tr(pt[:, q4 * 128:(q4 + 1) * 128], srcT[:, tt_ * 128:(tt_ + 1) * 128], ident_b)
                            cp("act", dst[:, half * 4:(half + 1) * 4, h4 * 128:(h4 + 1) * 128],
                               pt[:, 0:512].rearrange("p (a b) -> p a b", a=4))

                conv_s1(0)
                for j in range(12):
                    if j + 1 < 12:
                        conv_s1(j + 1)
                    conv_s2(j)
                    if j >= 1:
                        conv_s3(j - 1)
                conv_s3(11)
                if hg == 0:
                    dbgdump("kT", kT[:, 0, :])
                    dbgdump("qT", qT[:, 0, :])
                    dbgdump("v_tm", v_tm.rearrange("p a b -> p (a b)"))
                stage("gdn_conv%d_%d" % (l, hg))
                for d_ in range(2):
                    dma("sp", Sst[:, d_ * 4:(d_ + 1) * 4, :], s0gd[l][:, d_ * 8 + hg * 4: d_ * 8 + hg * 4 + 4, :])
                cp("act", Sbf, Sst)
                if l == 0 and hg == 0:
                    stage("gdn_st")
                AR.reset(AR_C1)
                MG.reset()
                NW = 256

                class TB:
                    pass
                sets = []
                def anyalloc(dims, dt=F32):
                    n = 1
                    for d__ in dims:
                        n *= d__
                    nb = (n * (4 if dt == F32 else 2) + 511) // 512 * 512
                    reg = MG if MG.cur + nb <= MG.base + MG.size else AR
                    return reg.alloc(dims, dt)
                for si in range(2):
                    b = TB()
                    b.pool = "g%d" % si
                    b.M = anyalloc([NW])
                    b.Dm = anyalloc([NW])
                    b.E2 = anyalloc([NW])
                    b.ER = [anyalloc([NW]) for _ in range(2)]
                    b.tmp = anyalloc([NW])
                    for nm in ("A", "AT", "P0", "P0T", "R", "RT", "E1a", "E1Ta", "E2a", "Pa", "PaT", "Pb", "PbT", "Y", "Z", "T2", "W2", "W3", "attnT"):
                        setattr(b, nm, anyalloc([NW], BF16))
                    b.vb = anyalloc([512], BF16)
                    b.kbg = anyalloc([512], BF16)
                    b.kdec = anyalloc([512], BF16)
                    b.negwT = anyalloc([512], BF16)
                    b.vnew = anyalloc([512], BF16)
                    sets.append(b)

                def v3(ap, a):
                    return ap.rearrange("p (a b) -> p a b", a=a)

                def mmblk(ps, lhsT_src, rhs_src):
                    for c2 in range(2):
                        pr = slice(c2 * 64, (c2 + 1) * 64)
                        for h4 in range(4):
                            cs = slice(h4 * 64, (h4 + 1) * 64)
                            mm(ps[pr, cs], lhsT_src[pr, cs], rhs_src[pr, cs])

                def gdn_prep(tt_, d_, b):
                    hs = slice(d_ * 8 + hg * 4, d_ * 8 + hg * 4 + 4)
                    gcs = gc_tm[:, tt_, hs]
                    bts = beta_tm[:, tt_, hs]
                    mA = (mSL_b, mSU_b)[d_]
                    mT = (mUi_b, mLi_b)[d_]
                    pk = PSB(b.pool)
                    pq = PSB(b.pool)
                    for c2 in range(2):
                        pr = slice(c2 * 64, (c2 + 1) * 64)
                        tok = slice(tt_ * 128 + c2 * 64, tt_ * 128 + c2 * 64 + 64)
                        for h4 in range(4):
                            cs = slice(h4 * 64, (h4 + 1) * 64)
                            mm(pk[pr, cs], kT[:, h4, tok], kT[:, h4, tok])
                            mm(pq[pr, cs], kT[:, h4, tok], qT[:, h4, tok])
                    first_ = (l == 0 and hg == 0 and tt_ == 0 and d_ == 0)
                    sec_ = (l == 0 and hg == 0 and tt_ == 7 and d_ == 1)
                    if first_:
                        stage("p_a")
                    if sec_:
                        stage("q_a")
                    tt("pool", v3(b.M, 4), bc(gcs, [P, 4, 64], 2), bc(I64_b, [P, 4, 64], 1), ALU.mult)
                    if first_:
                        stage("p_b")
                    if sec_:
                        stage("q_b")
                    pr_ = [PSB(b.pool), PSB(b.pool)]
                    for c2 in range(2):
                        prr = slice(c2 * 64, (c2 + 1) * 64)
                        mm(pr_[c2][:, 0:NW], sel_f[c2], b.M)
                        if first_ and c2 == 0:
                            stage("p_c")
                        if sec_:
                            stage("q_c%d" % c2)
                        act(b.ER[c2], pr_[c2][:, 0:NW], AF.Exp)
                        if first_ and c2 == 0:
                            stage("p_d")
                        if sec_:
                            stage("q_d%d" % c2)
                        tt("dve", v3(b.Dm[prr, :], 4), bc(gcs[prr, :], [64, 4, 64], 2), v3(pr_[c2][prr, 0:NW], 4), ALU.subtract)
                        if first_:
                            stage("p_e%d" % c2)
                        if sec_:
                            stage("q_e%d" % c2)
                    yield
                    ts("dve", b.E2, b.Dm, -1.0, 0.0, ALU.mult, ALU.min)
                    ts("dve", b.Dm, b.Dm, 0.0, None, ALU.min)
                    act(b.Dm, b.Dm, AF.Exp)
                    act(b.E2, b.E2, AF.Exp)
                    tt("pool", v3(b.Dm, 4), v3(b.Dm, 4), bc(mA, [P, 4, 64], 1), ALU.mult)
                    tt("pool", v3(b.Dm, 4), v3(b.Dm, 4), bc(bts, [P, 4, 64], 2), ALU.mult)
                    tt("dve", b.A, pk[:, 0:NW], b.Dm, ALU.mult)
                    tt("pool", v3(b.E2, 4), v3(b.E2, 4), bc(mT, [P, 4, 64], 1), ALU.mult)
                    tt("dve", b.attnT, pq[:, 0:NW], b.E2, ALU.mult)
                    pa = PSB(b.pool, bf=True)
                    for c2 in range(2):
                        prr = slice(c2 * 64, (c2 + 1) * 64)
                        for h4 in range(4):
                            cs = slice(h4 * 64, (h4 + 1) * 64)
                            tr(pa[prr, cs], b.A[prr, cs], ident_b[prr, prr])
                    cp("act", b.AT, pa[:, 0:NW])
                    yield
                    nb = bc(nb16_b, [P, 4, 64], 1)
                    tt("dve", v3(b.P0, 4), v3(b.A, 4), nb, ALU.mult)
                    tt("dve", v3(b.P0T, 4), v3(b.AT, 4), nb, ALU.mult)
                    tt("pool", v3(b.R, 4), v3(b.P0, 4), bc(I64_b, [P, 4, 64], 1), ALU.add)
                    tt("pool", v3(b.RT, 4), v3(b.P0T, 4), bc(I64_b, [P, 4, 64], 1), ALU.add)
                    tt("pool", v3(b.E1a, 4), v3(b.A, 4), bc(E1m_b, [P, 4, 64], 1), ALU.mult)
                    tt("pool", v3(b.E1Ta, 4), v3(b.AT, 4), bc(E1m_b, [P, 4, 64], 1), ALU.mult)
                    tt("pool", v3(b.E2a, 4), v3(b.A, 4), bc(E2m_b, [P, 4, 64], 1), ALU.mult)
                    tt("pool", v3(b.vb, 4), v3(v_tm[:, tt_, :], 4), bc(bts, [P, 4, 128], 2), ALU.mult)
                    tt("pool", v3(b.kbg, 4), v3(k_tm[:, tt_, :], 4), bc(bg_tm[:, tt_, hs], [P, 4, 128], 2), ALU.mult)
                    tt("pool", v3(b.kdec, 4), v3(k_tm[:, tt_, :], 4), bc(ekd_tm[:, tt_, hs], [P, 4, 128], 2), ALU.mult)
                    Ps = [(b.P0, b.P0T), (b.Pa, b.PaT), (b.Pb, b.PbT), (b.Y, b.Z)]
                    for rnd_ in range(4):
                        Pc, PcT = Ps[rnd_]
                        if rnd_ < 3:
                            Pn, PnT = Ps[rnd_ + 1]
                            p1 = PSB(b.pool)
                            p2 = PSB(b.pool)
                            mmblk(p1, PcT, Pc)
                            mmblk(p2, Pc, PcT)
                        if rnd_ >= 1:
                            p3 = PSB(b.pool)
                            p4 = PSB(b.pool)
                            mmblk(p3, PcT, b.R)
                            mmblk(p4, Pc, b.RT)
                        if rnd_ < 3:
                            cp("act", Pn, p1[:, 0:NW])
                            cp("act", PnT, p2[:, 0:NW])
                        if rnd_ >= 1:
                            tt("dve", b.R, p3[:, 0:NW], b.R, ALU.add)
                            tt("dve", b.RT, p4[:, 0:NW], b.RT, ALU.add)
                        yield
                    p1 = PSB(b.pool)
                    p2 = PSB(b.pool)
                    mmblk(p1, b.E1Ta, b.R)
                    mmblk(p2, b.E1a, b.RT)
                    cp("act", b.Y, p1[:, 0:NW])
                    cp("dve", b.Z, p2[:, 0:NW])
                    yield
                    p3 = PSB(b.pool)
                    p4 = PSB(b.pool)
                    mmblk(p3, b.RT, b.Y)
                    mmblk(p4, b.R, b.Z)
                    tt("dve", b.T2, b.R, p3[:, 0:NW], ALU.subtract)
                    tt("dve", b.W2, b.RT, p4[:, 0:NW], ALU.subtract)
                    yield
                    p1 = PSB(b.pool)
                    mmblk(p1, b.E2a, b.W2)
                    cp("act", b.Z, p1[:, 0:NW])
                    yield
                    p3 = PSB(b.pool)
                    mmblk(p3, b.T2, b.Z)
                    tt("dve", b.W3, b.W2, p3[:, 0:NW], ALU.subtract)
                    yield
                    for c2 in range(2):
                        prr = slice(c2 * 64, (c2 + 1) * 64)
                        pw = PSB(b.pool)
                        for h4 in range(4):
                            mm(pw[:, h4 * 64:(h4 + 1) * 64], b.kbg[prr, h4 * 128:(h4 + 1) * 128], b.W3[prr, h4 * 64:(h4 + 1) * 64])
                        act(b.negwT[:, c2 * 256:(c2 + 1) * 256], pw[:, 0:256], AF.Identity, scale=-1.0)
                    yield

                def gdn_state(tt_, d_, b):
                    order = (0, 1) if d_ == 0 else (1, 0)
                    for c2 in order:
                        c = tt_ * 2 + c2
                        prr = slice(c2 * 64, (c2 + 1) * 64)
                        tok = slice(c * 64, c * 64 + 64)
                        pv = PSB(b.pool)
                        for h4 in range(4):
                            cs = slice(h4 * 128, (h4 + 1) * 128)
                            mm(pv[prr, cs], b.W3[prr, h4 * 64:(h4 + 1) * 64], b.vb[prr, cs], start=True, stop=False)
                            mm(pv[prr, cs], b.negwT[:, c2 * 256 + h4 * 64: c2 * 256 + (h4 + 1) * 64], Sbf[:, d_ * 4 + h4, :], start=False, stop=True)
                        cp("act", b.vnew[prr, :], pv[prr, :])
                        yield
                        poi = PSB(b.pool)
                        poa = PSB(b.pool)
                        pds = PSB(b.pool)
                        for h4 in range(4):
                            cs = slice(h4 * 128, (h4 + 1) * 128)
                            mm(pds[:, cs], b.kdec[prr, cs], b.vnew[prr, cs])
                        for h4 in range(4):
                            cs = slice(h4 * 128, (h4 + 1) * 128)
                            c64 = slice(h4 * 64, (h4 + 1) * 64)
                            mm(poi[:, c64], Sbf[:, d_ * 4 + h4, :], qT[:, h4, tok])
                            mm(poa[:, c64], b.vnew[prr, cs], b.attnT[prr, c64])
                        Sd = Sst[:, d_ * 4:(d_ + 1) * 4, :]
                        jl = 63 if d_ == 0 else 0
                        egl = v3(b.ER[c2], 4)[:, :, jl:jl + 1].to_broadcast([P, 4, 128])
                        tt("pool", Sd, Sd, egl, ALU.mult)
                        last = (c == 15) if d_ == 0 else (c == 0)
                        segend = (c % 4 == 3) if d_ == 0 else (c % 4 == 0)
                        if not segend:
                            tt("dve", Sbf[:, d_ * 4:(d_ + 1) * 4, :], Sd, v3(pds, 4), ALU.add)
                            tt("dve", Sd, Sd, v3(pds, 4), ALU.add)
                        else:
                            tt("dve", Sd, Sd, v3(pds, 4), ALU.add)
                            seg = c // 4
                            outs_dma.append(dma("sp", nsgd[seg, l, d_, hg * 4:hg * 4 + 4].rearrange("h p v -> p h v"), Sd))
                            if not last:
                                ts("dve", Sd, Sd, carry, None, ALU.mult)
                                cp("act", Sbf[:, d_ * 4:(d_ + 1) * 4, :], Sd)
                        yield
                        tt("dve", b.tmp, poi[:, 0:NW], b.ER[c2], ALU.mult)
                        first = (tt_ <= 3) if d_ == 0 else (tt_ >= 4)
                        ov = opart[:, :, tok]
                        if first:
                            tt("dve", ov, v3(b.tmp, 4), v3(poa[:, 0:NW], 4), ALU.add)
                        else:
                            tt("dve", b.tmp, b.tmp, poa[:, 0:NW], ALU.add)
                            tt("pool", ov, ov, v3(b.tmp, 4), ALU.add)

                def run_gens(gens):
                    gens = list(gens)
                    if stop == "only_g2":
                        gens = gens[1:]
                    if stop == "only_g1":
                        gens = gens[:1]
                    rnd = 0
                    while gens:
                        rnd += 1
                        if l == 0 and hg == 0:
                            stage("gp_r%d" % rnd)
                            if rnd == 2 and stop in ("only_g1", "only_g2"):
                                raise _Stop()
                        nxt = []
                        for g_ in gens:
                            try:
                                next(g_)
                                nxt.append(g_)
                            except StopIteration:
                                pass
                        gens = nxt

                for i in range(8):
                    run_gens([gdn_prep(i, 0, sets[0]), gdn_prep(7 - i, 1, sets[1])])
                    if l == 0 and hg == 0 and i == 0:
                        dbgdump("A0", sets[0].A)
                        dbgdump("W30", sets[0].W3)
                        dbgdump("attnT0", sets[0].attnT)
                    if l == 0 and hg == 0 and i == 0:
                        stage("gp0")
                    run_gens([gdn_state(i, 0, sets[0]), gdn_state(7 - i, 1, sets[1])])
                    if l == 0 and hg == 0 and i == 0:
                        stage("gs0b")
                    if l == 0:
                        for k_ in range(2 if i % 2 == 0 else 1):
                            mod_tile(1, 24 + 0 * 0 + hg * 12 + (i // 2) * 3 + (i % 2) * 2 + k_ - 24)
                        if hg == 1 and i == 7:
                            mod_finish(1)
                if hg == 0:
                    dbgdump("opart", opart[:, 0, :])
                stage("gdn_loop%d_%d" % (l, hg))
                AR.reset(AR_C1)
                MG.reset()
                wt = W.next("L%d_ggo%d" % (l, hg)).rearrange("p (k c) -> p k c", k=16)
                go = [AR.alloc([T], BF16) for _ in range(2)]
                on = [AR.alloc([T]) for _ in range(2)]
                for h4 in range(4):
                    pp = PSP()
                    for hf in range(2):
                        for kc in range(KC):
                            mm(pp[:, hf * 512:(hf + 1) * 512], wt[:, kc, h4 * 128:(h4 + 1) * 128], h_bf[:, kc, hf * 512:(hf + 1) * 512],
                               start=(kc == 0), stop=(kc == KC - 1))
                    act(go[h4 % 2], pp, AF.Silu)
                    MG.reset()
                    rstd = rms_rstd([opart[:, h4, :]], 128.0, [MG.alloc([T], BF16)], MG.alloc([T]))
                    stt(on[h4 % 2], opart[:, h4, :], vec_t[:, 306 + l:307 + l], rstd, ALU.mult, ALU.mult)
                    tt("dve", o_c[:, hg * 4 + h4, :], on[h4 % 2], go[h4 % 2], ALU.mult)
            dbgdump("o_c", o_c[:, 0, :])
            stage("gdn_fin%d" % l)
            AR.reset(AR_C0)
            AR_tmp_sig = [AR.alloc([T], BF16) for _ in range(2)]
            merge_branch(o_c, "C", True)
            dbgdump("mergedC", merged[:, 0, :])
            stage("mergeC%d" % l)

            AR.reset()
            TM.reset()
            o_a = AR.alloc([8, T], BF16)
            v_tm = TM.alloc([8, T], BF16)
            for cg in range(2):
                wt = W.next("L%d_hi%d" % (l, cg)).rearrange("p (k c) -> p k c", k=16)
                for tt_ in range(8):
                    pp = PSB()
                    for kc in range(KC):
                        mm(pp, h_bf[:, kc, tt_ * 128:(tt_ + 1) * 128], wt[:, kc, :], start=(kc == 0), stop=(kc == KC - 1))
                    cp("act" if tt_ % 2 else "dve", v_tm[:, tt_, cg * 512:(cg + 1) * 512], pp)
            AR_A0 = AR.mark()
            for hp in range(4):
                AR.reset(AR_A0)
                q_b = AR.alloc([2, T], BF16)
                go_b = AR.alloc([2, T], BF16)
                oacc = AR.alloc([2, T])
                wA = W.next("L%d_hA%d" % (l, hp)).rearrange("p (k c) -> p k c", k=16)
                for i in range(4):
                    pp = PSP()
                    for hf in range(2):
                        for kc in range(KC):
                            mm(pp[:, hf * 512:(hf + 1) * 512], wA[:, kc, i * 128:(i + 1) * 128], h_bf[:, kc, hf * 512:(hf + 1) * 512],
                               start=(kc == 0), stop=(kc == KC - 1))
                    act((q_b, go_b)[i // 2][:, i % 2, :], pp, AF.Silu)
                wB = W.next("L%d_hB%d" % (l, hp)).rearrange("p (k c) -> p k c", k=16)
                sig = AR.alloc([T])
                lf = AR.alloc([T])
                kk = AR.alloc([T])
                Bc = AR.alloc([T])
                t1, t2 = sig, lf
                sb = {}
                for nm in ("q16", "kl0", "kl1", "qd", "kds", "sT"):
                    sb[nm] = AR.alloc([T], BF16)
                S.add("pool", (lambda ap_: lambda e: e.memset(ap_, 0.0))(sb["sT"]), writes=[sb["sT"]])
                sb["kds_tm"] = AR.alloc([8, 128], BF16)
                sb["eBl"] = AR.alloc([16])
                sb["S"] = AR.alloc([128])
                sb["Sbf"] = AR.alloc([128], BF16)
                sb["Sbf2"] = AR.alloc([128], BF16)
                gate_pp = psum[:, 2 * 512:4 * 512]

                def gate_proj_part(i_, part_):
                    for idx_ in range(part_ * 8, part_ * 8 + 8):
                        hf, kc = idx_ // KC, idx_ % KC
                        mm(gate_pp[:, hf * 512:(hf + 1) * 512], wB[:, kc, i_ * 128:(i_ + 1) * 128], h_bf[:, kc, hf * 512:(hf + 1) * 512],
                           start=(kc == 0), stop=(kc == KC - 1))
                for part_ in range(4):
                    gate_proj_part(0, part_)
                for i in range(4):
                    d_ = i // 2
                    hh = i % 2
                    h_ = hp * 2 + hh
                    act(sig, gate_pp, AF.Sigmoid)
                    lbc = lbv[:, d_ * 8 + h_: d_ * 8 + h_ + 1]
                    omc = omlv[:, d_ * 8 + h_: d_ * 8 + h_ + 1]
                    nomc = nomlv[:, d_ * 8 + h_: d_ * 8 + h_ + 1]
                    ts("dve", lf, sig, omc, lbc, ALU.mult, ALU.add)
                    act(lf, lf, AF.Ln)
                    ts("dve", kk, sig, nomc, omc, ALU.mult, ALU.add)
                    if d_ == 0:
                        scan(Bc, reset64, lf)
                    else:
                        scan(rev(Bc), rev(reset63), rev(lf))
                    Bv = Bc.rearrange("p (c j) -> p c j", j=64)
                    jl = 63 if d_ == 0 else 0
                    qh = q_b[:, hh, :]
                    B4 = Bc.rearrange("p (c i j) -> p c i j", i=4, j=16)
                    t14 = t1.rearrange("p (c i j) -> p c i j", i=4, j=16)
                    if d_ == 0:
                        cp("dve", t14[:, :, 0, :], B4[:, :, 0, :])
                        tt("dve", t14[:, :, 1:4, :], B4[:, :, 1:4, :], B4[:, :, 0:3, 15:16].to_broadcast([P, 16, 3, 16]), ALU.subtract)
                    else:
                        cp("dve", t14[:, :, 3, :], B4[:, :, 3, :])
                        tt("dve", t14[:, :, 0:3, :], B4[:, :, 0:3, :], B4[:, :, 1:4, 0:1].to_broadcast([P, 16, 3, 16]), ALU.subtract)
                    act(t1, t1, AF.Exp)
                    tt("dve", sb["q16"], qh, t1, ALU.mult)
                    tt("dve", t2.rearrange("p (c j) -> p c j", j=64), Bv[:, :, jl:jl + 1].to_broadcast([P, 16, 64]), Bv, ALU.subtract)
                    act(sb["eBl"], Bv[:, :, jl], AF.Exp)
                    act(t2, t2, AF.Exp)
                    tt("dve", sb["kds"], kk, t2, ALU.mult)
                    act(t1, Bc, AF.Exp)
                    tt("dve", sb["qd"], qh, t1, ALU.mult)
                    for half in range(2):
                        pt = PSB("s", bf=True)
                        for q4 in range(4):
                            tt_ = half * 4 + q4
                            tr(pt[:, q4 * 128:(q4 + 1) * 128], sb["kds"][:, tt_ * 128:(tt_ + 1) * 128], ident_b)
                        cp("act", sb["kds_tm"][:, half * 4:(half + 1) * 4, :], pt[:, 0:512].rearrange("p (a b) -> p a b", a=4))
                    hm = (hmaskF_b, hmaskB_b)[d_]
                    pts = [PSB("lo"), PSB("lo")]
                    kk3 = kk.rearrange("p (c j) -> p c j", j=64)
                    t13s = [t1.rearrange("p (c j) -> p c j", j=64), t2.rearrange("p (c j) -> p c j", j=64)]
                    for ip, I in enumerate((0, 1, 2, 3) if d_ == 0 else (3, 2, 1, 0)):
                        t13 = t13s[ip % 2]
                        kl = sb["kl%d" % (ip % 2)]
                        kl3 = kl.rearrange("p (c j) -> p c j", j=64)
                        if ip < 2:
                            S.add("pool", (lambda ap_: lambda e: e.memset(ap_, 0.0))(kl), writes=[kl])
                        rng_ = slice(0, 16 * (I + 1)) if d_ == 0 else slice(16 * I, 64)
                        n_ = rng_.stop - rng_.start
                        if d_ == 0:
                            if I == 0:
                                ts("dve", t13[:, :, rng_], Bv[:, :, rng_], -1.0, None, ALU.mult)
                            else:
                                tt("dve", t13[:, :, rng_], Bv[:, :, 16 * I - 1:16 * I].to_broadcast([P, 16, n_]), Bv[:, :, rng_], ALU.subtract)
                        else:
                            if I == 3:
                                ts("dve", t13[:, :, rng_], Bv[:, :, rng_], -1.0, None, ALU.mult)
                            else:
                                tt("dve", t13[:, :, rng_], Bv[:, :, 16 * (I + 1):16 * (I + 1) + 1].to_broadcast([P, 16, n_]), Bv[:, :, rng_], ALU.subtract)
                        act(t13[:, :, rng_], t13[:, :, rng_], AF.Exp)
                        tt("dve", kl3[:, :, rng_], kk3[:, :, rng_], t13[:, :, rng_], ALU.mult)
                        for tb in range(8):
                            for c2 in range(2):
                                c = tb * 2 + c2
                                col = (tb % 4) * 128 + c2 * 64 + I * 16
                                mm(pts[tb // 4][c2 * 64:(c2 + 1) * 64, col:col + 16], kl[:, c * 64:(c + 1) * 64], sb["q16"][:, c * 64 + I * 16: c * 64 + I * 16 + 16])
                    for half in range(2):
                        for c2 in range(2):
                            prr = slice(c2 * 64, (c2 + 1) * 64)
                            cs_ = slice(c2 * 64, (c2 + 1) * 64)
                            tt("dve", sb["sT"][prr, half * 512:(half + 1) * 512].rearrange("p (a b) -> p a b", a=4)[:, :, cs_],
                               pts[half][prr, :].rearrange("p (a b) -> p a b", a=4)[:, :, cs_], bc(hm[prr, cs_], [64, 4, 64], 1), ALU.mult)
                    dma("sp", sb["S"], s0hg[l][:, d_ * 8 + h_, :])
                    sring = [sb["Sbf"], sb["Sbf2"]]
                    cur = 0
                    cp("act", sring[cur], sb["S"])
                    pdb = [PSB("s") for _ in range(4)]
                    def pdt(c):
                        tb_, c2_ = c // 2, c % 2
                        return pdb[c2_ * 2 + tb_ // 4][:, (tb_ % 4) * 128:(tb_ % 4 + 1) * 128]
                    for c in range(16):
                        tb, c2 = c // 2, c % 2
                        prr = slice(c2 * 64, (c2 + 1) * 64)
                        mm(pdt(c), sb["kds_tm"][prr, tb, :], v_tm[prr, tb, h_ * 128:(h_ + 1) * 128])
                    tbs = range(8) if d_ == 0 else range(7, -1, -1)
                    po = None
                    for n_, tb in enumerate(tbs):
                        if i < 3 and n_ % 2 == 0:
                            gate_proj_part(i + 1, n_ // 2)
                        if n_ % 4 == 0:
                            po = PSB("lo")
                            hfidx = tb // 4
                        q4 = tb % 4
                        blk = slice(tb * 128, (tb + 1) * 128)
                        mm(po[:, q4 * 128:(q4 + 1) * 128], v_tm[:, tb, h_ * 128:(h_ + 1) * 128], sb["sT"][:, blk], start=True, stop=False)
                        c2s = (0, 1) if d_ == 0 else (1, 0)
                        for ci, c2 in enumerate(c2s):
                            c = tb * 2 + c2
                            tok = slice(c * 64, (c + 1) * 64)
                            mm(po[:, q4 * 128 + c2 * 64: q4 * 128 + c2 * 64 + 64], sring[cur], sb["qd"][:, tok], start=False, stop=(ci == 1))
                            pd = pdt(c)
                            last = (c == 15) if d_ == 0 else (c == 0)
                            segend = (c % 4 == 3) if d_ == 0 else (c % 4 == 0)
                            if not segend:
                                stt(sb["S"], sb["S"], sb["eBl"][:, c:c + 1], pd, ALU.mult, ALU.add)
                                cp("act", sring[1 - cur], sb["S"])
                                cur = 1 - cur
                            else:
                                stt(sb["S"], sb["S"], sb["eBl"][:, c:c + 1], pd, ALU.mult, ALU.add)
                                outs_dma.append(dma("sp", nshg[c // 4, l, d_, h_], sb["S"]))
                                if not last:
                                    ts("dve", sb["S"], sb["S"], carry, None, ALU.mult)
                                    cp("act", sring[1 - cur], sb["S"])
                                    cur = 1 - cur
                        if n_ % 4 == 3:
                            ov = oacc[:, hh, hfidx * 512:(hfidx + 1) * 512]
                            if d_ == 0:
                                cp("act", ov, po)
                            else:
                                tt("dve", ov, ov, po, ALU.add)
                if hp == 0:
                    dbgdump("oacc", oacc[:, 0, :])
                sqb = [kk[:, 0:512].bitcast(BF16), kk[:, 512:1024].bitcast(BF16)]
                for hh in range(2):
                    rstd = rms_rstd([oacc[:, hh, :]], 128.0, sqb, Bc)
                    stt(t1, oacc[:, hh, :], vec_t[:, 304 + l:305 + l], rstd, ALU.mult, ALU.mult)
                    tt("dve", o_a[:, hp * 2 + hh, :], t1, go_b[:, hh, :], ALU.mult)
            dbgdump("o_a", o_a[:, 0, :])
            stage("hgrn%d" % l)
            AR.reset(AR_A0)
            AR_tmp_sig = [AR.alloc([T], BF16) for _ in range(2)]
            merge_branch(o_a, "A", False)

            AR.reset()
            TM.reset()
            o_b = AR.alloc([8, T], BF16)
            u_b = AR.alloc([8, T], BF16)
            bsb = AR.alloc([T])
            dma("sp", bsb, cmbs[l].partition_broadcast(P))
            vn_tm = TM.alloc([8, T], BF16)
            for cg in range(2):
                wt = W.next("L%d_cu%d" % (l, cg)).rearrange("p (k c) -> p k c", k=16)
                for j in range(4):
                    pp = PSP()
                    for hf in range(2):
                        for kc in range(KC):
                            mm(pp[:, hf * 512:(hf + 1) * 512], wt[:, kc, j * 128:(j + 1) * 128], h_bf[:, kc, hf * 512:(hf + 1) * 512],
                               start=(kc == 0), stop=(kc == KC - 1))
                    act(u_b[:, cg * 4 + j, :], pp, AF.Gelu)
            gv = [AR.alloc([512]) for _ in range(2)]
            sqv = AR.alloc([512])
            ssv = AR.alloc([8, 8])
            for cg in range(2):
                wt = W.next("L%d_cv%d" % (l, cg)).rearrange("p (k c) -> p k c", k=16)
                for tt_ in range(8):
                    pp = PSB()
                    for kc in range(KC):
                        mm(pp, h_bf[:, kc, tt_ * 128:(tt_ + 1) * 128], wt[:, kc, :], start=(kc == 0), stop=(kc == KC - 1))
                    g_ = gv[tt_ % 2]
                    act(g_, pp, AF.Gelu)
                    act(sqv, g_, AF.Square)
                    ssl = ssv[:, tt_, cg * 4:(cg + 1) * 4]
                    S.add("dve", (lambda ssl=ssl: lambda e: e.reduce_sum(out=ssl, in_=sqv.rearrange("p (g c) -> p g c", g=4), axis=mybir.AxisListType.X))(),
                          reads=[sqv], writes=[ssl])
                    act(ssl, ssl, AF.Ln, scale=1.0 / 128.0, bias=EPS)
                    act(ssl, ssl, AF.Exp, scale=-0.5)
                    tt("dve", vn_tm[:, tt_, cg * 512:(cg + 1) * 512].rearrange("p (g c) -> p g c", g=4), g_.rearrange("p (g c) -> p g c", g=4),
                       bc(ssl, [P, 4, 128], 2), ALU.mult)
            for g in range(8):
                for half in range(2):
                    pt = PSB()
                    for q4 in range(4):
                        tt_ = half * 4 + q4
                        mm(pt[:, q4 * 128:(q4 + 1) * 128], vn_tm[:, tt_, g * 128:(g + 1) * 128], wsT_b[:, g, :])
                    s_ = gv[half]
                    stt(s_.rearrange("p (a b) -> p a b", a=4), pt.rearrange("p (a b) -> p a b", a=4), vec_t[:, 308 + 8 * l + g: 309 + 8 * l + g],
                        bc(bsb[:, g * 128:(g + 1) * 128], [P, 4, 128], 1), ALU.mult, ALU.add)
                    tt("dve", o_b[:, g, half * 512:(half + 1) * 512], s_, u_b[:, g, half * 512:(half + 1) * 512], ALU.mult)
            dbgdump("o_b", o_b[:, 0, :])
            stage("gmlp%d" % l)
            AR_tmp_sig = [AR.alloc([T], BF16) for _ in range(2)]
            merge_branch(o_b, "B", False)
            dbgdump("merged", merged[:, 0, :])
            stage("merged%d" % l)

            for j4 in range(4):
                wt = W.next("L%d_wo%d" % (l, j4)).rearrange("p (k c) -> p k c", k=16)
                for jj in range(4):
                    j = j4 * 4 + jj
                    if l == 0:
                        dma("sp", xs[:, j, :], xTv[:, j, :])
                    else:
                        dma("sp", xs[:, j, :], xsv[:, j, :], rk=[("xscr", j)])
                    pp = PSP()
                    for hf in range(2):
                        for kc in range(KC):
                            mm(pp[:, hf * 512:(hf + 1) * 512], wt[:, kc, jj * 128:(jj + 1) * 128], merged[:, kc, hf * 512:(hf + 1) * 512],
                               start=(kc == 0), stop=(kc == KC - 1))
                    stt(xs[:, j, :], pp, gate1[:, j:j + 1], xs[:, j, :], ALU.mult, ALU.add)
            dbgdump("x_mid%d" % l, xs[:, 0, :])
            stage("xmid%d" % l)

            def out_h2(c, t_, _gs=gs2[l], _sh=sh2):
                act(h_bf[:, c, :], t_, AF.Identity, scale=_gs[:, c:c + 1], bias=_sh[:, c:c + 1])
            norm_mod(gs2[l], sh2, out_h2)
            a_b = V(OFF_MG, [16, T], BF16)
            sqf = [V(OFF_TM + i * 2048, [T], BF16) for i in range(2)]
            for g in range(4):
                for c4 in range(4):
                    wt = W.next("L%d_f1_%d_%d" % (l, g, c4)).rearrange("p (k c) -> p k c", k=16)
                    for jj in range(4):
                        pp = PSP()
                        for hf in range(2):
                            for kc in range(KC):
                                mm(pp[:, hf * 512:(hf + 1) * 512], wt[:, kc, jj * 128:(jj + 1) * 128], h_bf[:, kc, hf * 512:(hf + 1) * 512],
                                   start=(kc == 0), stop=(kc == KC - 1))
                        act(sqf[jj % 2], pp, AF.Square)
                        stt(a_b[:, c4 * 4 + jj, :], pp, 0.0, sqf[jj % 2], ALU.is_gt, ALU.mult)
                for j4 in range(4):
                    wt = W.next("L%d_f2_%d_%d" % (l, g, j4)).rearrange("p (k c) -> p k c", k=16)
                    for jj in range(4):
                        j = j4 * 4 + jj
                        pp = PSP()
                        for hf in range(2):
                            for kc in range(KC):
                                mm(pp[:, hf * 512:(hf + 1) * 512], wt[:, kc, jj * 128:(jj + 1) * 128], a_b[:, kc, hf * 512:(hf + 1) * 512],
                                   start=(kc == 0), stop=(kc == KC - 1))
                        stt(xs[:, j, :], pp, gate2[:, j:j + 1], xs[:, j, :], ALU.mult, ALU.add)
            dbgdump("x_out%d" % l, xs[:, 0, :])
            stage("xout%d" % l)

        yb = [V(OFF_H + i * 4096, [T]) for i in range(2)]

        def out_y(c, t_):
            act(yb[c % 2], t_, AF.Identity, scale=vec_t[:, 64 + c:65 + c])
            outs_dma.append(dma("sp", yTv[:, c, :], yb[c % 2]))
        norm_mod(None, None, out_y)
    except _Stop:
        pass

    S.emit(final_waits=outs_dma)
    es.close()
    return nc


def make_consts():
    c = np.zeros((P, 2048), np.float32)
    idx = np.arange(128)
    same64 = (idx[:, None] // 64) == (idx[None, :] // 64)
    c[:, 0:128] = same64
    c[:, 128:256] = same64 & (idx[:, None] <= idx[None, :])
    c[:, 256:384] = same64 & (idx[:, None] >= idx[None, :])
    c[0:64, 384:512] = 1.0
    c[64:128, 512:640] = 1.0
    o = 640
    c[:, o:o + 128] = np.eye(128)
    c[:, o + 128:o + 256] = same64 & (idx[:, None] <= idx[None, :])
    c[:, o + 256:o + 384] = same64 & (idx[:, None] >= idx[None, :])
    c[:, o + 384:o + 512] = 1.0
    i64 = np.arange(64)
    p64 = idx % 64
    m = o + 512
    c[:, m:m + 64] = (p64[:, None] == i64[None, :])
    c[:, m + 64:m + 128] = (p64[:, None] > i64[None, :])
    c[:, m + 128:m + 192] = (p64[:, None] < i64[None, :])
    c[:, m + 192:m + 256] = (p64[:, None] >= i64[None, :])
    c[:, m + 256:m + 320] = (p64[:, None] <= i64[None, :])
    b16 = (p64[:, None] // 16) == (i64[None, :] // 16)
    b32 = (p64[:, None] // 32) == (i64[None, :] // 32)
    c[:, m + 320:m + 384] = -1.0 * b16
    c[:, m + 384:m + 448] = b32 & ~b16
    c[:, m + 448:m + 512] = ~b32
    return c


_CACHE = {}


def _get_program(dbg=None):
    key = None if not dbg else tuple(sorted(dbg.items()))
    if key not in _CACHE:
        _CACHE[key] = build_program(dbg)
    return _CACHE[key]


def kernel(x_prompt, x_sample, c, state_hgrn, state_gdn, c_ctx, norm1_g, norm2_g, w_mod, b_mod, w_in,
           hg_lb, hg_onorm_g, cm_vnorm_g, cm_ws, cm_bs, gdn_conv, gdn_A_log, gdn_dt_bias, gdn_onorm_g,
           w_br_hg, w_br_cm, w_br_gdn, w_out, w_ff1, w_ff2, final_g, _dbg=None, _stop=None, _cores=None):
    f32 = np.float32
    A = lambda t: np.asarray(t, f32)
    x_prompt, x_sample, c, state_hgrn, state_gdn, c_ctx = map(A, (x_prompt, x_sample, c, state_hgrn, state_gdn, c_ctx))
    w_mod, b_mod, w_in = A(w_mod), A(b_mod), A(w_in)
    wsl = [build_wstream(w_in[l], A(w_br_hg)[l], A(w_br_cm)[l], A(w_br_gdn)[l], A(w_out)[l], A(w_ff1)[l], A(w_ff2)[l])
           for l in range(DEPTH)]
    parts = []
    for l in range(DEPTH):
        for t_ in range(24):
            for kc in range(16):
                parts.append(w_mod[l, kc * 128:(kc + 1) * 128, t_ * 512:(t_ + 1) * 512])
    wmod_h = np.ascontiguousarray(np.concatenate(parts, axis=1))
    wgab_h = np.ascontiguousarray(np.stack([
        np.concatenate([w_in[l, kc * 128:(kc + 1) * 128, C_GA:C_GA + 32] for kc in range(16)], axis=1) for l in range(DEPTH)]))
    vecs_h = np.zeros((P, 512), f32)
    for l in range(DEPTH):
        vecs_h[:, 16 * l:16 * l + 16] = fm(A(norm1_g)[l], 16)
        vecs_h[:, 32 + 16 * l:48 + 16 * l] = fm(A(norm2_g)[l], 16)
        vecs_h[:, 80 + 96 * l:176 + 96 * l] = fm(b_mod[l], 96)
        vecs_h[:, 272 + 16 * l:288 + 16 * l] = fm(A(hg_lb)[l].reshape(-1), 16)
        vecs_h[:, 304 + l] = A(hg_onorm_g)[l]
        vecs_h[:, 306 + l] = A(gdn_onorm_g)[l]
        vecs_h[:, 308 + 8 * l:316 + 8 * l] = fm(A(cm_vnorm_g)[l], 8)
    vecs_h[:, 64:80] = fm(A(final_g), 16)
    gc_ = A(gdn_conv)
    convw_h = np.ascontiguousarray(np.stack([
        gc_[l].reshape(9, 24, 128).transpose(2, 1, 0).reshape(P, 24 * 9) for l in range(DEPTH)]))
    cmwsT_h = np.ascontiguousarray(np.stack([A(cm_ws)[l].transpose(2, 0, 1).reshape(P, 8 * 128) for l in range(DEPTH)]))
    cmbs_h = np.ascontiguousarray(A(cm_bs).reshape(DEPTH, 1, 1024))
    rowc_h = np.zeros((DEPTH, 1, 64), f32)
    for l in range(DEPTH):
        rowc_h[l, 0, 0:16] = A(gdn_A_log)[l].reshape(-1)
        rowc_h[l, 0, 16:32] = A(gdn_dt_bias)[l].reshape(-1)
    consts_h = make_consts()
    tpos = np.arange(T)
    in_maps = []
    for core in range(NCORES):
        sample = core < 4
        if sample:
            b = core
            xt = x_sample[b]
            cv = c[b]
            shg = state_hgrn[b]
            sgd = state_gdn[b]
            period = 64
        else:
            b0 = (core - 4) * 4
            xt = x_prompt[b0:b0 + 4].reshape(T, D)
            cv = c_ctx
            shg = np.zeros_like(state_hgrn[0])
            sgd = np.zeros_like(state_gdn[0])
            period = 256
        flags_h = np.zeros((P, 16), f32)
        flags_h[:, 0] = 1.0 if sample else 0.0
        for dr in range(3):
            for dc in range(3):
                flags_h[:, 1 + dr * 3 + dc] = 1.0 if (sample or dr == 1) else 0.0
        cm = np.zeros((2, T), f32)
        cm[0] = (tpos % period) != (period - 1)
        cm[1] = (tpos % period) != 0
        m = {
            "xT": np.ascontiguousarray(xt.T),
            "cond": fm(cv, 16),
            "s0hg": np.ascontiguousarray(shg.reshape(DEPTH, 16, 128, 128).transpose(0, 2, 1, 3)),
            "s0gd": np.ascontiguousarray(sgd.reshape(DEPTH, 16, 128, 128).transpose(0, 2, 1, 3)),
            "flags": flags_h, "cmask": cm, "wmod": wmod_h, "wgab": wgab_h, "vecs": vecs_h, "convw": convw_h,
            "cmwsT": cmwsT_h, "cmbs": cmbs_h, "rowc": rowc_h, "consts": consts_h,
        }
        for l in range(DEPTH):
            m["ws%d" % l] = wsl[l]
        in_maps.append(m)
    if _cores is not None:
        nc = build_program(_dbg, _stop)
        res = run_bass_kernel_spmd(nc, [in_maps[k] for k in _cores], core_ids=list(range(len(_cores))))
        return [{k: np.asarray(v) for k, v in ri.items()} for ri in res.results]
    nc = _get_program(_dbg)
    res = run_bass_kernel_spmd(nc, in_maps, core_ids=list(range(NCORES)))
    r = res.results
    y_sample = np.stack([r[i]["yT"].T for i in range(4)]).astype(f32)
    y_prompt = np.concatenate([r[i]["yT"].T.reshape(4, 256, D) for i in range(4, 8)]).astype(f32)
    nhg = np.concatenate([r[i]["nshg"] for i in range(4, 8)]).astype(f32)
    ngd = np.concatenate([r[i]["nsgd"] for i in range(4, 8)]).astype(f32)
    if _dbg:
        kernel.last_dbg = [{k: v for k, v in ri.items() if k.startswith("dbg_")} for ri in r]
    return (y_prompt, y_sample, nhg, ngd)
```
